# Optimizing a Trainium2 kernel written in Bass

```python
import math
import jax, jax.numpy as jnp
from jax import lax
import numpy as np


D_MODEL = 2048
BATCH = 2
SEQ = 16384
DEPTH = 1

MEM_LEN = 256
EPS = 1e-6
SG_CHUNK = 128
SG_GROUPS = 8
SG_WIDTH = D_MODEL // 2
SG_GROUP_DIM = SG_WIDTH // SG_GROUPS
GLA_HEADS = 4
GLA_DK = D_MODEL // 2
GLA_DV = D_MODEL
GLA_HEAD_K = GLA_DK // GLA_HEADS
GLA_HEAD_V = GLA_DV // GLA_HEADS
GLA_GATE_RANK = 16
GLA_TAU = 16.0
GLA_CHUNK = 64
GLA_LOG_DECAY_MIN = -1.0
XA_HEADS = 4
XA_HEAD_DIM = 128
XA_WIDTH = XA_HEADS * XA_HEAD_DIM
D_FF = 5632
CONV_WIDTH = 3
IN_SPLITS = (2 * SG_WIDTH, GLA_DK, GLA_DK, GLA_DV, GLA_DV, GLA_GATE_RANK, D_MODEL, D_MODEL)
N_IN = 2 * SG_WIDTH + 2 * GLA_DK + 2 * GLA_DV + GLA_GATE_RANK + 2 * D_MODEL

kernel_name = 'hybrid_sgmlp_gla_xattn_convffn_block'


def rmsnorm(x, g):
    xf = x.astype(jnp.float32)
    y = xf * lax.rsqrt(jnp.mean(xf * xf, axis=-1, keepdims=True) + EPS)
    return (y * g.astype(jnp.float32)).astype(x.dtype)


def layernorm(x, g, b):
    xf = x.astype(jnp.float32)
    mu = jnp.mean(xf, axis=-1, keepdims=True)
    xc = xf - mu
    y = xc * lax.rsqrt(jnp.mean(xc * xc, axis=-1, keepdims=True) + EPS)
    return (y * g.astype(jnp.float32) + b.astype(jnp.float32)).astype(x.dtype)


def spatial_gating(u, v, w_s, b_s):
    B, S, _ = v.shape
    n = S // SG_CHUNK
    vc = v.reshape(B, n, SG_CHUNK, SG_GROUPS, SG_GROUP_DIM)
    causal = jnp.tril(jnp.ones((SG_CHUNK, SG_CHUNK), dtype=bool))
    w = jnp.where(causal[None], w_s, 0).astype(v.dtype)
    s = jnp.einsum('gts,bnsge->bntge', w, vc) + b_s.T.astype(v.dtype)[None, None, :, :, None]
    return u * s.reshape(B, S, SG_WIDTH)


def gla_chunked(q, k, v, log_a):
    B, S, H, dk = q.shape
    dv = v.shape[-1]
    n = S // GLA_CHUNK

    def to_chunks(t):
        return t.astype(jnp.float32).reshape(B, n, GLA_CHUNK, H, t.shape[-1]).transpose(1, 0, 3, 2, 4)

    qc = to_chunks(q) * (dk ** -0.5)
    kc = to_chunks(k)
    vc = to_chunks(v)
    bcum = jnp.cumsum(to_chunks(log_a), axis=3)
    b_last = bcum[:, :, :, -1:, :]
    q_in = qc * jnp.exp(bcum)
    k_in = kc * jnp.exp(-bcum)
    k_st = kc * jnp.exp(b_last - bcum)
    causal = jnp.tril(jnp.ones((GLA_CHUNK, GLA_CHUNK), dtype=bool))
    attn = jnp.where(causal, jnp.einsum('nbhcd,nbhsd->nbhcs', q_in, k_in), 0.0)
    o_intra = jnp.einsum('nbhcs,nbhse->nbhce', attn, vc)

    def step(state, xs):
        q_c, k_c, v_c, dec = xs
        o = jnp.einsum('bhcd,bhde->bhce', q_c, state)
        state = state * dec[:, :, 0, :, None] + jnp.einsum('bhcd,bhce->bhde', k_c, v_c)
        return state, o

    init = jnp.zeros((B, H, dk, dv), jnp.float32)
    _, o_inter = lax.scan(step, init, (q_in, k_st, vc, jnp.exp(b_last)))
    o = o_intra + o_inter
    return o.transpose(1, 0, 3, 2, 4).reshape(B, S, H, dv).astype(v.dtype)


def token_mixer(h, w_in, sg_ln_g, sg_ln_b, sg_w, sg_b, gla_w_gate2, gla_b_gate, gla_norm_g,
                w_proj_a, w_proj_b, w_out):
    B, S, _ = h.shape
    idx = [int(i) for i in np.cumsum(IN_SPLITS)[:-1]]
    z = h @ w_in
    z_sg, q, k, v, og, g_lr, m_a, m_b = jnp.split(z, idx, axis=-1)
    z_sg = jax.nn.gelu(z_sg, approximate=False)
    u, vs = jnp.split(z_sg, 2, axis=-1)
    y_a = spatial_gating(u, layernorm(vs, sg_ln_g, sg_ln_b), sg_w, sg_b)
    gate_logit = (g_lr @ gla_w_gate2 + gla_b_gate).astype(jnp.float32)
    log_a = jnp.maximum(jax.nn.log_sigmoid(gate_logit) / GLA_TAU, GLA_LOG_DECAY_MIN)
    o = gla_chunked(q.reshape(B, S, GLA_HEADS, GLA_HEAD_K), k.reshape(B, S, GLA_HEADS, GLA_HEAD_K),
                    v.reshape(B, S, GLA_HEADS, GLA_HEAD_V), log_a.reshape(B, S, GLA_HEADS, GLA_HEAD_K))
    o = rmsnorm(o, gla_norm_g) * jax.nn.silu(og.reshape(B, S, GLA_HEADS, GLA_HEAD_V))
    y_b = o.reshape(B, S, GLA_DV)
    merged = jax.nn.sigmoid(m_a) * (y_a @ w_proj_a) + jax.nn.sigmoid(m_b) * (y_b @ w_proj_b)
    return merged @ w_out


def memory_cross_attention(h, mem_n, wq, wk, wv, wo):
    B, S, _ = h.shape
    M = mem_n.shape[1]
    q = (h @ wq).reshape(B, S, XA_HEADS, XA_HEAD_DIM)
    k = (mem_n @ wk).reshape(B, M, XA_HEADS, XA_HEAD_DIM)
    v = (mem_n @ wv).reshape(B, M, XA_HEADS, XA_HEAD_DIM)
    s = jnp.einsum('bshd,bmhd->bhsm', q, k).astype(jnp.float32) * (XA_HEAD_DIM ** -0.5)
    p = jax.nn.softmax(s, axis=-1).astype(v.dtype)
    o = jnp.einsum('bhsm,bmhd->bshd', p, v).reshape(B, S, XA_WIDTH)
    return o @ wo


def conv_ffn(h, w_up, conv_w, conv_b, w_down):
    hid = h @ w_up
    rhs = conv_w[:, None, :].astype(hid.dtype)
    hid = lax.conv_general_dilated(hid, rhs, window_strides=(1,), padding=[(CONV_WIDTH - 1, 0)],
                                   dimension_numbers=('NWC', 'WIO', 'NWC'),
                                   feature_group_count=2 * D_FF) + conv_b.astype(hid.dtype)
    gate, up = jnp.split(hid, 2, axis=-1)
    return (jax.nn.gelu(gate, approximate=True) * up) @ w_down


def setup_inputs(seed: int = 0) -> dict:
    key = jax.random.key(seed)
    ks = jax.random.split(key, 32)
    f32 = jnp.float32

    def nrm(k, shape, scale):
        return jax.random.normal(k, shape, f32) * scale

    def gain(k, shape):
        return 1.0 + 0.1 * jax.random.normal(k, shape, f32)

    L, D = DEPTH, D_MODEL
    return {
        'x': nrm(ks[0], (BATCH, SEQ, D), 1.0),
        'mem': nrm(ks[1], (BATCH, MEM_LEN, D), 1.0),
        'pre_norm_mix': gain(ks[2], (L, D)),
        'w_in': nrm(ks[3], (L, D, N_IN), D ** -0.5),
        'sg_ln_g': gain(ks[4], (L, SG_WIDTH)),
        'sg_ln_b': nrm(ks[5], (L, SG_WIDTH), 0.02),
        'sg_w': nrm(ks[6], (L, SG_GROUPS, SG_CHUNK, SG_CHUNK), SG_CHUNK ** -0.5),
        'sg_b': gain(ks[7], (L, SG_GROUPS, SG_CHUNK)),
        'gla_w_gate2': nrm(ks[8], (L, GLA_GATE_RANK, GLA_DK), GLA_GATE_RANK ** -0.5),
        'gla_b_gate': nrm(ks[9], (L, GLA_DK), 0.1),
        'gla_norm_g': gain(ks[10], (L, GLA_HEAD_V)),
        'w_proj_a': nrm(ks[11], (L, SG_WIDTH, D), SG_WIDTH ** -0.5),
        'w_proj_b': nrm(ks[12], (L, GLA_DV, D), GLA_DV ** -0.5),
        'w_out': nrm(ks[13], (L, D, D), D ** -0.5),
        'post_norm_mix': gain(ks[14], (L, D)),
        'pre_norm_xa': gain(ks[15], (L, D)),
        'mem_norm_g': gain(ks[16], (L, D)),
        'xa_wq': nrm(ks[17], (L, D, XA_WIDTH), D ** -0.5),
        'xa_wk': nrm(ks[18], (L, D, XA_WIDTH), D ** -0.5),
        'xa_wv': nrm(ks[19], (L, D, XA_WIDTH), D ** -0.5),
        'xa_wo': nrm(ks[20], (L, XA_WIDTH, D), XA_WIDTH ** -0.5),
        'post_norm_xa': gain(ks[21], (L, D)),
        'pre_norm_ffn': gain(ks[22], (L, D)),
        'ffn_w_up': nrm(ks[23], (L, D, 2 * D_FF), D ** -0.5),
        'ffn_conv_w': nrm(ks[24], (L, CONV_WIDTH, 2 * D_FF), CONV_WIDTH ** -0.5),
        'ffn_conv_b': nrm(ks[25], (L, 2 * D_FF), 0.02),
        'ffn_w_down': nrm(ks[26], (L, D_FF, D), D_FF ** -0.5),
        'post_norm_ffn': gain(ks[27], (L, D)),
    }


def reference(x, mem, pre_norm_mix, w_in, sg_ln_g, sg_ln_b, sg_w, sg_b, gla_w_gate2, gla_b_gate,
              gla_norm_g, w_proj_a, w_proj_b, w_out, post_norm_mix, pre_norm_xa, mem_norm_g,
              xa_wq, xa_wk, xa_wv, xa_wo, post_norm_xa, pre_norm_ffn, ffn_w_up, ffn_conv_w,
              ffn_conv_b, ffn_w_down, post_norm_ffn):
    for l in range(DEPTH):
        h = rmsnorm(x, pre_norm_mix[l])
        y = token_mixer(h, w_in[l], sg_ln_g[l], sg_ln_b[l], sg_w[l], sg_b[l], gla_w_gate2[l],
                        gla_b_gate[l], gla_norm_g[l], w_proj_a[l], w_proj_b[l], w_out[l])
        x = x + rmsnorm(y, post_norm_mix[l])
        h = rmsnorm(x, pre_norm_xa[l])
        m = rmsnorm(mem, mem_norm_g[l])
        y = memory_cross_attention(h, m, xa_wq[l], xa_wk[l], xa_wv[l], xa_wo[l])
        x = x + rmsnorm(y, post_norm_xa[l])
        h = rmsnorm(x, pre_norm_ffn[l])
        y = conv_ffn(h, ffn_w_up[l], ffn_conv_w[l], ffn_conv_b[l], ffn_w_down[l])
        x = x + rmsnorm(y, post_norm_ffn[l])
    return x
```

```python
import contextlib
import numpy as np
import concourse.bass as bass
import concourse.mybir as mybir
from concourse.bass_utils import run_bass_kernel_spmd

F32 = mybir.dt.float32
BF16 = mybir.dt.bfloat16
AF = mybir.ActivationFunctionType
ALU = mybir.AluOpType
AX = mybir.AxisListType

D = 2048
N_IN = 12304
DFF = 5632
EPS = 1e-6
NB = 4
NCORES = 8
C_U, C_VS, C_Q, C_K, C_V, C_OG, C_GLR, C_MA, C_MB = 0, 1024, 2048, 3072, 4096, 6144, 8192, 8208, 10256
NW = 2
CROWS = 1026


class SemW:
    def __init__(self, h):
        self.h = h
        self.n = 0


class Eng:
    def __init__(self, raw, semw, name):
        self.raw = raw
        self.sem = semw
        self.name = name
        self.seen = {}

    def wait(self, tok):
        if tok is None:
            return
        sw, v = tok
        if v <= 0 or (sw is self.sem and self.name in ("pe", "sp")):
            return
        if self.seen.get(id(sw), 0) >= v:
            return
        self.raw.wait_ge(sw.h, v)
        self.seen[id(sw)] = v


class Reg:
    __slots__ = ("w", "rs")

    def __init__(self):
        self.w = None
        self.rs = {}


def op(E, fn, R=(), W=(), sig=True):
    for r in R:
        E.wait(r.w)
    for w in W:
        E.wait(w.w)
        for t in w.rs.values():
            E.wait(t)
    ins = fn()
    if sig or E.name != "pe":
        E.sem.n += 1
        ins.then_inc(E.sem.h, 1)
        tok = (E.sem, E.sem.n)
    else:
        tok = (E.sem, E.sem.n + 1)
    for r in R:
        r.rs[E.name] = tok
    for w in W:
        w.w = tok
        w.rs = {}
    return tok


def dma(Q, semw, out, in_, R=(), W=(), **kw):
    for r in R:
        Q.wait(r.w)
    for w in W:
        Q.wait(w.w)
        for t in w.rs.values():
            Q.wait(t)
    ins = Q.raw.dma_start(out=out, in_=in_, **kw)
    semw.n += 16
    ins.then_inc(semw.h, 16)
    tok = (semw, semw.n)
    for r in R:
        r.rs["dma%d" % id(semw)] = tok
    for w in W:
        w.w = tok
        w.rs = {}
    return tok


class Ring:
    def __init__(self, items):
        self.items = items
        self.i = 0

    def next(self):
        it = self.items[self.i]
        self.i = (self.i + 1) % len(self.items)
        return it


class Buf:
    def __init__(self, t):
        self.t = t
        self.r = Reg()


def build(SEG, stages="ABC", use_cc=True, debug=False):
    nc = bass.Bass("TRN2", target_bir_lowering=False)
    NBLK = SEG // 128 + 1
    T = 128 * NB
    assert (NBLK - 1) % NB == 0
    NT2 = (NBLK - 1) // NB

    def din(name, shape):
        return nc.dram_tensor(name, shape, F32, kind="ExternalInput")

    x_d = din("x", [NBLK * 128, D])
    xpre_d = None if use_cc else din("xpre", [3 * SEG, D])
    mem_d = din("mem", [256, D])
    cm_d = din("cm", [128, 16])
    w_in_d = din("w_in", [D, N_IN])
    pre_mix_d = din("pre_norm_mix", [D])
    sg_ln_g_d = din("sg_ln_g", [1024])
    sg_ln_b_d = din("sg_ln_b", [1024])
    sg_w_d = din("sg_w", [8, 128, 128])
    sg_b_d = din("sg_b", [8, 128])
    w2_d = din("gla_w_gate2", [16, 1024])
    bg_d = din("gla_b_gate", [1024])
    gn_d = din("gla_norm_g", [512])
    wpa_d = din("w_proj_a", [1024, D])
    wpb_d = din("w_proj_b", [D, D])
    wout_d = din("w_out", [D, D])
    post_mix_d = din("post_norm_mix", [D])
    pre_xa_d = din("pre_norm_xa", [D])
    memg_d = din("mem_norm_g", [D])
    wq_d = din("xa_wq", [D, 512])
    wk_d = din("xa_wk", [D, 512])
    wv_d = din("xa_wv", [D, 512])
    wo_d = din("xa_wo", [512, D])
    post_xa_d = din("post_norm_xa", [D])
    pre_ffn_d = din("pre_norm_ffn", [D])
    wup_d = din("ffn_w_up", [D, 2 * DFF])
    cw_d = din("ffn_conv_w", [3, 2 * DFF])
    cb_d = din("ffn_conv_b", [2 * DFF])
    wdn_d = din("ffn_w_down", [DFF, D])
    post_ffn_d = din("post_norm_ffn", [D])
    out_d = nc.dram_tensor("out", [SEG, D], F32, kind="ExternalOutput")
    dbgb_d = nc.dram_tensor("dbgb", [12, 128, 512], BF16, kind="ExternalOutput") if debug else None
    dbgf_d = nc.dram_tensor("dbgf", [12, 128, 512], F32, kind="ExternalOutput") if debug else None
    DBG = {"on": False}
    cin_d = nc.dram_tensor("cc_in", [CROWS, 512], F32)
    cout_d = nc.dram_tensor("cc_out", [NCORES * CROWS, 512], F32)

    es = contextlib.ExitStack()
    with es:
        def sb(name, shape, dt):
            return es.enter_context(nc.sbuf_tensor(name, shape, dt))

        def newsem(name):
            return SemW(es.enter_context(nc.semaphore(name)))

        PE = Eng(nc.tensor, newsem("s_pe"), "pe")
        ACT = Eng(nc.scalar, newsem("s_act"), "act")
        DVE = Eng(nc.vector, newsem("s_dve"), "dve")
        POOL = Eng(nc.gpsimd, newsem("s_pool"), "pool")
        SP = Eng(nc.sync, newsem("s_sp"), "sp")

        xs = sb("xs", [128, NB, D], F32)
        xs_r = [Reg() for _ in range(NB)]
        xs_sem = [newsem("xs%d" % j) for j in range(NB)]
        S = sb("S", [128, 8, 512], F32)
        S_r = [Reg() for _ in range(8)]
        Sb = sb("Sb", [128, 8, 512], BF16)
        Sb_r = [Reg() for _ in range(8)]
        hT = sb("hT", [128, 16, T], BF16)
        hT_r = [Reg() for _ in range(NB)]
        big = sb("big", [128, 44 * T], BF16)
        big_r = [Reg() for _ in range(44)]
        wsl = []
        for i in range(NW):
            b = Buf(sb("wsl%d" % i, [128, 16 * 512], BF16))
            b.sem = newsem("w%d" % i)
            wsl.append(b)
        wring = Ring(wsl)
        xn = Buf(sb("xn", [128, D], BF16))
        gbc = Buf(sb("gbc", [128, D], F32))
        gbc.sem = newsem("gbc")
        NP32 = 8
        p32all = sb("p32all", [128, NP32 * (T + 4)], F32)
        P32 = Ring([Buf(p32all[:, i * (T + 4):(i + 1) * (T + 4)]) for i in range(NP32)])
        p32w = sb("p32w", [128, 1024], F32)
        P32W = Ring([Buf(p32w[:, i * 512:(i + 1) * 512]) for i in range(2)])

        class _GV:
            t = p32w
        gv = _GV()
        gv_R = [b_.r for b_ in P32W.items]

        def yhv(stage, j):
            if stage == "A":
                if j < 3:
                    return big[:, 8 * j * T:8 * (j + 1) * T].bitcast(F32), big_r[8 * j:8 * j + 8]
                return hT[:, 0:8, :].rearrange("p k t -> p (k t)").bitcast(F32), list(hT_r)
            if stage == "B":
                return big[:, (8 + 8 * j) * T:(16 + 8 * j) * T].bitcast(F32), big_r[8 + 8 * j:16 + 8 * j]
            if j < 2:
                return hT[:, 8 * j:8 * j + 8, :].rearrange("p k t -> p (k t)").bitcast(F32), list(hT_r)
            lo = (j - 2) * 2048
            regs = [P32.items[i].r for i in range(NP32) if i * (T + 4) < lo + 2048 and (i + 1) * (T + 4) > lo]
            return p32all[:, lo:lo + 2048], regs
        junk = sb("junk", [128, 512], BF16)
        atb = Ring([Buf(sb("atb%d" % i, [128, 256], BF16)) for i in range(2)])
        ptb = Ring([Buf(sb("ptb%d" % i, [128, 256], BF16)) for i in range(2)])
        cvec = sb("cvec", [128, 128], F32)
        gpm = cvec[:, 0:16]
        gpx = cvec[:, 16:32]
        gpf = cvec[:, 32:48]
        gmem = cvec[:, 48:64]
        nbg = cvec[:, 64:72]
        gn = cvec[:, 72:76]
        lng = cvec[:, 76:84]
        rawv = big[:, 0:1024].bitcast(F32).rearrange("p (s c) -> p s c", c=128)
        cw = sb("cw", [128, 3, 88], F32)
        cb = sb("cb", [128, 88], F32)
        cm = sb("cm_sb", [128, 16], F32)
        omm = sb("omm", [128, 16], F32)
        ident = sb("ident", [128, 128], BF16)
        identf = sb("identf", [128, 128], F32)
        tri = sb("tri", [128, 128], F32)
        rmask = sb("rmask", [128, T], F32)
        hmask = sb("hmask", [128, T], BF16)
        WmT = sb("WmT", [128, 8, 128], BF16)
        Qsg = sb("Qsg", [128, 8, 128], F32)
        wsgb = big[:, 1024:2048].rearrange("p (g s) -> p g s", s=128)
        Bb = big[:, 2048:3072]
        bsbc = big[:, 3072:5120].bitcast(F32)
        xk = sb("xk", [128, 4, 256], BF16)
        xv = sb("xv", [128, 2, 512], BF16)
        W2b = sb("W2b", [16, 1024], BF16)
        Wglr = sb("Wglr", [128, 16, 16], BF16)
        glr = Buf(sb("glr", [16, T], BF16))
        halo = sb("halo", [128, 88, 2], F32)
        hprev = Buf(sb("hprev", [128, 16, 2], BF16))
        dec = Buf(sb("dec", [128, 8, NB], F32))
        Dtot = Buf(sb("Dtot", [128, 8], F32))
        Dl = Buf(sb("Dl", [128, 8], F32))
        Dl.sem = newsem("dl")
        De = sb("De", [128, 8], F32)
        st = sb("st", [128, 64], F32)
        st_r = Reg()
        halo_r = Reg()
        De_r = Reg()
        cst_r = Reg()
        cst_sem = newsem("cst")
        out_sem = [newsem("o%d" % j) for j in range(NB)]
        cc_sem = newsem("cc")
        ccio_sem = newsem("ccio")
        yl_sem = newsem("yl")

        print("SBUF bytes remaining per partition:", nc.sbuf_bytes_remaining)
        FP = Ring([Buf(es.enter_context(nc.psum_tensor("pf%d" % i, [128, 512], F32))) for i in range(6)])
        HP = Ring([Buf(es.enter_context(nc.psum_tensor("ph%d" % i, [128, 1024], BF16))) for i in range(2)])

        def bigc(c, n=1):
            return big[:, c * T:(c + n) * T]

        def bigr(c, n=1):
            return big_r[c:c + n]

        def mm_group(out_ap, pairs, R, W):
            n = len(pairs)
            for r in R:
                PE.wait(r.w)
            for w in W:
                PE.wait(w.w)
                for t in w.rs.values():
                    PE.wait(t)
            for i, (l, r_) in enumerate(pairs):
                ins = nc.tensor.matmul(out_ap, l, r_, start=(i == 0), stop=(i == n - 1))
            PE.sem.n += 1
            ins.then_inc(PE.sem.h, 1)
            tok = (PE.sem, PE.sem.n)
            for r in R:
                r.rs["pe"] = tok
            for w in W:
                w.w = tok
                w.rs = {}
            return tok

        def pe_multi(fns, R, W):
            for r in R:
                PE.wait(r.w)
            for w in W:
                PE.wait(w.w)
                for t in w.rs.values():
                    PE.wait(t)
            for f in fns:
                ins = f()
            PE.sem.n += 1
            ins.then_inc(PE.sem.h, 1)
            tok = (PE.sem, PE.sem.n)
            for r in R:
                r.rs["pe"] = tok
            for w in W:
                w.w = tok
                w.rs = {}
            return tok

        def act(out, in_, func, R, W, **kw):
            return op(ACT, lambda: nc.scalar.activation(out=out, in_=in_, func=func, **kw), R, W)

        SCR = {}

        def mkscratch(name, src_d, rows, cols, rchunk):
            t_ = nc.dram_tensor(name + "_bf", [rows, cols], BF16)
            reg = Reg()
            semw = newsem("cv_" + name)
            for r0 in range(0, rows, rchunk):
                ins = nc.gpsimd.dma_start(out=t_.ap()[r0:r0 + rchunk, :], in_=src_d.ap()[r0:r0 + rchunk, :])
                semw.n += 16
                ins.then_inc(semw.h, 16)
            reg.w = (semw, semw.n)
            SCR[id(src_d)] = (t_, reg)

        def wsrc(dt_, r0, nr, c0, ncol):
            if dt_ is w_in_d and c0 >= C_K and c0 + ncol <= C_OG:
                return kv_t.ap()[r0:r0 + nr, c0 - C_K:c0 - C_K + ncol], kv_reg
            sc_t, sc_r = SCR[id(dt_)]
            return sc_t.ap()[r0:r0 + nr, c0:c0 + ncol], sc_r
        mkscratch("wq", wq_d, D, 512, D)
        mkscratch("wo", wo_d, 512, D, 512)
        mkscratch("wpa", wpa_d, 1024, D, 512)
        mkscratch("wpb", wpb_d, D, D, 512)
        mkscratch("wout", wout_d, D, D, 512)
        mkscratch("wup", wup_d, D, 2 * DFF, 128)
        mkscratch("wdn", wdn_d, DFF, D, 704)

        def load_w(pieces):
            slot = wring.next()
            for (dt_, r0, nr, c0, ncol, wdt) in pieces:
                nk = nr // 128
                src0, sc_r = wsrc(dt_, r0, nr, c0, ncol)
                src = src0.rearrange("(kc p) c -> p kc c", p=128)
                dst = slot.t[:, 0:nk * wdt].rearrange("p (kc c) -> p kc c", c=wdt)[:, :, 0:ncol]
                dma(POOL, slot.sem, dst, src, R=[sc_r], W=[slot.r])
            return slot

        def wview(slot, nk, wdt):
            return slot.t[:, 0:nk * wdt].rearrange("p (kc c) -> p kc c", c=wdt)

        def load_w_std(dt_, c0, ncol=512, nrows=D, r0=0):
            slot = wring.next()
            nk = nrows // 128
            src0, sc_r = wsrc(dt_, r0, nrows, c0, ncol)
            src = src0.rearrange("(kc p) c -> p kc c", p=128)
            dst = slot.t[:, 0:nk * 512].rearrange("p (kc c) -> p kc c", c=512)[:, :, 0:ncol]
            dma(POOL, slot.sem, dst, src, R=[sc_r], W=[slot.r])
            return slot

        def colvec(dst, dram1d, k):
            dma(SP, cst_sem, dst, dram1d.ap().rearrange("(k p) -> p k", p=128), W=[cst_r],
                allow_slow_non_contiguous=True)

        def rawrows(slot, r0, dram1d, k):
            dma(SP, cst_sem, rawv[r0:r0 + k, slot, :], dram1d.rearrange("(k p) -> k p", p=128), W=[cst_r])

        op(DVE, lambda: nc.vector.memset(rawv, 0.0), W=[cst_r])
        rawrows(0, 0, pre_mix_d.ap(), 16)
        rawrows(0, 16, pre_xa_d.ap(), 16)
        rawrows(0, 32, pre_ffn_d.ap(), 16)
        rawrows(0, 48, memg_d.ap(), 16)
        rawrows(0, 64, bg_d.ap(), 8)
        rawrows(0, 72, gn_d.ap(), 4)
        rawrows(0, 76, sg_ln_g_d.ap(), 8)
        rawrows(1, 0, cb_d.ap(), 88)
        rawrows(2, 0, cw_d.ap()[0, :], 88)
        rawrows(3, 0, cw_d.ap()[1, :], 88)
        dma(SP, cst_sem, cm[:], cm_d.ap(), W=[cst_r])
        dma(SP, cst_sem, bsbc, sg_b_d.ap().rearrange("g t -> (g t)").partition_broadcast(128), W=[cst_r])
        dma(POOL, cst_sem, W2b[:], w2_d.ap(), W=[cst_r])
        dma(POOL, cst_sem, Wglr[:], w_in_d.ap()[:, C_GLR:C_GLR + 16].rearrange("(kc p) c -> p kc c", p=128),
            W=[cst_r], allow_slow_non_contiguous=True)
        dma(POOL, cst_sem, wsgb, sg_w_d.ap().rearrange("g t s -> t g s"), W=[cst_r])
        dma(POOL, cst_sem, Bb, sg_ln_b_d.ap().partition_broadcast(128), W=[cst_r])
        V = nc.vector
        op(DVE, lambda: V.memset(identf[:], 1.0), W=[cst_r], sig=False)
        op(DVE, lambda: V.memset(tri[:], 1.0), sig=False)
        op(DVE, lambda: V.memset(rmask[:], 1.0), sig=False)
        op(DVE, lambda: V.memset(hmask[:], 0.0), sig=False)
        op(DVE, lambda: V.memset(halo[:], 0.0), sig=False)
        op(DVE, lambda: V.memset(S[:], 0.0), W=S_r, sig=False)
        op(DVE, lambda: V.memset(Dtot.t[:], 1.0), W=[Dtot.r], sig=False)
        for j in range(NB):
            op(DVE, lambda j=j: V.memset(rmask[:, j * 128:j * 128 + 1], 0.0), sig=False)
            op(DVE, lambda j=j: V.memset(hmask[:, j * 128:j * 128 + 64], 1.0), sig=False)
        tok_c = op(DVE, lambda: V.memset(st[:], 0.0), W=[st_r])
        POOL.wait(tok_c)
        G = nc.gpsimd
        op(POOL, lambda: G.affine_select(out=identf[:], in_=identf[:], pattern=[[-1, 128]],
                                         compare_op=ALU.is_equal, fill=0.0, base=0, channel_multiplier=1),
           W=[cst_r], sig=False)
        op(POOL, lambda: G.affine_select(out=tri[:], in_=tri[:], pattern=[[1, 128]],
                                         compare_op=ALU.is_ge, fill=0.0, base=0, channel_multiplier=-1),
           W=[cst_r], sig=False)
        op(POOL, lambda: G.tensor_copy(out=ident[:], in_=identf[:]), W=[cst_r])
        for sl, dst in ((0, cvec[:, :]), (1, None), (2, None), (3, None)):
            pr = FP.next()
            pe_multi([lambda sl=sl, pr=pr: nc.tensor.transpose(pr.t[:, 0:128], rawv[:, sl, :], identf[:])],
                     R=[cst_r], W=[pr.r])
            if sl == 0:
                op(DVE, lambda pr=pr: V.tensor_copy(out=cvec[:, :], in_=pr.t[:, 0:128]), R=[pr.r], W=[cst_r])
            elif sl == 1:
                op(DVE, lambda pr=pr: V.tensor_copy(out=cb[:, :], in_=pr.t[:, 0:88]), R=[pr.r], W=[cst_r])
            else:
                op(DVE, lambda pr=pr, sl=sl: V.tensor_copy(out=cw[:, sl - 2, :], in_=pr.t[:, 0:88]), R=[pr.r], W=[cst_r])
        dma(SP, cst_sem, rawv[0:88, 0, :], cw_d.ap()[2, :].rearrange("(k p) -> k p", p=128), W=[cst_r])
        pr = FP.next()
        pe_multi([lambda: nc.tensor.transpose(pr.t[:, 0:128], rawv[:, 0, :], identf[:])], R=[cst_r], W=[pr.r])
        op(DVE, lambda: V.tensor_copy(out=cw[:, 2, :], in_=pr.t[:, 0:88]), R=[pr.r], W=[cst_r])
        op(DVE, lambda: V.tensor_scalar(out=nbg, in0=nbg, scalar1=-1.0, scalar2=None, op0=ALU.mult),
           R=[cst_r], W=[cst_r])
        op(DVE, lambda: V.tensor_scalar(out=omm[:], in0=cm[:], scalar1=-1.0, scalar2=1.0, op0=ALU.mult,
                                        op1=ALU.add), R=[cst_r])
        pb = HP.next()
        pe_multi([lambda g=g: nc.tensor.transpose(pb.t[:, g * 128:(g + 1) * 128], wsgb[:, g, :], ident[:])
                  for g in range(8)], R=[cst_r], W=[pb.r])
        op(DVE, lambda: V.tensor_tensor(out=WmT[:], in0=pb.t[:, 0:1024].rearrange("p (g t) -> p g t", g=8),
                                        in1=tri[:].unsqueeze(1).broadcast_to([128, 8, 128]), op=ALU.mult),
           R=[pb.r], W=[cst_r])
        for h2 in range(2):
            pq = FP.next()
            pe_multi([lambda g=g: nc.tensor.matmul(pq.t[:, (g % 4) * 128:(g % 4 + 1) * 128],
                                                   Bb[:, g * 128:(g + 1) * 128], WmT[:, g, :],
                                                   start=True, stop=True)
                      for g in range(h2 * 4, h2 * 4 + 4)], R=[cst_r], W=[pq.r])
            op(DVE, lambda h2=h2, pq=pq: V.tensor_tensor(
                out=Qsg[:, h2 * 4:(h2 + 1) * 4, :], in0=pq.t[:, 0:512].rearrange("p (g t) -> p g t", g=4),
                in1=bsbc[:, h2 * 512:(h2 + 1) * 512].rearrange("p (g t) -> p g t", g=4), op=ALU.add),
               R=[pq.r], W=[cst_r])

        kv_t = nc.dram_tensor("w_in_kv_bf", [D, 3072], BF16)
        kv_reg = Reg()
        kv_sem = newsem("cv_kv")
        for r0 in range(0, D, 512):
            ins = nc.gpsimd.dma_start(out=kv_t.ap()[r0:r0 + 512, :], in_=w_in_d.ap()[r0:r0 + 512, C_K:C_OG])
            kv_sem.n += 16
            ins.then_inc(kv_sem.h, 16)
        kv_reg.w = (kv_sem, kv_sem.n)
        mkscratch("wk", wk_d, D, 512, D)
        mkscratch("wv", wv_d, D, 512, D)
        mkscratch("w_in", w_in_d, D, N_IN, 128)

        dbg_sem = newsem("dbg")

        def dump(idx, ap, regions, w, f32=False):
            if not (debug and DBG["on"]):
                return
            dst = (dbgf_d if f32 else dbgb_d).ap()[idx, :, 0:w]
            dma(SP, dbg_sem, dst, ap, R=regions)

        def barrier():
            toks = [(E.sem, E.sem.n) for E in (PE, ACT, DVE, POOL)] + [(cst_sem, cst_sem.n)]
            for E in (PE, ACT, DVE, POOL, SP):
                for t_ in toks:
                    E.wait(t_)

        barrier()

        cur = {"hT": hT, "hT_r": hT_r}
        hT2 = big[:, 0:16 * T].rearrange("p (k t) -> p k t", t=T)
        hT2_r = [Reg() for _ in range(NB)]

        def rmsnorm_T(blocks, gcol, nblk, hsel=None, only=None):
            hT = cur["hT"] if hsel is None else hsel[0]
            hT_r = cur["hT_r"] if hsel is None else hsel[1]
            for j, (xa, xr) in enumerate(blocks):
                if only is not None and j != only:
                    continue
                act(xn.t[:], xa, AF.Square, R=[xr], W=[xn.r, st_r], accum_out=st[:, 0:1])
                act(st[:, 1:2], st[:, 0:1], AF.Ln, R=[], W=[st_r], scale=1.0 / D, bias=EPS)
                act(st[:, 1:2], st[:, 1:2], AF.Exp, R=[], W=[st_r], scale=-0.5)
                act(xn.t[:], xa, AF.Copy, R=[xr, st_r], W=[xn.r], scale=st[:, 1:2])
                for h in range(2):
                    pb_ = HP.next()
                    pe_multi([lambda i=i, pb_=pb_, h=h: nc.tensor.transpose(
                        pb_.t[:, i * 128:(i + 1) * 128], xn.t[:, (h * 8 + i) * 128:(h * 8 + i + 1) * 128], ident[:])
                        for i in range(8)], R=[xn.r], W=[pb_.r])
                    op(DVE, lambda h=h, pb_=pb_, j=j: V.tensor_tensor(
                        out=hT[:, h * 8:(h + 1) * 8, j * 128:(j + 1) * 128],
                        in0=pb_.t[:, 0:1024].rearrange("p (k t) -> p k t", k=8),
                        in1=gcol[:, h * 8:(h + 1) * 8].unsqueeze(2).broadcast_to([128, 8, 128]), op=ALU.mult),
                       R=[pb_.r], W=[hT_r[j]])

        def load_gbc(dram1d):
            dma(SP, gbc.sem, gbc.t[:], dram1d.ap().partition_broadcast(128), W=[gbc.r])

        def y_evac(py, j, fg, stage):
            ya, yr = yhv(stage, j)
            act(ya[:, fg * 512:(fg + 1) * 512], py.t[:, 0:512], AF.Copy, R=[py.r], W=yr)
            act(junk[:], py.t[:, 0:512], AF.Square, R=[py.r], W=[st_r], accum_out=st[:, 40 + j * 4 + fg:41 + j * 4 + fg])

        def residual_update(j, stage):
            ya, yr = yhv(stage, j)
            op(DVE, lambda: V.reduce_sum(out=st[:, 4:5], in_=st[:, 40 + j * 4:44 + j * 4], axis=AX.X),
               R=[st_r], W=[st_r])
            act(st[:, 5:6], st[:, 4:5], AF.Ln, R=[st_r], W=[st_r], scale=1.0 / D, bias=EPS)
            act(st[:, 5:6], st[:, 5:6], AF.Exp, R=[], W=[st_r], scale=-0.5)
            op(DVE, lambda: V.scalar_tensor_tensor(out=ya, in0=ya, scalar=st[:, 5:6],
                                                   in1=gbc.t[:], op0=ALU.mult, op1=ALU.mult),
               R=[st_r, gbc.r] + yr, W=yr)
            op(DVE, lambda: V.tensor_tensor(out=xs[:, j, :], in0=xs[:, j, :], in1=ya, op=ALU.add),
               R=yr, W=[xs_r[j]])

        def load_x(blk0, nblk, src=None):
            src = x_d if src is None else src
            for j in range(nblk):
                dma(SP, xs_sem[j], xs[:, j, :], src.ap()[(blk0 + j) * 128:(blk0 + j + 1) * 128, :], W=[xs_r[j]])

        def stageA(nblk, phase, pre_done=False, hook=None):
            Tt = 128 * nblk
            hT = cur["hT"]
            hT_r = cur["hT_r"]
            if not pre_done:
                rmsnorm_T([(xs[:, j, :], xs_r[j]) for j in range(nblk)], gpm, nblk)
            hR = hT_r[0:nblk]
            full = (phase == 2)
            dump(0, hT[:, 0, 0:Tt], hR, Tt)
            dump(1, hT[:, 15, 0:Tt], hR, Tt)
            if full:
                wa = load_w_std(w_in_d, C_VS)
                wb = load_w_std(w_in_d, C_VS + 512)
                vln = big[:, 24 * T:24 * T + NB * 1024].rearrange("p (j e) -> p j e", e=1024)
                vln_r = bigr(24, 8)
                for j in range(nblk):
                    for hf, w_ in enumerate((wa, wb)):
                        ps = FP.next()
                        wv_ = wview(w_, 16, 512)
                        mm_group(ps.t[:, 0:512], [(hT[:, kc, j * 128:(j + 1) * 128], wv_[:, kc, :]) for kc in range(16)],
                                 R=[hT_r[j], w_.r], W=[ps.r])
                        act(gv.t[:, hf * 512:(hf + 1) * 512], ps.t[:, 0:512], AF.Gelu, R=[ps.r], W=gv_R + [st_r],
                            accum_out=st[:, 16 + hf:17 + hf])
                        act(junk[:], gv.t[:, hf * 512:(hf + 1) * 512], AF.Square, R=gv_R, W=[st_r],
                            accum_out=st[:, 18 + hf:19 + hf])
                    op(DVE, lambda: V.tensor_tensor(out=st[:, 20:21], in0=st[:, 16:17], in1=st[:, 17:18], op=ALU.add),
                       R=[st_r], W=[st_r])
                    op(DVE, lambda: V.tensor_tensor(out=st[:, 21:22], in0=st[:, 18:19], in1=st[:, 19:20], op=ALU.add),
                       W=[st_r])
                    op(DVE, lambda: V.tensor_scalar(out=st[:, 20:21], in0=st[:, 20:21], scalar1=1.0 / 1024, scalar2=None,
                                                    op0=ALU.mult), W=[st_r])
                    op(DVE, lambda: V.tensor_tensor(out=st[:, 22:23], in0=st[:, 20:21], in1=st[:, 20:21], op=ALU.mult),
                       W=[st_r])
                    op(DVE, lambda: V.scalar_tensor_tensor(out=st[:, 22:23], in0=st[:, 21:22], scalar=1.0 / 1024,
                                                           in1=st[:, 22:23], op0=ALU.mult, op1=ALU.subtract), W=[st_r])
                    act(st[:, 23:24], st[:, 22:23], AF.Ln, R=[st_r], W=[st_r], bias=EPS)
                    act(st[:, 23:24], st[:, 23:24], AF.Exp, R=[], W=[st_r], scale=-0.5)
                    op(DVE, lambda j=j: V.tensor_scalar(out=vln[:, j, :], in0=gv.t[:, 0:1024], scalar1=st[:, 20:21],
                                                        scalar2=st[:, 23:24], op0=ALU.subtract, op1=ALU.mult),
                       R=gv_R + [st_r], W=vln_r)
                for t2 in range(2):
                    wu = load_w_std(w_in_d, C_U + t2 * 512)
                    wuv = wview(wu, 16, 512)
                    for gl in range(4):
                        g = t2 * 4 + gl
                        pg = FP.next()
                        pe_multi([lambda j=j, pg=pg, g=g: nc.tensor.matmul(
                            pg.t[:, j * 128:(j + 1) * 128], vln[:, j, g * 128:(g + 1) * 128], WmT[:, g, :],
                            start=True, stop=True) for j in range(nblk)], R=vln_r, W=[pg.r])
                        pu = FP.next()
                        mm_group(pu.t[:, 0:Tt], [(wuv[:, kc, gl * 128:(gl + 1) * 128], hT[:, kc, 0:Tt]) for kc in range(16)],
                                 R=hR + [wu.r], W=[pu.r])
                        ub = P32.next()
                        act(ub.t[:, 0:Tt], pu.t[:, 0:Tt], AF.Gelu, R=[pu.r], W=[ub.r])
                        svb = P32.next()
                        op(DVE, lambda pg=pg, g=g, svb=svb: V.scalar_tensor_tensor(
                            out=svb.t[:, 0:Tt].rearrange("p (j t) -> p j t", t=128),
                            in0=pg.t[:, 0:Tt].rearrange("p (j t) -> p j t", t=128), scalar=lng[:, g:g + 1],
                            in1=Qsg[:, g, :].unsqueeze(1).broadcast_to([128, nblk, 128]), op0=ALU.mult, op1=ALU.add),
                           R=[pg.r], W=[svb.r])
                        op(DVE, lambda g=g, ub=ub, svb=svb: V.tensor_tensor(
                            out=bigc(g)[:, 0:Tt], in0=ub.t[:, 0:Tt], in1=svb.t[:, 0:Tt], op=ALU.mult),
                           R=[ub.r, svb.r], W=bigr(g))
            if full:
                dump(2, bigc(0)[:, 0:Tt], bigr(0), Tt)
                dump(3, bigc(7)[:, 0:Tt], bigr(7), Tt)
                dump(0, Qsg[:, 0, :], [cst_r], 128, f32=True)
                dump(1, cvec[:, :], [cst_r], 128, f32=True)
            pgl = FP.next()
            mm_group(pgl.t[0:16, 0:Tt], [(Wglr[:, kc, :], hT[:, kc, 0:Tt]) for kc in range(16)], R=hR, W=[pgl.r])
            act(glr.t[:, 0:Tt], pgl.t[0:16, 0:Tt], AF.Copy, R=[pgl.r], W=[glr.r])
            qp = [bigc(24 + d2) for d2 in range(2)]
            qt = [bigc(26 + d2) for d2 in range(2)]
            kp = [bigc(28 + d2) for d2 in range(2)]
            kz = [bigc(30 + d2) for d2 in range(2)]
            kh = [bigc(32 + d2) for d2 in range(2)]
            khT = big[:, 34 * T:34 * T + NB * 256].rearrange("p (j d) -> p j d", d=256)
            khT_r = bigr(34, 2)
            vt = big[:, 36 * T:36 * T + NB * 512].rearrange("p (j e) -> p j e", e=512)
            vt_r = bigr(36, 4)
            ybt_ring = Ring([Buf(big[:, 40 * T + i * 512:40 * T + (i + 1) * 512]) for i in range(4 * T // 512)])
            for i_, b_ in enumerate(ybt_ring.items):
                b_.r = big_r[40 + (i_ * 512) // T]
            for hd in range(4):
                wqk = load_w([(w_in_d, 0, D, C_K + hd * 256, 256, 512)]) if not full else None
                if full:
                    wqk = wring.next()
                    for ci, c0 in enumerate((C_Q + hd * 256, C_K + hd * 256)):
                        src0, sc_r = wsrc(w_in_d, 0, D, c0, 256)
                        src = src0.rearrange("(kc p) c -> p kc c", p=128)
                        dst = wqk.t[:].rearrange("p (kc c) -> p kc c", c=512)[:, :, ci * 256:(ci + 1) * 256]
                        dma(POOL, wqk.sem, dst, src, R=[sc_r], W=[wqk.r])
                    kcol0 = 256
                else:
                    kcol0 = 0
                wqv = wview(wqk, 16, 512)
                for d2 in range(2):
                    dc = 2 * hd + d2
                    pl = FP.next()
                    mm_group(pl.t[:, 0:Tt], [(W2b[0:16, dc * 128:(dc + 1) * 128], glr.t[0:16, 0:Tt])], R=[glr.r], W=[pl.r])
                    la = P32.next()
                    act(la.t[:, 0:Tt], pl.t[:, 0:Tt], AF.Exp, R=[pl.r], W=[la.r], scale=-1.0, bias=nbg[:, dc:dc + 1])
                    act(la.t[:, 0:Tt], la.t[:, 0:Tt], AF.Ln, R=[], W=[la.r], bias=1.0)
                    op(DVE, lambda la=la: V.tensor_scalar(out=la.t[:, 0:Tt], in0=la.t[:, 0:Tt], scalar1=-1.0 / 16,
                                                          scalar2=-1.0, op0=ALU.mult, op1=ALU.max), R=[la.r], W=[la.r])
                    bc = P32.next()
                    op(DVE, lambda la=la, bc=bc: V.tensor_tensor_scan(
                        out=bc.t[:, 0:Tt], data0=rmask[:, 0:Tt], data1=la.t[:, 0:Tt], initial=0.0,
                        op0=ALU.mult, op1=ALU.add), R=[la.r], W=[bc.r])
                    bc3 = bc.t[:, 0:Tt].rearrange("p (j t) -> p j t", t=128)
                    A2 = P32.next()
                    A23 = A2.t[:, 0:Tt].rearrange("p (j t) -> p j t", t=128)
                    op(DVE, lambda bc3=bc3, A23=A23: V.tensor_tensor(
                        out=A23, in0=bc3[:, :, 127:128].broadcast_to([128, nblk, 128]), in1=bc3, op=ALU.subtract),
                       R=[bc.r], W=[A2.r])
                    if full:
                        A1 = P32.next()
                        A13 = A1.t[:, 0:Tt].rearrange("p (j t) -> p j t", t=128)
                        op(DVE, lambda bc3=bc3, A13=A13: V.tensor_tensor(
                            out=A13, in0=bc3, in1=bc3[:, :, 63:64].broadcast_to([128, nblk, 128]), op=ALU.subtract),
                           R=[bc.r], W=[A1.r])
                        Ek = P32.next()
                        act(Ek.t[:, 0:Tt], A1.t[:, 0:Tt], AF.Exp, R=[A1.r], W=[Ek.r], scale=-1.0)
                        act(A1.t[:, 0:Tt], A1.t[:, 0:Tt], AF.Exp, R=[], W=[A1.r])
                    act(A2.t[:, 0:Tt], A2.t[:, 0:Tt], AF.Exp, R=[A2.r], W=[A2.r])
                    act(dec.t[:, dc, 0:nblk], bc3[:, :, 127], AF.Exp, R=[bc.r], W=[dec.r])
                    if full:
                        act(bc.t[:, 0:Tt], bc.t[:, 0:Tt], AF.Exp, R=[], W=[bc.r])
                        pq_ = FP.next()
                        mm_group(pq_.t[:, 0:Tt], [(wqv[:, kc, d2 * 128:(d2 + 1) * 128], hT[:, kc, 0:Tt]) for kc in range(16)],
                                 R=hR + [wqk.r], W=[pq_.r])
                        op(DVE, lambda pq_=pq_, A1=A1, d2=d2: V.scalar_tensor_tensor(
                            out=qp[d2][:, 0:Tt], in0=pq_.t[:, 0:Tt], scalar=1.0 / 16, in1=A1.t[:, 0:Tt],
                            op0=ALU.mult, op1=ALU.mult), R=[pq_.r, A1.r], W=bigr(24 + d2))
                        op(DVE, lambda pq_=pq_, bc=bc, d2=d2: V.scalar_tensor_tensor(
                            out=qt[d2][:, 0:Tt], in0=pq_.t[:, 0:Tt], scalar=1.0 / 16, in1=bc.t[:, 0:Tt],
                            op0=ALU.mult, op1=ALU.mult), R=[pq_.r, bc.r], W=bigr(26 + d2))
                    pk_ = FP.next()
                    mm_group(pk_.t[:, 0:Tt], [(wqv[:, kc, kcol0 + d2 * 128:kcol0 + (d2 + 1) * 128], hT[:, kc, 0:Tt])
                                              for kc in range(16)], R=hR + [wqk.r], W=[pk_.r])
                    if full:
                        op(DVE, lambda pk_=pk_, Ek=Ek, d2=d2: V.tensor_tensor(
                            out=kp[d2][:, 0:Tt], in0=pk_.t[:, 0:Tt], in1=Ek.t[:, 0:Tt], op=ALU.mult),
                           R=[pk_.r, Ek.r], W=bigr(28 + d2))
                        op(DVE, lambda d2=d2: V.tensor_tensor(
                            out=kz[d2][:, 0:Tt], in0=kp[d2][:, 0:Tt], in1=hmask[:, 0:Tt], op=ALU.mult),
                           R=bigr(28 + d2), W=bigr(30 + d2))
                    op(DVE, lambda pk_=pk_, A2=A2, d2=d2: V.tensor_tensor(
                        out=kh[d2][:, 0:Tt], in0=pk_.t[:, 0:Tt], in1=A2.t[:, 0:Tt], op=ALU.mult),
                       R=[pk_.r, A2.r], W=bigr(32 + d2))
                wv_ = load_w_std(w_in_d, C_V + hd * 512)
                wvv = wview(wv_, 16, 512)
                for j in range(nblk):
                    if j == nblk // 2:
                        if full and hd == 0:
                            dump(4, qp[0][:, 0:Tt], bigr(24), Tt)
                            dump(5, qt[0][:, 0:Tt], bigr(26), Tt)
                            dump(6, kp[0][:, 0:Tt], bigr(28), Tt)
                            dump(7, kh[0][:, 0:Tt], bigr(32), Tt)
                            dump(2, dec.t[:, :, :].rearrange("p a b -> p (a b)"), [dec.r], 8 * NB, f32=True)
                        ph = HP.next()
                        pe_multi([lambda j=j, d2=d2, ph=ph: nc.tensor.transpose(
                            ph.t[:, (j * 2 + d2) * 128:(j * 2 + d2 + 1) * 128], kh[d2][:, j * 128:(j + 1) * 128], ident[:])
                            for j in range(nblk) for d2 in range(2)], R=bigr(32, 2), W=[ph.r])
                        act(khT[:, 0:nblk, :], ph.t[:, 0:nblk * 256].rearrange("p (j d) -> p j d", d=256), AF.Copy,
                            R=[ph.r], W=khT_r)

                    pv = FP.next()
                    mm_group(pv.t[:, 0:512], [(hT[:, kc, j * 128:(j + 1) * 128], wvv[:, kc, :]) for kc in range(16)],
                             R=[hT_r[j], wv_.r], W=[pv.r])
                    act(vt[:, j, :], pv.t[:, 0:512], AF.Copy, R=[pv.r], W=vt_r)
                if full:
                    wog = load_w_std(w_in_d, C_OG + hd * 512)
                    wogv = wview(wog, 16, 512)
                def e_pp(j):
                    jc = slice(j * 128, (j + 1) * 128)
                    pp = FP.next()
                    pe_multi([
                        lambda: nc.tensor.matmul(pp.t[:, 0:64], kz[0][:, jc], qp[0][:, j * 128:j * 128 + 64],
                                                 start=True, stop=False),
                        lambda: nc.tensor.matmul(pp.t[:, 0:64], kz[1][:, jc], qp[1][:, j * 128:j * 128 + 64],
                                                 start=False, stop=True),
                        lambda: nc.tensor.matmul(pp.t[:, 64:128], kp[0][:, jc], qp[0][:, j * 128 + 64:(j + 1) * 128],
                                                 start=True, stop=False),
                        lambda: nc.tensor.matmul(pp.t[:, 64:128], kp[1][:, jc], qp[1][:, j * 128 + 64:(j + 1) * 128],
                                                 start=False, stop=True)],
                        R=bigr(24, 2) + bigr(28, 4), W=[pp.r])
                    at_ = atb.next()
                    op(DVE, lambda: V.tensor_tensor(out=at_.t[:, 0:128], in0=pp.t[:, 0:128],
                                                    in1=tri[:], op=ALU.mult), R=[pp.r], W=[at_.r])
                    return at_

                def e_pog(j):
                    jc = slice(j * 128, (j + 1) * 128)
                    pog = FP.next()
                    mm_group(pog.t[:, 0:512], [(hT[:, kc, jc], wogv[:, kc, :]) for kc in range(16)],
                             R=[hT_r[j], wog.r], W=[pog.r])
                    return pog

                def e_po(j, at_):
                    jc = slice(j * 128, (j + 1) * 128)
                    po = FP.next()
                    mm_group(po.t[:, 0:512], [(at_.t[:, 0:128], vt[:, j, :]),
                                              (qt[0][:, jc], Sb[:, 2 * hd, :]),
                                              (qt[1][:, jc], Sb[:, 2 * hd + 1, :])],
                             R=[at_.r] + vt_r + bigr(26, 2) + Sb_r[2 * hd:2 * hd + 2], W=[po.r])
                    return po

                def e_state(j):
                    for d2 in range(2):
                        dc = 2 * hd + d2
                        psu = FP.next()
                        mm_group(psu.t[:, 0:512], [(khT[:, j, d2 * 128:(d2 + 1) * 128], vt[:, j, :])],
                                 R=khT_r + vt_r, W=[psu.r])
                        op(DVE, lambda: V.scalar_tensor_tensor(
                            out=S[:, dc, :], in0=S[:, dc, :], scalar=dec.t[:, dc, j:j + 1], in1=psu.t[:, 0:512],
                            op0=ALU.mult, op1=ALU.add), R=[psu.r, dec.r], W=[S_r[dc]])
                        if full:
                            act(Sb[:, dc, :], S[:, dc, :], AF.Copy, R=[S_r[dc]], W=[Sb_r[dc]])

                def e_gate(j, po, pog):
                    act(junk[:], po.t[:, 0:512], AF.Square, R=[po.r], W=[st_r], accum_out=st[:, 24:25])
                    act(st[:, 25:26], st[:, 24:25], AF.Ln, R=[], W=[st_r], scale=1.0 / 512, bias=EPS)
                    act(st[:, 25:26], st[:, 25:26], AF.Exp, R=[], W=[st_r], scale=-0.5)
                    sgo = P32W.next()
                    act(sgo.t[:, 0:512], pog.t[:, 0:512], AF.Silu, R=[pog.r], W=[sgo.r])
                    yb_ = ybt_ring.next()
                    op(DVE, lambda: V.scalar_tensor_tensor(
                        out=yb_.t, in0=po.t[:, 0:512], scalar=st[:, 25:26], in1=sgo.t[:, 0:512],
                        op0=ALU.mult, op1=ALU.mult), R=[po.r, sgo.r, st_r], W=[yb_.r])
                    return yb_

                def e_ytrans(j, yb_):
                    jc = slice(j * 128, (j + 1) * 128)
                    ph2 = HP.next()
                    pe_multi([lambda e=e: nc.tensor.transpose(
                        ph2.t[:, e * 128:(e + 1) * 128], yb_.t[:, e * 128:(e + 1) * 128], ident[:])
                        for e in range(4)], R=[yb_.r], W=[ph2.r])
                    ybd = big[:, (8 + hd * 4) * T:(12 + hd * 4) * T].rearrange("p (e t) -> p e t", t=T)[:, :, jc]
                    op(DVE, lambda: V.tensor_tensor(
                        out=ybd, in0=ph2.t[:, 0:512].rearrange("p (e t) -> p e t", t=128),
                        in1=gn[:, 0:4].unsqueeze(2).broadcast_to([128, 4, 128]), op=ALU.mult),
                       R=[ph2.r], W=bigr(8 + hd * 4, 4))

                if not full:
                    for j in range(nblk):
                        e_state(j)
                    if hook is not None:
                        hook(hd)
                else:
                    at_n = e_pp(0)
                    pog_n = e_pog(0)
                    for j in range(nblk):
                        at_c, pog_c = at_n, pog_n
                        po = e_po(j, at_c)
                        yb_ = e_gate(j, po, pog_c)
                        e_state(j)
                        if j + 1 < nblk:
                            at_n = e_pp(j + 1)
                            pog_n = e_pog(j + 1)
                        e_ytrans(j, yb_)
            if not full:
                for j in range(nblk):
                    op(DVE, lambda j=j: V.tensor_tensor(out=Dtot.t[:], in0=Dtot.t[:], in1=dec.t[:, :, j], op=ALU.mult),
                       R=[dec.r], W=[Dtot.r])
                return
            dump(8, bigc(8)[:, 0:Tt], bigr(8), Tt)
            dump(9, bigc(23)[:, 0:Tt], bigr(23), Tt)
            dump(3, S[:, 0, :], [S_r[0]], 512, f32=True)
            for fg in range(4):
                sa, sbb = [], []
                for (c0, lst) in ((C_MA, sa), (C_MB, sbb)):
                    wm = load_w_std(w_in_d, c0 + fg * 512)
                    wmv = wview(wm, 16, 512)
                    for i in range(4):
                        pm = FP.next()
                        mm_group(pm.t[:, 0:Tt], [(wmv[:, kc, i * 128:(i + 1) * 128], hT[:, kc, 0:Tt]) for kc in range(16)],
                                 R=hR + [wm.r], W=[pm.r])
                        sg_ = P32.next()
                        act(sg_.t[:, 0:Tt], pm.t[:, 0:Tt], AF.Sigmoid, R=[pm.r], W=[sg_.r])
                        lst.append(sg_)
                wa_ = load_w_std(wpa_d, fg * 512, nrows=1024)
                wav = wview(wa_, 8, 512)
                for i in range(4):
                    pa_ = FP.next()
                    mm_group(pa_.t[:, 0:Tt], [(wav[:, kc, i * 128:(i + 1) * 128], bigc(kc)[:, 0:Tt]) for kc in range(8)],
                             R=bigr(0, 8) + [wa_.r], W=[pa_.r])
                    op(DVE, lambda pa_=pa_, s_=sa[i]: V.tensor_tensor(out=s_.t[:, 0:Tt], in0=pa_.t[:, 0:Tt],
                                                                     in1=s_.t[:, 0:Tt], op=ALU.mult),
                       R=[pa_.r, sa[i].r], W=[sa[i].r])
                wb_ = load_w_std(wpb_d, fg * 512)
                wbv = wview(wb_, 16, 512)
                for i in range(4):
                    pb_ = FP.next()
                    mm_group(pb_.t[:, 0:Tt], [(wbv[:, kc, i * 128:(i + 1) * 128], bigc(8 + kc)[:, 0:Tt]) for kc in range(16)],
                             R=bigr(8, 16) + [wb_.r], W=[pb_.r])
                    op(DVE, lambda pb_=pb_, s_=sbb[i]: V.tensor_tensor(out=s_.t[:, 0:Tt], in0=pb_.t[:, 0:Tt],
                                                                      in1=s_.t[:, 0:Tt], op=ALU.mult),
                       R=[pb_.r, sbb[i].r], W=[sbb[i].r])
                    op(DVE, lambda i=i, fg=fg, a_=sa[i], b_=sbb[i]: V.tensor_tensor(
                        out=bigc(24 + fg * 4 + i)[:, 0:Tt], in0=a_.t[:, 0:Tt], in1=b_.t[:, 0:Tt], op=ALU.add),
                       R=[sa[i].r, sbb[i].r], W=bigr(24 + fg * 4 + i))
            dump(10, bigc(24)[:, 0:Tt], bigr(24), Tt)
            dump(11, bigc(39)[:, 0:Tt], bigr(39), Tt)
            load_gbc(post_mix_d)
            for fg in range(4):
                wo4 = load_w_std(wout_d, fg * 512)
                wov = wview(wo4, 16, 512)
                for j in range(nblk):
                    py = FP.next()
                    mm_group(py.t[:, 0:512], [(bigc(24 + kc)[:, j * 128:(j + 1) * 128], wov[:, kc, :]) for kc in range(16)],
                             R=bigr(24, 16) + [wo4.r], W=[py.r])
                    y_evac(py, j, fg, "A")
            for j in range(nblk):
                residual_update(j, "A")
            dump(5, xs[:, 0, 0:512], [xs_r[0]], 512, f32=True)

        def stageB(nblk):
            Tt = 128 * nblk
            rmsnorm_T([(xs[:, j, :], xs_r[j]) for j in range(nblk)], gpx, nblk)
            hR = hT_r[0:nblk]
            wq_ = load_w_std(wq_d, 0)
            wqv = wview(wq_, 16, 512)
            xq = [bigc(hd) for hd in range(4)]
            xo = [bigc(4 + hd) for hd in range(4)]
            for hd in range(4):
                pq_ = FP.next()
                mm_group(pq_.t[:, 0:Tt], [(wqv[:, kc, hd * 128:(hd + 1) * 128], hT[:, kc, 0:Tt]) for kc in range(16)],
                         R=hR + [wq_.r], W=[pq_.r])
                act(xq[hd][:, 0:Tt], pq_.t[:, 0:Tt], AF.Copy, R=[pq_.r], W=bigr(hd))
            wo_ = wring.next()
            dma(POOL, wo_.sem, wo_.t[:, 0:4 * D].rearrange("p (k c) -> p k c", c=D),
                SCR[id(wo_d)][0].ap().rearrange("(k p) c -> p k c", p=128), R=[SCR[id(wo_d)][1]], W=[wo_.r])
            wov = wo_.t[:, 0:4 * D].rearrange("p (k c) -> p k c", c=D)
            load_gbc(post_xa_d)
            sc = 128.0 ** -0.5
            for j in range(nblk):
                jc = slice(j * 128, (j + 1) * 128)
                for hd in range(4):
                    ps_ = FP.next()
                    mm_group(ps_.t[:, 0:256], [(xq[hd][:, jc], xk[:, hd, :])], R=bigr(hd), W=[ps_.r])
                    op(DVE, lambda ps_=ps_: V.reduce_max(out=st[:, 32:33], in_=ps_.t[:, 0:256], axis=AX.X),
                       R=[ps_.r], W=[st_r])
                    op(DVE, lambda: V.tensor_scalar(out=st[:, 33:34], in0=st[:, 32:33], scalar1=-sc, scalar2=None,
                                                    op0=ALU.mult), W=[st_r])
                    pe_ = P32.next()
                    act(pe_.t[:, 0:256], ps_.t[:, 0:256], AF.Exp, R=[ps_.r, st_r], W=[pe_.r, st_r], scale=sc,
                        bias=st[:, 33:34], accum_out=st[:, 34:35])
                    op(DVE, lambda: V.reciprocal(out=st[:, 35:36], in_=st[:, 34:35]), R=[st_r], W=[st_r])
                    pn = atb.next()
                    op(DVE, lambda pe_=pe_, pn=pn: V.tensor_scalar(out=pn.t[:, 0:256], in0=pe_.t[:, 0:256],
                                                                   scalar1=st[:, 35:36], scalar2=None, op0=ALU.mult),
                       R=[pe_.r, st_r], W=[pn.r])
                    ph = HP.next()
                    pe_multi([lambda mb=mb, ph=ph, pn=pn: nc.tensor.transpose(
                        ph.t[:, mb * 128:(mb + 1) * 128], pn.t[:, mb * 128:(mb + 1) * 128], ident[:])
                        for mb in range(2)], R=[pn.r], W=[ph.r])
                    pt = ptb.next()
                    act(pt.t[:, 0:256], ph.t[:, 0:256], AF.Copy, R=[ph.r], W=[pt.r])
                    po = FP.next()
                    mm_group(po.t[:, 0:128], [(xv[:, mb, hd * 128:(hd + 1) * 128], pt.t[:, mb * 128:(mb + 1) * 128])
                                              for mb in range(2)], R=[pt.r], W=[po.r])
                    act(xo[hd][:, jc], po.t[:, 0:128], AF.Copy, R=[po.r], W=bigr(4 + hd))
                for fg in range(4):
                    py = FP.next()
                    mm_group(py.t[:, 0:512], [(xo[hd][:, jc], wov[:, hd, fg * 512:(fg + 1) * 512]) for hd in range(4)],
                             R=bigr(4, 4) + [wo_.r], W=[py.r])
                    y_evac(py, j, fg, "B")
                residual_update(j, "B")

        def stageC(nblk, first, out_blk0):
            Tt = 128 * nblk
            rmsnorm_T([(xs[:, j, :], xs_r[j]) for j in range(nblk)], gpf, nblk)
            hR = hT_r[0:nblk]

            ringA = Ring(P32.items[0:4])
            ringB = Ring(P32.items[4:8])

            def conv_chunk(w_, wv_, i, c, gate):
                ph = FP.next()
                mm_group(ph.t[:, 0:Tt], [(wv_[:, kc, i * 128:(i + 1) * 128], hT[:, kc, 0:Tt]) for kc in range(16)],
                         R=hR + [w_.r], W=[ph.r])
                hb = ringA.next()
                if first:
                    ph2 = FP.next()
                    mm_group(ph2.t[:, 0:2], [(wv_[:, kc, i * 128:(i + 1) * 128], hprev.t[:, kc, :]) for kc in range(16)],
                             R=[hprev.r, w_.r], W=[ph2.r])
                    act(hb.t[:, 0:2], ph2.t[:, 0:2], AF.Copy, R=[ph2.r], W=[hb.r], scale=cm[:, 0:1])
                else:
                    act(hb.t[:, 0:2], halo[:, c, :], AF.Copy, R=[halo_r], W=[hb.r])
                act(hb.t[:, 2:Tt + 2], ph.t[:, 0:Tt], AF.Copy, R=[ph.r], W=[hb.r])
                act(halo[:, c, :], hb.t[:, Tt:Tt + 2], AF.Copy, R=[hb.r], W=[halo_r])
                acc = (ringB if gate else ringA).next()
                act(acc.t[:, 0:Tt], ph.t[:, 0:Tt], AF.Identity, R=[ph.r], W=[acc.r], scale=cw[:, 2, c:c + 1],
                    bias=cb[:, c:c + 1])
                op(DVE, lambda: V.scalar_tensor_tensor(out=acc.t[:, 0:Tt], in0=hb.t[:, 1:Tt + 1], scalar=cw[:, 1, c:c + 1],
                                                       in1=acc.t[:, 0:Tt], op0=ALU.mult, op1=ALU.add),
                   R=[hb.r, acc.r], W=[acc.r])
                op(DVE, lambda: V.scalar_tensor_tensor(out=acc.t[:, 0:Tt], in0=hb.t[:, 0:Tt], scalar=cw[:, 0, c:c + 1],
                                                       in1=acc.t[:, 0:Tt], op0=ALU.mult, op1=ALU.add),
                   R=[hb.r, acc.r], W=[acc.r])
                if gate:
                    act(acc.t[:, 0:Tt], acc.t[:, 0:Tt], AF.Gelu_apprx_tanh, R=[acc.r], W=[acc.r])
                return acc

            for gt in range(11):
                wg = load_w_std(wup_d, gt * 512)
                wgv = wview(wg, 16, 512)
                gas = [conv_chunk(wg, wgv, i, gt * 4 + i, True) for i in range(4)]
                wu = load_w_std(wup_d, DFF + gt * 512)
                wuv = wview(wu, 16, 512)
                for i in range(4):
                    c = gt * 4 + i
                    ga = gas[i]
                    au = conv_chunk(wu, wuv, i, c + 44, False)
                    op(DVE, lambda ga=ga, au=au, c=c: V.tensor_tensor(out=bigc(c)[:, 0:Tt], in0=ga.t[:, 0:Tt],
                                                                     in1=au.t[:, 0:Tt], op=ALU.mult),
                       R=[ga.r, au.r], W=bigr(c))
            load_gbc(post_ffn_d)
            pieces = [(0, 16), (16, 16), (32, 12)]
            for fg in range(4):
                pys = [FP.next() for _ in range(nblk)]
                for pi, (k0, nk) in enumerate(pieces):
                    wd = load_w_std(wdn_d, fg * 512, nrows=nk * 128, r0=k0 * 128)
                    wdv = wview(wd, nk, 512)
                    for j in range(nblk):
                        py = pys[j]
                        for r in bigr(k0, nk) + [wd.r]:
                            PE.wait(r.w)
                        if pi == 0:
                            PE.wait(py.r.w)
                            for t in py.r.rs.values():
                                PE.wait(t)
                        for kc in range(nk):
                            ins = nc.tensor.matmul(py.t[:, 0:512], bigc(k0 + kc)[:, j * 128:(j + 1) * 128], wdv[:, kc, :],
                                                   start=(pi == 0 and kc == 0), stop=(pi == 2 and kc == nk - 1))
                        PE.sem.n += 1
                        ins.then_inc(PE.sem.h, 1)
                        tok = (PE.sem, PE.sem.n)
                        for r in bigr(k0, nk) + [wd.r]:
                            r.rs["pe"] = tok
                        py.r.w = tok
                        py.r.rs = {}
                for j in range(nblk):
                    y_evac(pys[j], j, fg, "C")
            for j in range(nblk):
                residual_update(j, "C")
                dma(SP, out_sem[j], out_d.ap()[(out_blk0 + j) * 128:(out_blk0 + j + 1) * 128, :], xs[:, j, :],
                    R=[xs_r[j]])

        for mb in range(2):
            dma(SP, xs_sem[0], xs[:, 0, :], mem_d.ap()[mb * 128:(mb + 1) * 128, :], W=[xs_r[0]])
            xa, xr = xs[:, 0, :], xs_r[0]
            act(xn.t[:], xa, AF.Square, R=[xr], W=[xn.r, st_r], accum_out=st[:, 0:1])
            act(st[:, 1:2], st[:, 0:1], AF.Ln, R=[], W=[st_r], scale=1.0 / D, bias=EPS)
            act(st[:, 1:2], st[:, 1:2], AF.Exp, R=[], W=[st_r], scale=-0.5)
            act(xn.t[:], xa, AF.Copy, R=[xr, st_r], W=[xn.r], scale=st[:, 1:2])
            for h in range(2):
                pb_ = HP.next()
                pe_multi([lambda i=i, pb_=pb_, h=h: nc.tensor.transpose(
                    pb_.t[:, i * 128:(i + 1) * 128], xn.t[:, (h * 8 + i) * 128:(h * 8 + i + 1) * 128], ident[:])
                    for i in range(8)], R=[xn.r], W=[pb_.r])
                op(DVE, lambda h=h, pb_=pb_, mb=mb: V.tensor_tensor(
                    out=hT[:, h * 8:(h + 1) * 8, mb * 128:(mb + 1) * 128],
                    in0=pb_.t[:, 0:1024].rearrange("p (k t) -> p k t", k=8),
                    in1=gmem[:, h * 8:(h + 1) * 8].unsqueeze(2).broadcast_to([128, 8, 128]), op=ALU.mult),
                   R=[pb_.r, cst_r], W=[hT_r[mb]])
        wk_ = load_w_std(wk_d, 0)
        wkv = wview(wk_, 16, 512)
        for hd in range(4):
            pk_ = FP.next()
            mm_group(pk_.t[:, 0:256], [(wkv[:, kc, hd * 128:(hd + 1) * 128], hT[:, kc, 0:256]) for kc in range(16)],
                     R=hT_r[0:2] + [wk_.r], W=[pk_.r])
            act(xk[:, hd, :], pk_.t[:, 0:256], AF.Copy, R=[pk_.r], W=[cst_r])
        wv2 = load_w_std(wv_d, 0)
        wvv2 = wview(wv2, 16, 512)
        for mb in range(2):
            pv = FP.next()
            mm_group(pv.t[:, 0:512], [(hT[:, kc, mb * 128:(mb + 1) * 128], wvv2[:, kc, :]) for kc in range(16)],
                     R=hT_r[0:2] + [wv2.r], W=[pv.r])
            act(xv[:, mb, :], pv.t[:, 0:512], AF.Copy, R=[pv.r], W=[cst_r])

        if use_cc:
            for t in range(NT2):
                load_x(t * NB, NB)
                stageA(NB, 1)
            cin = cin_d.ap()
            dma(SP, ccio_sem, cin[0:1024, :].rearrange("(dc d) e -> d dc e", d=128), S[:], R=S_r)
            dma(SP, ccio_sem, cin[1024:1026, :].rearrange("a x -> (a x)").rearrange("(d dc) -> d dc", dc=8),
                Dtot.t[:], R=[Dtot.r])
            POOL.wait((ccio_sem, ccio_sem.n))
            POOL.wait((kv_sem, kv_sem.n))
            for sc_t_, sc_r_ in SCR.values():
                POOL.wait(sc_r_.w)
            for sl_ in wsl:
                POOL.wait((sl_.sem, sl_.sem.n))
            POOL.wait((cst_sem, cst_sem.n))
            ins = nc.gpsimd.collective_compute("AllGather", ALU.bypass, replica_groups=[list(range(NCORES))],
                                               ins=[cin_d.ap().opt()], outs=[cout_d.ap().opt()])
            cc_sem.n += 1
            ins.then_inc(cc_sem.h, 1)
            cc_tok = (cc_sem, cc_sem.n)
            SP.wait(cc_tok)
            DVE.wait((ccio_sem, ccio_sem.n))
            op(DVE, lambda: V.memset(S[:], 0.0), W=S_r)
            cout = cout_d.ap()
            Sl = big[:, 0:8192].bitcast(F32).rearrange("p (dc e) -> p dc e", e=512)
            yh_r = big_r[0:16]
            for r in range(NCORES):
                dma(SP, yl_sem, Sl, cout[r * CROWS:r * CROWS + 1024, :].rearrange("(dc d) e -> d dc e", d=128),
                    W=yh_r)
                dma(SP, Dl.sem, Dl.t[:], cout[r * CROWS + 1024:r * CROWS + 1026, :].rearrange("a x -> (a x)")
                    .rearrange("(d dc) -> d dc", dc=8), W=[Dl.r])
                op(DVE, lambda r=r: V.tensor_scalar(out=De[:], in0=Dl.t[:], scalar1=cm[:, 1 + r:2 + r],
                                                    scalar2=omm[:, 1 + r:2 + r], op0=ALU.mult, op1=ALU.add),
                   R=[Dl.r], W=[De_r])
                for dc in range(8):
                    op(DVE, lambda dc=dc: V.tensor_scalar(out=S[:, dc, :], in0=S[:, dc, :], scalar1=De[:, dc:dc + 1],
                                                          scalar2=None, op0=ALU.mult), R=[De_r], W=[S_r[dc]])
                    op(DVE, lambda dc=dc, r=r: V.scalar_tensor_tensor(
                        out=S[:, dc, :], in0=Sl[:, dc, :], scalar=cm[:, 1 + r:2 + r], in1=S[:, dc, :],
                        op0=ALU.mult, op1=ALU.add), R=yh_r, W=[S_r[dc]])
        if not use_cc:
            NPRE = 3 * SEG // T
            bufs = [(hT, hT_r), (hT2, hT2_r)]
            load_x(0, NB, src=xpre_d)
            rmsnorm_T([(xs[:, j, :], xs_r[j]) for j in range(NB)], gpm, NB, hsel=bufs[0])
            if NPRE > 1:
                load_x(NB, NB, src=xpre_d)
            for t in range(NPRE):
                cur["hT"], cur["hT_r"] = bufs[t % 2]
                nxt = bufs[(t + 1) % 2]

                def hook(hd, t=t, nxt=nxt):
                    if t + 1 >= NPRE:
                        return
                    rmsnorm_T([(xs[:, j, :], xs_r[j]) for j in range(NB)], gpm, NB, hsel=nxt, only=hd)
                    dma(SP, xs_sem[hd], xs[:, hd, :],
                        xpre_d.ap()[((t + 2) * NB + hd) * 128:((t + 2) * NB + hd + 1) * 128, :], W=[xs_r[hd]]) \
                        if t + 2 < NPRE else None

                stageA(NB, 1, pre_done=True, hook=hook)
            cur["hT"], cur["hT_r"] = hT, hT_r
            barrier()
        for dc in range(8):
            act(Sb[:, dc, :], S[:, dc, :], AF.Copy, R=[S_r[dc]], W=[Sb_r[dc]])

        load_x(0, 1)
        stageA(1, 2)
        if "B" in stages:
            stageB(1)
        if "C" in stages:
            rmsnorm_T([(xs[:, 0, :], xs_r[0])], gpf, 1)
            op(DVE, lambda: V.tensor_copy(out=hprev.t[:], in_=hT[:, :, 126:128]), R=[hT_r[0]], W=[hprev.r])
        for t in range(NT2):
            load_x(1 + t * NB, NB)
            DBG["on"] = (t == 0)
            stageA(NB, 2)
            DBG["on"] = False
            if "B" in stages:
                stageB(NB)
            if "C" in stages:
                stageC(NB, t == 0, t * NB)
            else:
                for j in range(NB):
                    dma(SP, out_sem[j], out_d.ap()[(t * NB + j) * 128:(t * NB + j + 1) * 128, :], xs[:, j, :],
                        R=[xs_r[j]])
        for j in range(NB):
            SP.wait((out_sem[j], out_sem[j].n))
        SP.wait((dbg_sem, dbg_sem.n))
        for E in (PE, ACT, DVE, POOL):
            SP.wait((E.sem, E.sem.n))
    return nc


WEIGHT_KEYS = ["w_in", "pre_norm_mix", "sg_ln_g", "sg_ln_b", "sg_w", "sg_b", "gla_w_gate2", "gla_b_gate",
               "gla_norm_g", "w_proj_a", "w_proj_b", "w_out", "post_norm_mix", "pre_norm_xa", "mem_norm_g",
               "xa_wq", "xa_wk", "xa_wv", "xa_wo", "post_norm_xa", "pre_norm_ffn", "ffn_w_up", "ffn_conv_w",
               "ffn_conv_b", "ffn_w_down", "post_norm_ffn"]

_CACHE = {}


def kernel(_stages="ABC", _use_cc=False, _cores=None, _debug=False, **inputs):
    x = np.asarray(inputs["x"], dtype=np.float32)
    mem = np.asarray(inputs["mem"], dtype=np.float32)
    B, SEQ, _ = x.shape
    assert B == 2
    SEG = SEQ // 4
    key = (SEG, _stages, _use_cc, _debug)
    if key not in _CACHE:
        _CACHE[key] = build(SEG, _stages, _use_cc, _debug)
    nc = _CACHE[key]
    shared = {}
    for k in WEIGHT_KEYS:
        a = np.asarray(inputs[k], dtype=np.float32)
        shared[k] = np.ascontiguousarray(a[0])
    in_maps = []
    for c in range(NCORES):
        b, p = divmod(c, 4)
        xe = np.zeros((SEG + 128, D), np.float32)
        if p > 0:
            xe[:] = x[b, p * SEG - 128:(p + 1) * SEG]
        else:
            xe[128:] = x[b, 0:SEG]
        cmv = np.zeros((128, 16), np.float32)
        cmv[:, 0] = 1.0 if p > 0 else 0.0
        for j in range(NCORES):
            if j // 4 == b and j < c:
                cmv[:, 1 + j] = 1.0
        m = dict(shared)
        m["x"] = xe
        m["mem"] = np.ascontiguousarray(mem[b])
        m["cm"] = cmv
        if not _use_cc:
            xp = np.zeros((3 * SEG, D), np.float32)
            n = p * SEG - 128
            if n > 0:
                xp[3 * SEG - n:] = x[b, 0:n]
            m["xpre"] = xp
        in_maps.append(m)
    if _cores is not None:
        res = run_bass_kernel_spmd(nc, [in_maps[c] for c in _cores], core_ids=list(range(len(_cores))))
        return res.results
    res = run_bass_kernel_spmd(nc, in_maps, core_ids=list(range(NCORES)))
    out = np.empty((B, SEQ, D), np.float32)
    for c in range(NCORES):
        b, p = divmod(c, 4)
        out[b, p * SEG:(p + 1) * SEG] = res.results[c]["out"]
    return out
```

```python
import contextlib
import numpy as np
import concourse.bass as bass
import concourse.mybir as mybir
from concourse.bass_utils import run_bass_kernel_spmd

F32 = mybir.dt.float32
BF16 = mybir.dt.bfloat16
AF = mybir.ActivationFunctionType
ALU = mybir.AluOpType
AX = mybir.AxisListType

D = 2048
N_IN = 12304
DFF = 5632
EPS = 1e-6
NB = 4
NCORES = 8
C_U, C_VS, C_Q, C_K, C_V, C_OG, C_GLR, C_MA, C_MB = 0, 1024, 2048, 3072, 4096, 6144, 8192, 8208, 10256
NW = 2
CROWS = 1026


class SemW:
    def __init__(self, h):
        self.h = h
        self.n = 0


class Eng:
    def __init__(self, raw, semw, name):
        self.raw = raw
        self.sem = semw
        self.name = name
        self.seen = {}

    def wait(self, tok):
        if tok is None:
            return
        sw, v = tok
        if v <= 0 or (sw is self.sem and self.name in ("pe", "sp")):
            return
        if self.seen.get(id(sw), 0) >= v:
            return
        self.raw.wait_ge(sw.h, v)
        self.seen[id(sw)] = v


class Reg:
    __slots__ = ("w", "rs")

    def __init__(self):
        self.w = None
        self.rs = {}


def op(E, fn, R=(), W=(), sig=True):
    for r in R:
        E.wait(r.w)
    for w in W:
        E.wait(w.w)
        for t in w.rs.values():
            E.wait(t)
    ins = fn()
    if sig or E.name != "pe":
        E.sem.n += 1
        ins.then_inc(E.sem.h, 1)
        tok = (E.sem, E.sem.n)
    else:
        tok = (E.sem, E.sem.n + 1)
    for r in R:
        r.rs[E.name] = tok
    for w in W:
        w.w = tok
        w.rs = {}
    return tok


def dma(Q, semw, out, in_, R=(), W=(), **kw):
    for r in R:
        Q.wait(r.w)
    for w in W:
        Q.wait(w.w)
        for t in w.rs.values():
            Q.wait(t)
    ins = Q.raw.dma_start(out=out, in_=in_, **kw)
    semw.n += 16
    ins.then_inc(semw.h, 16)
    tok = (semw, semw.n)
    for r in R:
        r.rs["dma%d" % id(semw)] = tok
    for w in W:
        w.w = tok
        w.rs = {}
    return tok


class Ring:
    def __init__(self, items):
        self.items = items
        self.i = 0

    def next(self):
        it = self.items[self.i]
        self.i = (self.i + 1) % len(self.items)
        return it


class Buf:
    def __init__(self, t):
        self.t = t
        self.r = Reg()


def build(SEG, stages="ABC", use_cc=True, debug=False):
    nc = bass.Bass("TRN2", target_bir_lowering=False)
    NBLK = SEG // 128 + 1
    T = 128 * NB
    assert (NBLK - 1) % NB == 0
    NT2 = (NBLK - 1) // NB

    def din(name, shape):
        return nc.dram_tensor(name, shape, F32, kind="ExternalInput")

    x_d = din("x", [NBLK * 128, D])
    xpre_d = None if use_cc else din("xpre", [3 * SEG, D])
    mem_d = din("mem", [256, D])
    cm_d = din("cm", [128, 16])
    w_in_d = din("w_in", [D, N_IN])
    pre_mix_d = din("pre_norm_mix", [D])
    sg_ln_g_d = din("sg_ln_g", [1024])
    sg_ln_b_d = din("sg_ln_b", [1024])
    sg_w_d = din("sg_w", [8, 128, 128])
    sg_b_d = din("sg_b", [8, 128])
    w2_d = din("gla_w_gate2", [16, 1024])
    bg_d = din("gla_b_gate", [1024])
    gn_d = din("gla_norm_g", [512])
    wpa_d = din("w_proj_a", [1024, D])
    wpb_d = din("w_proj_b", [D, D])
    wout_d = din("w_out", [D, D])
    post_mix_d = din("post_norm_mix", [D])
    pre_xa_d = din("pre_norm_xa", [D])
    memg_d = din("mem_norm_g", [D])
    wq_d = din("xa_wq", [D, 512])
    wk_d = din("xa_wk", [D, 512])
    wv_d = din("xa_wv", [D, 512])
    wo_d = din("xa_wo", [512, D])
    post_xa_d = din("post_norm_xa", [D])
    pre_ffn_d = din("pre_norm_ffn", [D])
    wup_d = din("ffn_w_up", [D, 2 * DFF])
    cw_d = din("ffn_conv_w", [3, 2 * DFF])
    cb_d = din("ffn_conv_b", [2 * DFF])
    wdn_d = din("ffn_w_down", [DFF, D])
    post_ffn_d = din("post_norm_ffn", [D])
    out_d = nc.dram_tensor("out", [SEG, D], F32, kind="ExternalOutput")
    dbgb_d = nc.dram_tensor("dbgb", [12, 128, 512], BF16, kind="ExternalOutput") if debug else None
    dbgf_d = nc.dram_tensor("dbgf", [12, 128, 512], F32, kind="ExternalOutput") if debug else None
    DBG = {"on": False}
    cin_d = nc.dram_tensor("cc_in", [CROWS, 512], F32)
    cout_d = nc.dram_tensor("cc_out", [NCORES * CROWS, 512], F32)

    es = contextlib.ExitStack()
    with es:
        def sb(name, shape, dt):
            return es.enter_context(nc.sbuf_tensor(name, shape, dt))

        def newsem(name):
            return SemW(es.enter_context(nc.semaphore(name)))

        PE = Eng(nc.tensor, newsem("s_pe"), "pe")
        ACT = Eng(nc.scalar, newsem("s_act"), "act")
        DVE = Eng(nc.vector, newsem("s_dve"), "dve")
        POOL = Eng(nc.gpsimd, newsem("s_pool"), "pool")
        SP = Eng(nc.sync, newsem("s_sp"), "sp")

        xs = sb("xs", [128, NB, D], F32)
        xs_r = [Reg() for _ in range(NB)]
        xs_sem = [newsem("xs%d" % j) for j in range(NB)]
        S = sb("S", [128, 8, 512], F32)
        S_r = [Reg() for _ in range(8)]
        Sb = sb("Sb", [128, 8, 512], BF16)
        Sb_r = [Reg() for _ in range(8)]
        hT = sb("hT", [128, 16, T], BF16)
        hT_r = [Reg() for _ in range(NB)]
        big = sb("big", [128, 44 * T], BF16)
        big_r = [Reg() for _ in range(44)]
        wsl = []
        for i in range(NW):
            b = Buf(sb("wsl%d" % i, [128, 16 * 512], BF16))
            b.sem = newsem("w%d" % i)
            wsl.append(b)
        wring = Ring(wsl)
        xn = Buf(sb("xn", [128, D], BF16))
        gbc = Buf(sb("gbc", [128, D], F32))
        gbc.sem = newsem("gbc")
        NP32 = 8
        p32all = sb("p32all", [128, NP32 * (T + 4)], F32)
        P32 = Ring([Buf(p32all[:, i * (T + 4):(i + 1) * (T + 4)]) for i in range(NP32)])
        p32w = sb("p32w", [128, 1024], F32)
        P32W = Ring([Buf(p32w[:, i * 512:(i + 1) * 512]) for i in range(2)])

        class _GV:
            t = p32w
        gv = _GV()
        gv_R = [b_.r for b_ in P32W.items]

        def yhv(stage, j):
            if stage == "A":
                if j < 3:
                    return big[:, 8 * j * T:8 * (j + 1) * T].bitcast(F32), big_r[8 * j:8 * j + 8]
                return hT[:, 0:8, :].rearrange("p k t -> p (k t)").bitcast(F32), list(hT_r)
            if stage == "B":
                return big[:, (8 + 8 * j) * T:(16 + 8 * j) * T].bitcast(F32), big_r[8 + 8 * j:16 + 8 * j]
            if j < 2:
                return hT[:, 8 * j:8 * j + 8, :].rearrange("p k t -> p (k t)").bitcast(F32), list(hT_r)
            lo = (j - 2) * 2048
            regs = [P32.items[i].r for i in range(NP32) if i * (T + 4) < lo + 2048 and (i + 1) * (T + 4) > lo]
            return p32all[:, lo:lo + 2048], regs
        junk = sb("junk", [128, 512], BF16)
        atb = Ring([Buf(sb("atb%d" % i, [128, 256], BF16)) for i in range(2)])
        ptb = Ring([Buf(sb("ptb%d" % i, [128, 256], BF16)) for i in range(2)])
        cvec = sb("cvec", [128, 128], F32)
        gpm = cvec[:, 0:16]
        gpx = cvec[:, 16:32]
        gpf = cvec[:, 32:48]
        gmem = cvec[:, 48:64]
        nbg = cvec[:, 64:72]
        gn = cvec[:, 72:76]
        lng = cvec[:, 76:84]
        rawv = big[:, 0:1024].bitcast(F32).rearrange("p (s c) -> p s c", c=128)
        cw = sb("cw", [128, 3, 88], F32)
        cb = sb("cb", [128, 88], F32)
        cm = sb("cm_sb", [128, 16], F32)
        omm = sb("omm", [128, 16], F32)
        ident = sb("ident", [128, 128], BF16)
        identf = sb("identf", [128, 128], F32)
        tri = sb("tri", [128, 128], F32)
        rmask = sb("rmask", [128, T], F32)
        hmask = sb("hmask", [128, T], BF16)
        WmT = sb("WmT", [128, 8, 128], BF16)
        Qsg = sb("Qsg", [128, 8, 128], F32)
        wsgb = big[:, 1024:2048].rearrange("p (g s) -> p g s", s=128)
        Bb = big[:, 2048:3072]
        bsbc = big[:, 3072:5120].bitcast(F32)
        xk = sb("xk", [128, 4, 256], BF16)
        xv = sb("xv", [128, 2, 512], BF16)
        W2b = sb("W2b", [16, 1024], BF16)
        Wglr = sb("Wglr", [128, 16, 16], BF16)
        glr = Buf(sb("glr", [16, T], BF16))
        halo = sb("halo", [128, 88, 2], F32)
        hprev = Buf(sb("hprev", [128, 16, 2], BF16))
        dec = Buf(sb("dec", [128, 8, NB], F32))
        Dtot = Buf(sb("Dtot", [128, 8], F32))
        Dl = Buf(sb("Dl", [128, 8], F32))
        Dl.sem = newsem("dl")
        De = sb("De", [128, 8], F32)
        st = sb("st", [128, 64], F32)
        st_r = Reg()
        halo_r = Reg()
        De_r = Reg()
        cst_r = Reg()
        cst_sem = newsem("cst")
        out_sem = [newsem("o%d" % j) for j in range(NB)]
        cc_sem = newsem("cc")
        ccio_sem = newsem("ccio")
        yl_sem = newsem("yl")

        print("SBUF bytes remaining per partition:", nc.sbuf_bytes_remaining)
        FP = Ring([Buf(es.enter_context(nc.psum_tensor("pf%d" % i, [128, 512], F32))) for i in range(6)])
        HP = Ring([Buf(es.enter_context(nc.psum_tensor("ph%d" % i, [128, 1024], BF16))) for i in range(2)])

        def bigc(c, n=1):
            return big[:, c * T:(c + n) * T]

        def bigr(c, n=1):
            return big_r[c:c + n]

        def mm_group(out_ap, pairs, R, W):
            n = len(pairs)
            for r in R:
                PE.wait(r.w)
            for w in W:
                PE.wait(w.w)
                for t in w.rs.values():
                    PE.wait(t)
            for i, (l, r_) in enumerate(pairs):
                ins = nc.tensor.matmul(out_ap, l, r_, start=(i == 0), stop=(i == n - 1))
            PE.sem.n += 1
            ins.then_inc(PE.sem.h, 1)
            tok = (PE.sem, PE.sem.n)
            for r in R:
                r.rs["pe"] = tok
            for w in W:
                w.w = tok
                w.rs = {}
            return tok

        def pe_multi(fns, R, W):
            for r in R:
                PE.wait(r.w)
            for w in W:
                PE.wait(w.w)
                for t in w.rs.values():
                    PE.wait(t)
            for f in fns:
                ins = f()
            PE.sem.n += 1
            ins.then_inc(PE.sem.h, 1)
            tok = (PE.sem, PE.sem.n)
            for r in R:
                r.rs["pe"] = tok
            for w in W:
                w.w = tok
                w.rs = {}
            return tok

        def act(out, in_, func, R, W, **kw):
            return op(ACT, lambda: nc.scalar.activation(out=out, in_=in_, func=func, **kw), R, W)

        SCR = {}

        conv_jobs = []

        def mkscratch(name, src_d, rows, cols, rchunk, now=False, front=False):
            t_ = nc.dram_tensor(name + "_bf", [rows, cols], BF16)
            reg = Reg()
            semw = newsem("cv_" + name)
            SCR[id(src_d)] = (t_, reg)
            r0s = list(range(0, rows, rchunk))
            jobs = [(t_, src_d, r0, rchunk, semw, reg, r0 == r0s[-1]) for r0 in r0s]
            if now or front:
                conv_jobs[0:0] = jobs
            else:
                conv_jobs.extend(jobs)
            if now:
                issue_conv(len(r0s))

        def issue_conv(n):
            for _ in range(n):
                if not conv_jobs:
                    return
                t_, src_d, r0, rchunk, semw, reg, last = conv_jobs.pop(0)
                ins = nc.gpsimd.dma_start(out=t_.ap()[r0:r0 + rchunk, :], in_=src_d.ap()[r0:r0 + rchunk, :])
                semw.n += 16
                ins.then_inc(semw.h, 16)
                if last:
                    reg.w = (semw, semw.n)

        def wsrc(dt_, r0, nr, c0, ncol):
            if dt_ is w_in_d and c0 >= C_K and c0 + ncol <= C_OG:
                return kv_t.ap()[r0:r0 + nr, c0 - C_K:c0 - C_K + ncol], kv_reg
            sc_t, sc_r = SCR[id(dt_)]
            return sc_t.ap()[r0:r0 + nr, c0:c0 + ncol], sc_r
        mkscratch("wq", wq_d, D, 512, D)
        mkscratch("wo", wo_d, 512, D, 512)
        mkscratch("wpa", wpa_d, 1024, D, 512)
        mkscratch("wpb", wpb_d, D, D, 512)
        mkscratch("wout", wout_d, D, D, 512)
        mkscratch("wup", wup_d, D, 2 * DFF, 128)
        mkscratch("wdn", wdn_d, DFF, D, 704)

        def load_w(pieces):
            slot = wring.next()
            for (dt_, r0, nr, c0, ncol, wdt) in pieces:
                nk = nr // 128
                src0, sc_r = wsrc(dt_, r0, nr, c0, ncol)
                src = src0.rearrange("(kc p) c -> p kc c", p=128)
                dst = slot.t[:, 0:nk * wdt].rearrange("p (kc c) -> p kc c", c=wdt)[:, :, 0:ncol]
                dma(POOL, slot.sem, dst, src, R=[sc_r], W=[slot.r])
            return slot

        def wview(slot, nk, wdt):
            return slot.t[:, 0:nk * wdt].rearrange("p (kc c) -> p kc c", c=wdt)

        def load_w_std(dt_, c0, ncol=512, nrows=D, r0=0):
            slot = wring.next()
            nk = nrows // 128
            src0, sc_r = wsrc(dt_, r0, nrows, c0, ncol)
            src = src0.rearrange("(kc p) c -> p kc c", p=128)
            dst = slot.t[:, 0:nk * 512].rearrange("p (kc c) -> p kc c", c=512)[:, :, 0:ncol]
            dma(POOL, slot.sem, dst, src, R=[sc_r], W=[slot.r])
            return slot

        def colvec(dst, dram1d, k):
            dma(SP, cst_sem, dst, dram1d.ap().rearrange("(k p) -> p k", p=128), W=[cst_r],
                allow_slow_non_contiguous=True)

        def rawrows(slot, r0, dram1d, k):
            dma(SP, cst_sem, rawv[r0:r0 + k, slot, :], dram1d.rearrange("(k p) -> k p", p=128), W=[cst_r])

        op(DVE, lambda: nc.vector.memset(rawv, 0.0), W=[cst_r])
        rawrows(0, 0, pre_mix_d.ap(), 16)
        rawrows(0, 16, pre_xa_d.ap(), 16)
        rawrows(0, 32, pre_ffn_d.ap(), 16)
        rawrows(0, 48, memg_d.ap(), 16)
        rawrows(0, 64, bg_d.ap(), 8)
        rawrows(0, 72, gn_d.ap(), 4)
        rawrows(0, 76, sg_ln_g_d.ap(), 8)
        rawrows(1, 0, cb_d.ap(), 88)
        rawrows(2, 0, cw_d.ap()[0, :], 88)
        rawrows(3, 0, cw_d.ap()[1, :], 88)
        dma(SP, cst_sem, cm[:], cm_d.ap(), W=[cst_r])
        dma(SP, cst_sem, bsbc, sg_b_d.ap().rearrange("g t -> (g t)").partition_broadcast(128), W=[cst_r])
        dma(POOL, cst_sem, W2b[:], w2_d.ap(), W=[cst_r])
        dma(POOL, cst_sem, Wglr[:], w_in_d.ap()[:, C_GLR:C_GLR + 16].rearrange("(kc p) c -> p kc c", p=128),
            W=[cst_r], allow_slow_non_contiguous=True)
        dma(POOL, cst_sem, wsgb, sg_w_d.ap().rearrange("g t s -> t g s"), W=[cst_r])
        dma(POOL, cst_sem, Bb, sg_ln_b_d.ap().partition_broadcast(128), W=[cst_r])
        V = nc.vector
        op(DVE, lambda: V.memset(identf[:], 1.0), W=[cst_r], sig=False)
        op(DVE, lambda: V.memset(tri[:], 1.0), sig=False)
        op(DVE, lambda: V.memset(rmask[:], 1.0), sig=False)
        op(DVE, lambda: V.memset(hmask[:], 0.0), sig=False)
        op(DVE, lambda: V.memset(halo[:], 0.0), sig=False)
        op(DVE, lambda: V.memset(S[:], 0.0), W=S_r, sig=False)
        op(DVE, lambda: V.memset(Dtot.t[:], 1.0), W=[Dtot.r], sig=False)
        for j in range(NB):
            op(DVE, lambda j=j: V.memset(rmask[:, j * 128:j * 128 + 1], 0.0), sig=False)
            op(DVE, lambda j=j: V.memset(hmask[:, j * 128:j * 128 + 64], 1.0), sig=False)
        tok_c = op(DVE, lambda: V.memset(st[:], 0.0), W=[st_r])
        POOL.wait(tok_c)
        G = nc.gpsimd
        op(POOL, lambda: G.affine_select(out=identf[:], in_=identf[:], pattern=[[-1, 128]],
                                         compare_op=ALU.is_equal, fill=0.0, base=0, channel_multiplier=1),
           W=[cst_r], sig=False)
        op(POOL, lambda: G.affine_select(out=tri[:], in_=tri[:], pattern=[[1, 128]],
                                         compare_op=ALU.is_ge, fill=0.0, base=0, channel_multiplier=-1),
           W=[cst_r], sig=False)
        op(POOL, lambda: G.tensor_copy(out=ident[:], in_=identf[:]), W=[cst_r])
        for sl, dst in ((0, cvec[:, :]), (1, None), (2, None), (3, None)):
            pr = FP.next()
            pe_multi([lambda sl=sl, pr=pr: nc.tensor.transpose(pr.t[:, 0:128], rawv[:, sl, :], identf[:])],
                     R=[cst_r], W=[pr.r])
            if sl == 0:
                op(DVE, lambda pr=pr: V.tensor_copy(out=cvec[:, :], in_=pr.t[:, 0:128]), R=[pr.r], W=[cst_r])
            elif sl == 1:
                op(DVE, lambda pr=pr: V.tensor_copy(out=cb[:, :], in_=pr.t[:, 0:88]), R=[pr.r], W=[cst_r])
            else:
                op(DVE, lambda pr=pr, sl=sl: V.tensor_copy(out=cw[:, sl - 2, :], in_=pr.t[:, 0:88]), R=[pr.r], W=[cst_r])
        dma(SP, cst_sem, rawv[0:88, 0, :], cw_d.ap()[2, :].rearrange("(k p) -> k p", p=128), W=[cst_r])
        pr = FP.next()
        pe_multi([lambda: nc.tensor.transpose(pr.t[:, 0:128], rawv[:, 0, :], identf[:])], R=[cst_r], W=[pr.r])
        op(DVE, lambda: V.tensor_copy(out=cw[:, 2, :], in_=pr.t[:, 0:88]), R=[pr.r], W=[cst_r])
        op(DVE, lambda: V.tensor_scalar(out=nbg, in0=nbg, scalar1=-1.0, scalar2=None, op0=ALU.mult),
           R=[cst_r], W=[cst_r])
        op(DVE, lambda: V.tensor_scalar(out=omm[:], in0=cm[:], scalar1=-1.0, scalar2=1.0, op0=ALU.mult,
                                        op1=ALU.add), R=[cst_r])
        pb = HP.next()
        pe_multi([lambda g=g: nc.tensor.transpose(pb.t[:, g * 128:(g + 1) * 128], wsgb[:, g, :], ident[:])
                  for g in range(8)], R=[cst_r], W=[pb.r])
        op(DVE, lambda: V.tensor_tensor(out=WmT[:], in0=pb.t[:, 0:1024].rearrange("p (g t) -> p g t", g=8),
                                        in1=tri[:].unsqueeze(1).broadcast_to([128, 8, 128]), op=ALU.mult),
           R=[pb.r], W=[cst_r])
        for h2 in range(2):
            pq = FP.next()
            pe_multi([lambda g=g: nc.tensor.matmul(pq.t[:, (g % 4) * 128:(g % 4 + 1) * 128],
                                                   Bb[:, g * 128:(g + 1) * 128], WmT[:, g, :],
                                                   start=True, stop=True)
                      for g in range(h2 * 4, h2 * 4 + 4)], R=[cst_r], W=[pq.r])
            op(DVE, lambda h2=h2, pq=pq: V.tensor_tensor(
                out=Qsg[:, h2 * 4:(h2 + 1) * 4, :], in0=pq.t[:, 0:512].rearrange("p (g t) -> p g t", g=4),
                in1=bsbc[:, h2 * 512:(h2 + 1) * 512].rearrange("p (g t) -> p g t", g=4), op=ALU.add),
               R=[pq.r], W=[cst_r])

        kv_t = nc.dram_tensor("w_in_kv_bf", [D, 3072], BF16)
        kv_reg = Reg()
        kv_sem = newsem("cv_kv")
        for r0 in range(0, D, 512):
            ins = nc.gpsimd.dma_start(out=kv_t.ap()[r0:r0 + 512, :], in_=w_in_d.ap()[r0:r0 + 512, C_K:C_OG])
            kv_sem.n += 16
            ins.then_inc(kv_sem.h, 16)
        kv_reg.w = (kv_sem, kv_sem.n)
        mkscratch("wk", wk_d, D, 512, D, now=True)
        mkscratch("wv", wv_d, D, 512, D, now=True)
        mkscratch("w_in", w_in_d, D, N_IN, 128, front=True)

        dbg_sem = newsem("dbg")

        def dump(idx, ap, regions, w, f32=False):
            if not (debug and DBG["on"]):
                return
            dst = (dbgf_d if f32 else dbgb_d).ap()[idx, :, 0:w]
            dma(SP, dbg_sem, dst, ap, R=regions)

        def barrier():
            toks = [(E.sem, E.sem.n) for E in (PE, ACT, DVE, POOL)] + [(cst_sem, cst_sem.n)]
            for E in (PE, ACT, DVE, POOL, SP):
                for t_ in toks:
                    E.wait(t_)

        barrier()

        cur = {"hT": hT, "hT_r": hT_r}
        hT2 = big[:, 0:16 * T].rearrange("p (k t) -> p k t", t=T)
        hT2_r = [Reg() for _ in range(NB)]

        def rmsnorm_T(blocks, gcol, nblk, hsel=None, only=None):
            hT = cur["hT"] if hsel is None else hsel[0]
            hT_r = cur["hT_r"] if hsel is None else hsel[1]
            for j, (xa, xr) in enumerate(blocks):
                if only is not None and j != only:
                    continue
                act(xn.t[:], xa, AF.Square, R=[xr], W=[xn.r, st_r], accum_out=st[:, 0:1])
                act(st[:, 1:2], st[:, 0:1], AF.Ln, R=[], W=[st_r], scale=1.0 / D, bias=EPS)
                act(st[:, 1:2], st[:, 1:2], AF.Exp, R=[], W=[st_r], scale=-0.5)
                act(xn.t[:], xa, AF.Copy, R=[xr, st_r], W=[xn.r], scale=st[:, 1:2])
                for h in range(2):
                    pb_ = HP.next()
                    pe_multi([lambda i=i, pb_=pb_, h=h: nc.tensor.transpose(
                        pb_.t[:, i * 128:(i + 1) * 128], xn.t[:, (h * 8 + i) * 128:(h * 8 + i + 1) * 128], ident[:])
                        for i in range(8)], R=[xn.r], W=[pb_.r])
                    op(DVE, lambda h=h, pb_=pb_, j=j: V.tensor_tensor(
                        out=hT[:, h * 8:(h + 1) * 8, j * 128:(j + 1) * 128],
                        in0=pb_.t[:, 0:1024].rearrange("p (k t) -> p k t", k=8),
                        in1=gcol[:, h * 8:(h + 1) * 8].unsqueeze(2).broadcast_to([128, 8, 128]), op=ALU.mult),
                       R=[pb_.r], W=[hT_r[j]])

        def load_gbc(dram1d):
            dma(SP, gbc.sem, gbc.t[:], dram1d.ap().partition_broadcast(128), W=[gbc.r])

        def y_evac(py, j, fg, stage):
            ya, yr = yhv(stage, j)
            act(ya[:, fg * 512:(fg + 1) * 512], py.t[:, 0:512], AF.Copy, R=[py.r], W=yr)
            act(junk[:], py.t[:, 0:512], AF.Square, R=[py.r], W=[st_r], accum_out=st[:, 40 + j * 4 + fg:41 + j * 4 + fg])

        def residual_update(j, stage):
            ya, yr = yhv(stage, j)
            op(DVE, lambda: V.reduce_sum(out=st[:, 4:5], in_=st[:, 40 + j * 4:44 + j * 4], axis=AX.X),
               R=[st_r], W=[st_r])
            act(st[:, 5:6], st[:, 4:5], AF.Ln, R=[st_r], W=[st_r], scale=1.0 / D, bias=EPS)
            act(st[:, 5:6], st[:, 5:6], AF.Exp, R=[], W=[st_r], scale=-0.5)
            op(DVE, lambda: V.scalar_tensor_tensor(out=ya, in0=ya, scalar=st[:, 5:6],
                                                   in1=gbc.t[:], op0=ALU.mult, op1=ALU.mult),
               R=[st_r, gbc.r] + yr, W=yr)
            op(DVE, lambda: V.tensor_tensor(out=xs[:, j, :], in0=xs[:, j, :], in1=ya, op=ALU.add),
               R=yr, W=[xs_r[j]])

        def load_x(blk0, nblk, src=None):
            src = x_d if src is None else src
            for j in range(nblk):
                dma(SP, xs_sem[j], xs[:, j, :], src.ap()[(blk0 + j) * 128:(blk0 + j + 1) * 128, :], W=[xs_r[j]])

        def stageA(nblk, phase, pre_done=False, hook=None):
            Tt = 128 * nblk
            hT = cur["hT"]
            hT_r = cur["hT_r"]
            if not pre_done:
                rmsnorm_T([(xs[:, j, :], xs_r[j]) for j in range(nblk)], gpm, nblk)
            hR = hT_r[0:nblk]
            full = (phase == 2)
            dump(0, hT[:, 0, 0:Tt], hR, Tt)
            dump(1, hT[:, 15, 0:Tt], hR, Tt)
            if full:
                wa = load_w_std(w_in_d, C_VS)
                wb = load_w_std(w_in_d, C_VS + 512)
                vln = big[:, 24 * T:24 * T + NB * 1024].rearrange("p (j e) -> p j e", e=1024)
                vln_r = bigr(24, 8)
                for j in range(nblk):
                    for hf, w_ in enumerate((wa, wb)):
                        ps = FP.next()
                        wv_ = wview(w_, 16, 512)
                        mm_group(ps.t[:, 0:512], [(hT[:, kc, j * 128:(j + 1) * 128], wv_[:, kc, :]) for kc in range(16)],
                                 R=[hT_r[j], w_.r], W=[ps.r])
                        act(gv.t[:, hf * 512:(hf + 1) * 512], ps.t[:, 0:512], AF.Gelu, R=[ps.r], W=gv_R + [st_r],
                            accum_out=st[:, 16 + hf:17 + hf])
                        act(junk[:], gv.t[:, hf * 512:(hf + 1) * 512], AF.Square, R=gv_R, W=[st_r],
                            accum_out=st[:, 18 + hf:19 + hf])
                    op(DVE, lambda: V.tensor_tensor(out=st[:, 20:21], in0=st[:, 16:17], in1=st[:, 17:18], op=ALU.add),
                       R=[st_r], W=[st_r])
                    op(DVE, lambda: V.tensor_tensor(out=st[:, 21:22], in0=st[:, 18:19], in1=st[:, 19:20], op=ALU.add),
                       W=[st_r])
                    op(DVE, lambda: V.tensor_scalar(out=st[:, 20:21], in0=st[:, 20:21], scalar1=1.0 / 1024, scalar2=None,
                                                    op0=ALU.mult), W=[st_r])
                    op(DVE, lambda: V.tensor_tensor(out=st[:, 22:23], in0=st[:, 20:21], in1=st[:, 20:21], op=ALU.mult),
                       W=[st_r])
                    op(DVE, lambda: V.scalar_tensor_tensor(out=st[:, 22:23], in0=st[:, 21:22], scalar=1.0 / 1024,
                                                           in1=st[:, 22:23], op0=ALU.mult, op1=ALU.subtract), W=[st_r])
                    act(st[:, 23:24], st[:, 22:23], AF.Ln, R=[st_r], W=[st_r], bias=EPS)
                    act(st[:, 23:24], st[:, 23:24], AF.Exp, R=[], W=[st_r], scale=-0.5)
                    op(DVE, lambda j=j: V.tensor_scalar(out=vln[:, j, :], in0=gv.t[:, 0:1024], scalar1=st[:, 20:21],
                                                        scalar2=st[:, 23:24], op0=ALU.subtract, op1=ALU.mult),
                       R=gv_R + [st_r], W=vln_r)
                for t2 in range(2):
                    wu = load_w_std(w_in_d, C_U + t2 * 512)
                    wuv = wview(wu, 16, 512)
                    for gl in range(4):
                        g = t2 * 4 + gl
                        pg = FP.next()
                        pe_multi([lambda j=j, pg=pg, g=g: nc.tensor.matmul(
                            pg.t[:, j * 128:(j + 1) * 128], vln[:, j, g * 128:(g + 1) * 128], WmT[:, g, :],
                            start=True, stop=True) for j in range(nblk)], R=vln_r, W=[pg.r])
                        pu = FP.next()
                        mm_group(pu.t[:, 0:Tt], [(wuv[:, kc, gl * 128:(gl + 1) * 128], hT[:, kc, 0:Tt]) for kc in range(16)],
                                 R=hR + [wu.r], W=[pu.r])
                        ub = P32.next()
                        act(ub.t[:, 0:Tt], pu.t[:, 0:Tt], AF.Gelu, R=[pu.r], W=[ub.r])
                        svb = P32.next()
                        op(DVE, lambda pg=pg, g=g, svb=svb: V.scalar_tensor_tensor(
                            out=svb.t[:, 0:Tt].rearrange("p (j t) -> p j t", t=128),
                            in0=pg.t[:, 0:Tt].rearrange("p (j t) -> p j t", t=128), scalar=lng[:, g:g + 1],
                            in1=Qsg[:, g, :].unsqueeze(1).broadcast_to([128, nblk, 128]), op0=ALU.mult, op1=ALU.add),
                           R=[pg.r], W=[svb.r])
                        op(DVE, lambda g=g, ub=ub, svb=svb: V.tensor_tensor(
                            out=bigc(g)[:, 0:Tt], in0=ub.t[:, 0:Tt], in1=svb.t[:, 0:Tt], op=ALU.mult),
                           R=[ub.r, svb.r], W=bigr(g))
            if full:
                dump(2, bigc(0)[:, 0:Tt], bigr(0), Tt)
                dump(3, bigc(7)[:, 0:Tt], bigr(7), Tt)
                dump(0, Qsg[:, 0, :], [cst_r], 128, f32=True)
                dump(1, cvec[:, :], [cst_r], 128, f32=True)
            pgl = FP.next()
            mm_group(pgl.t[0:16, 0:Tt], [(Wglr[:, kc, :], hT[:, kc, 0:Tt]) for kc in range(16)], R=hR, W=[pgl.r])
            act(glr.t[:, 0:Tt], pgl.t[0:16, 0:Tt], AF.Copy, R=[pgl.r], W=[glr.r])
            qp = [bigc(24 + d2) for d2 in range(2)]
            qt = [bigc(26 + d2) for d2 in range(2)]
            kp = [bigc(28 + d2) for d2 in range(2)]
            kz = [bigc(30 + d2) for d2 in range(2)]
            kh = [bigc(32 + d2) for d2 in range(2)]
            khT = big[:, 34 * T:34 * T + NB * 256].rearrange("p (j d) -> p j d", d=256)
            khT_r = bigr(34, 2)
            vt = big[:, 36 * T:36 * T + NB * 512].rearrange("p (j e) -> p j e", e=512)
            vt_r = bigr(36, 4)
            ybt_ring = Ring([Buf(big[:, 40 * T + i * 512:40 * T + (i + 1) * 512]) for i in range(4 * T // 512)])
            for i_, b_ in enumerate(ybt_ring.items):
                b_.r = big_r[40 + (i_ * 512) // T]
            for hd in range(4):
                wqk = load_w([(w_in_d, 0, D, C_K + hd * 256, 256, 512)]) if not full else None
                if full:
                    wqk = wring.next()
                    for ci, c0 in enumerate((C_Q + hd * 256, C_K + hd * 256)):
                        src0, sc_r = wsrc(w_in_d, 0, D, c0, 256)
                        src = src0.rearrange("(kc p) c -> p kc c", p=128)
                        dst = wqk.t[:].rearrange("p (kc c) -> p kc c", c=512)[:, :, ci * 256:(ci + 1) * 256]
                        dma(POOL, wqk.sem, dst, src, R=[sc_r], W=[wqk.r])
                    kcol0 = 256
                else:
                    kcol0 = 0
                wqv = wview(wqk, 16, 512)
                for d2 in range(2):
                    dc = 2 * hd + d2
                    pl = FP.next()
                    mm_group(pl.t[:, 0:Tt], [(W2b[0:16, dc * 128:(dc + 1) * 128], glr.t[0:16, 0:Tt])], R=[glr.r], W=[pl.r])
                    la = P32.next()
                    act(la.t[:, 0:Tt], pl.t[:, 0:Tt], AF.Exp, R=[pl.r], W=[la.r], scale=-1.0, bias=nbg[:, dc:dc + 1])
                    act(la.t[:, 0:Tt], la.t[:, 0:Tt], AF.Ln, R=[], W=[la.r], bias=1.0)
                    op(DVE, lambda la=la: V.tensor_scalar(out=la.t[:, 0:Tt], in0=la.t[:, 0:Tt], scalar1=-1.0 / 16,
                                                          scalar2=-1.0, op0=ALU.mult, op1=ALU.max), R=[la.r], W=[la.r])
                    bc = P32.next()
                    op(DVE, lambda la=la, bc=bc: V.tensor_tensor_scan(
                        out=bc.t[:, 0:Tt], data0=rmask[:, 0:Tt], data1=la.t[:, 0:Tt], initial=0.0,
                        op0=ALU.mult, op1=ALU.add), R=[la.r], W=[bc.r])
                    bc3 = bc.t[:, 0:Tt].rearrange("p (j t) -> p j t", t=128)
                    A2 = P32.next()
                    A23 = A2.t[:, 0:Tt].rearrange("p (j t) -> p j t", t=128)
                    op(DVE, lambda bc3=bc3, A23=A23: V.tensor_tensor(
                        out=A23, in0=bc3[:, :, 127:128].broadcast_to([128, nblk, 128]), in1=bc3, op=ALU.subtract),
                       R=[bc.r], W=[A2.r])
                    if full:
                        A1 = P32.next()
                        A13 = A1.t[:, 0:Tt].rearrange("p (j t) -> p j t", t=128)
                        op(DVE, lambda bc3=bc3, A13=A13: V.tensor_tensor(
                            out=A13, in0=bc3, in1=bc3[:, :, 63:64].broadcast_to([128, nblk, 128]), op=ALU.subtract),
                           R=[bc.r], W=[A1.r])
                        Ek = P32.next()
                        act(Ek.t[:, 0:Tt], A1.t[:, 0:Tt], AF.Exp, R=[A1.r], W=[Ek.r], scale=-1.0)
                        act(A1.t[:, 0:Tt], A1.t[:, 0:Tt], AF.Exp, R=[], W=[A1.r])
                    act(A2.t[:, 0:Tt], A2.t[:, 0:Tt], AF.Exp, R=[A2.r], W=[A2.r])
                    act(dec.t[:, dc, 0:nblk], bc3[:, :, 127], AF.Exp, R=[bc.r], W=[dec.r])
                    if full:
                        act(bc.t[:, 0:Tt], bc.t[:, 0:Tt], AF.Exp, R=[], W=[bc.r])
                        pq_ = FP.next()
                        mm_group(pq_.t[:, 0:Tt], [(wqv[:, kc, d2 * 128:(d2 + 1) * 128], hT[:, kc, 0:Tt]) for kc in range(16)],
                                 R=hR + [wqk.r], W=[pq_.r])
                        op(DVE, lambda pq_=pq_, A1=A1, d2=d2: V.scalar_tensor_tensor(
                            out=qp[d2][:, 0:Tt], in0=pq_.t[:, 0:Tt], scalar=1.0 / 16, in1=A1.t[:, 0:Tt],
                            op0=ALU.mult, op1=ALU.mult), R=[pq_.r, A1.r], W=bigr(24 + d2))
                        op(DVE, lambda pq_=pq_, bc=bc, d2=d2: V.scalar_tensor_tensor(
                            out=qt[d2][:, 0:Tt], in0=pq_.t[:, 0:Tt], scalar=1.0 / 16, in1=bc.t[:, 0:Tt],
                            op0=ALU.mult, op1=ALU.mult), R=[pq_.r, bc.r], W=bigr(26 + d2))
                    pk_ = FP.next()
                    mm_group(pk_.t[:, 0:Tt], [(wqv[:, kc, kcol0 + d2 * 128:kcol0 + (d2 + 1) * 128], hT[:, kc, 0:Tt])
                                              for kc in range(16)], R=hR + [wqk.r], W=[pk_.r])
                    if full:
                        op(DVE, lambda pk_=pk_, Ek=Ek, d2=d2: V.tensor_tensor(
                            out=kp[d2][:, 0:Tt], in0=pk_.t[:, 0:Tt], in1=Ek.t[:, 0:Tt], op=ALU.mult),
                           R=[pk_.r, Ek.r], W=bigr(28 + d2))
                        op(DVE, lambda d2=d2: V.tensor_tensor(
                            out=kz[d2][:, 0:Tt], in0=kp[d2][:, 0:Tt], in1=hmask[:, 0:Tt], op=ALU.mult),
                           R=bigr(28 + d2), W=bigr(30 + d2))
                    op(DVE, lambda pk_=pk_, A2=A2, d2=d2: V.tensor_tensor(
                        out=kh[d2][:, 0:Tt], in0=pk_.t[:, 0:Tt], in1=A2.t[:, 0:Tt], op=ALU.mult),
                       R=[pk_.r, A2.r], W=bigr(32 + d2))
                wv_ = load_w_std(w_in_d, C_V + hd * 512)
                wvv = wview(wv_, 16, 512)
                for j in range(nblk):
                    if j == nblk // 2:
                        if full and hd == 0:
                            dump(4, qp[0][:, 0:Tt], bigr(24), Tt)
                            dump(5, qt[0][:, 0:Tt], bigr(26), Tt)
                            dump(6, kp[0][:, 0:Tt], bigr(28), Tt)
                            dump(7, kh[0][:, 0:Tt], bigr(32), Tt)
                            dump(2, dec.t[:, :, :].rearrange("p a b -> p (a b)"), [dec.r], 8 * NB, f32=True)
                        ph = HP.next()
                        pe_multi([lambda j=j, d2=d2, ph=ph: nc.tensor.transpose(
                            ph.t[:, (j * 2 + d2) * 128:(j * 2 + d2 + 1) * 128], kh[d2][:, j * 128:(j + 1) * 128], ident[:])
                            for j in range(nblk) for d2 in range(2)], R=bigr(32, 2), W=[ph.r])
                        act(khT[:, 0:nblk, :], ph.t[:, 0:nblk * 256].rearrange("p (j d) -> p j d", d=256), AF.Copy,
                            R=[ph.r], W=khT_r)

                    pv = FP.next()
                    mm_group(pv.t[:, 0:512], [(hT[:, kc, j * 128:(j + 1) * 128], wvv[:, kc, :]) for kc in range(16)],
                             R=[hT_r[j], wv_.r], W=[pv.r])
                    act(vt[:, j, :], pv.t[:, 0:512], AF.Copy, R=[pv.r], W=vt_r)
                if full:
                    wog = load_w_std(w_in_d, C_OG + hd * 512)
                    wogv = wview(wog, 16, 512)
                def e_pp(j):
                    jc = slice(j * 128, (j + 1) * 128)
                    pp = FP.next()
                    pe_multi([
                        lambda: nc.tensor.matmul(pp.t[:, 0:64], kz[0][:, jc], qp[0][:, j * 128:j * 128 + 64],
                                                 start=True, stop=False),
                        lambda: nc.tensor.matmul(pp.t[:, 0:64], kz[1][:, jc], qp[1][:, j * 128:j * 128 + 64],
                                                 start=False, stop=True),
                        lambda: nc.tensor.matmul(pp.t[:, 64:128], kp[0][:, jc], qp[0][:, j * 128 + 64:(j + 1) * 128],
                                                 start=True, stop=False),
                        lambda: nc.tensor.matmul(pp.t[:, 64:128], kp[1][:, jc], qp[1][:, j * 128 + 64:(j + 1) * 128],
                                                 start=False, stop=True)],
                        R=bigr(24, 2) + bigr(28, 4), W=[pp.r])
                    at_ = atb.next()
                    op(DVE, lambda: V.tensor_tensor(out=at_.t[:, 0:128], in0=pp.t[:, 0:128],
                                                    in1=tri[:], op=ALU.mult), R=[pp.r], W=[at_.r])
                    return at_

                def e_pog(j):
                    jc = slice(j * 128, (j + 1) * 128)
                    pog = FP.next()
                    mm_group(pog.t[:, 0:512], [(hT[:, kc, jc], wogv[:, kc, :]) for kc in range(16)],
                             R=[hT_r[j], wog.r], W=[pog.r])
                    return pog

                def e_po(j, at_):
                    jc = slice(j * 128, (j + 1) * 128)
                    po = FP.next()
                    mm_group(po.t[:, 0:512], [(at_.t[:, 0:128], vt[:, j, :]),
                                              (qt[0][:, jc], Sb[:, 2 * hd, :]),
                                              (qt[1][:, jc], Sb[:, 2 * hd + 1, :])],
                             R=[at_.r] + vt_r + bigr(26, 2) + Sb_r[2 * hd:2 * hd + 2], W=[po.r])
                    return po

                def e_state(j):
                    for d2 in range(2):
                        dc = 2 * hd + d2
                        psu = FP.next()
                        mm_group(psu.t[:, 0:512], [(khT[:, j, d2 * 128:(d2 + 1) * 128], vt[:, j, :])],
                                 R=khT_r + vt_r, W=[psu.r])
                        op(DVE, lambda: V.scalar_tensor_tensor(
                            out=S[:, dc, :], in0=S[:, dc, :], scalar=dec.t[:, dc, j:j + 1], in1=psu.t[:, 0:512],
                            op0=ALU.mult, op1=ALU.add), R=[psu.r, dec.r], W=[S_r[dc]])
                        if full:
                            act(Sb[:, dc, :], S[:, dc, :], AF.Copy, R=[S_r[dc]], W=[Sb_r[dc]])

                def e_gate(j, po, pog):
                    act(junk[:], po.t[:, 0:512], AF.Square, R=[po.r], W=[st_r], accum_out=st[:, 24:25])
                    act(st[:, 25:26], st[:, 24:25], AF.Ln, R=[], W=[st_r], scale=1.0 / 512, bias=EPS)
                    act(st[:, 25:26], st[:, 25:26], AF.Exp, R=[], W=[st_r], scale=-0.5)
                    sgo = P32W.next()
                    act(sgo.t[:, 0:512], pog.t[:, 0:512], AF.Silu, R=[pog.r], W=[sgo.r])
                    yb_ = ybt_ring.next()
                    op(DVE, lambda: V.scalar_tensor_tensor(
                        out=yb_.t, in0=po.t[:, 0:512], scalar=st[:, 25:26], in1=sgo.t[:, 0:512],
                        op0=ALU.mult, op1=ALU.mult), R=[po.r, sgo.r, st_r], W=[yb_.r])
                    return yb_

                def e_ytrans(j, yb_):
                    jc = slice(j * 128, (j + 1) * 128)
                    ph2 = HP.next()
                    pe_multi([lambda e=e: nc.tensor.transpose(
                        ph2.t[:, e * 128:(e + 1) * 128], yb_.t[:, e * 128:(e + 1) * 128], ident[:])
                        for e in range(4)], R=[yb_.r], W=[ph2.r])
                    ybd = big[:, (8 + hd * 4) * T:(12 + hd * 4) * T].rearrange("p (e t) -> p e t", t=T)[:, :, jc]
                    op(DVE, lambda: V.tensor_tensor(
                        out=ybd, in0=ph2.t[:, 0:512].rearrange("p (e t) -> p e t", t=128),
                        in1=gn[:, 0:4].unsqueeze(2).broadcast_to([128, 4, 128]), op=ALU.mult),
                       R=[ph2.r], W=bigr(8 + hd * 4, 4))

                if not full:
                    for j in range(nblk):
                        e_state(j)
                    if hook is not None:
                        hook(hd)
                else:
                    at_n = e_pp(0)
                    pog_n = e_pog(0)
                    for j in range(nblk):
                        at_c, pog_c = at_n, pog_n
                        po = e_po(j, at_c)
                        yb_ = e_gate(j, po, pog_c)
                        e_state(j)
                        if j + 1 < nblk:
                            at_n = e_pp(j + 1)
                            pog_n = e_pog(j + 1)
                        e_ytrans(j, yb_)
            if not full:
                for j in range(nblk):
                    op(DVE, lambda j=j: V.tensor_tensor(out=Dtot.t[:], in0=Dtot.t[:], in1=dec.t[:, :, j], op=ALU.mult),
                       R=[dec.r], W=[Dtot.r])
                return
            dump(8, bigc(8)[:, 0:Tt], bigr(8), Tt)
            dump(9, bigc(23)[:, 0:Tt], bigr(23), Tt)
            dump(3, S[:, 0, :], [S_r[0]], 512, f32=True)
            for fg in range(4):
                sa, sbb = [], []
                for (c0, lst) in ((C_MA, sa), (C_MB, sbb)):
                    wm = load_w_std(w_in_d, c0 + fg * 512)
                    wmv = wview(wm, 16, 512)
                    for i in range(4):
                        pm = FP.next()
                        mm_group(pm.t[:, 0:Tt], [(wmv[:, kc, i * 128:(i + 1) * 128], hT[:, kc, 0:Tt]) for kc in range(16)],
                                 R=hR + [wm.r], W=[pm.r])
                        sg_ = P32.next()
                        act(sg_.t[:, 0:Tt], pm.t[:, 0:Tt], AF.Sigmoid, R=[pm.r], W=[sg_.r])
                        lst.append(sg_)
                wa_ = load_w_std(wpa_d, fg * 512, nrows=1024)
                wav = wview(wa_, 8, 512)
                for i in range(4):
                    pa_ = FP.next()
                    mm_group(pa_.t[:, 0:Tt], [(wav[:, kc, i * 128:(i + 1) * 128], bigc(kc)[:, 0:Tt]) for kc in range(8)],
                             R=bigr(0, 8) + [wa_.r], W=[pa_.r])
                    op(DVE, lambda pa_=pa_, s_=sa[i]: V.tensor_tensor(out=s_.t[:, 0:Tt], in0=pa_.t[:, 0:Tt],
                                                                     in1=s_.t[:, 0:Tt], op=ALU.mult),
                       R=[pa_.r, sa[i].r], W=[sa[i].r])
                wb_ = load_w_std(wpb_d, fg * 512)
                wbv = wview(wb_, 16, 512)
                for i in range(4):
                    pb_ = FP.next()
                    mm_group(pb_.t[:, 0:Tt], [(wbv[:, kc, i * 128:(i + 1) * 128], bigc(8 + kc)[:, 0:Tt]) for kc in range(16)],
                             R=bigr(8, 16) + [wb_.r], W=[pb_.r])
                    op(DVE, lambda pb_=pb_, s_=sbb[i]: V.tensor_tensor(out=s_.t[:, 0:Tt], in0=pb_.t[:, 0:Tt],
                                                                      in1=s_.t[:, 0:Tt], op=ALU.mult),
                       R=[pb_.r, sbb[i].r], W=[sbb[i].r])
                    op(DVE, lambda i=i, fg=fg, a_=sa[i], b_=sbb[i]: V.tensor_tensor(
                        out=bigc(24 + fg * 4 + i)[:, 0:Tt], in0=a_.t[:, 0:Tt], in1=b_.t[:, 0:Tt], op=ALU.add),
                       R=[sa[i].r, sbb[i].r], W=bigr(24 + fg * 4 + i))
            dump(10, bigc(24)[:, 0:Tt], bigr(24), Tt)
            dump(11, bigc(39)[:, 0:Tt], bigr(39), Tt)
            load_gbc(post_mix_d)
            for fg in range(4):
                wo4 = load_w_std(wout_d, fg * 512)
                wov = wview(wo4, 16, 512)
                for j in range(nblk):
                    py = FP.next()
                    mm_group(py.t[:, 0:512], [(bigc(24 + kc)[:, j * 128:(j + 1) * 128], wov[:, kc, :]) for kc in range(16)],
                             R=bigr(24, 16) + [wo4.r], W=[py.r])
                    y_evac(py, j, fg, "A")
            for j in range(nblk):
                residual_update(j, "A")
            dump(5, xs[:, 0, 0:512], [xs_r[0]], 512, f32=True)

        def stageB(nblk):
            Tt = 128 * nblk
            rmsnorm_T([(xs[:, j, :], xs_r[j]) for j in range(nblk)], gpx, nblk)
            hR = hT_r[0:nblk]
            wq_ = load_w_std(wq_d, 0)
            wqv = wview(wq_, 16, 512)
            xq = [bigc(hd) for hd in range(4)]
            xo = [bigc(4 + hd) for hd in range(4)]
            for hd in range(4):
                pq_ = FP.next()
                mm_group(pq_.t[:, 0:Tt], [(wqv[:, kc, hd * 128:(hd + 1) * 128], hT[:, kc, 0:Tt]) for kc in range(16)],
                         R=hR + [wq_.r], W=[pq_.r])
                act(xq[hd][:, 0:Tt], pq_.t[:, 0:Tt], AF.Copy, R=[pq_.r], W=bigr(hd))
            wo_ = wring.next()
            dma(POOL, wo_.sem, wo_.t[:, 0:4 * D].rearrange("p (k c) -> p k c", c=D),
                SCR[id(wo_d)][0].ap().rearrange("(k p) c -> p k c", p=128), R=[SCR[id(wo_d)][1]], W=[wo_.r])
            wov = wo_.t[:, 0:4 * D].rearrange("p (k c) -> p k c", c=D)
            load_gbc(post_xa_d)
            sc = 128.0 ** -0.5
            for j in range(nblk):
                jc = slice(j * 128, (j + 1) * 128)
                for hd in range(4):
                    ps_ = FP.next()
                    mm_group(ps_.t[:, 0:256], [(xq[hd][:, jc], xk[:, hd, :])], R=bigr(hd), W=[ps_.r])
                    op(DVE, lambda ps_=ps_: V.reduce_max(out=st[:, 32:33], in_=ps_.t[:, 0:256], axis=AX.X),
                       R=[ps_.r], W=[st_r])
                    op(DVE, lambda: V.tensor_scalar(out=st[:, 33:34], in0=st[:, 32:33], scalar1=-sc, scalar2=None,
                                                    op0=ALU.mult), W=[st_r])
                    pe_ = P32.next()
                    act(pe_.t[:, 0:256], ps_.t[:, 0:256], AF.Exp, R=[ps_.r, st_r], W=[pe_.r, st_r], scale=sc,
                        bias=st[:, 33:34], accum_out=st[:, 34:35])
                    op(DVE, lambda: V.reciprocal(out=st[:, 35:36], in_=st[:, 34:35]), R=[st_r], W=[st_r])
                    pn = atb.next()
                    op(DVE, lambda pe_=pe_, pn=pn: V.tensor_scalar(out=pn.t[:, 0:256], in0=pe_.t[:, 0:256],
                                                                   scalar1=st[:, 35:36], scalar2=None, op0=ALU.mult),
                       R=[pe_.r, st_r], W=[pn.r])
                    ph = HP.next()
                    pe_multi([lambda mb=mb, ph=ph, pn=pn: nc.tensor.transpose(
                        ph.t[:, mb * 128:(mb + 1) * 128], pn.t[:, mb * 128:(mb + 1) * 128], ident[:])
                        for mb in range(2)], R=[pn.r], W=[ph.r])
                    pt = ptb.next()
                    act(pt.t[:, 0:256], ph.t[:, 0:256], AF.Copy, R=[ph.r], W=[pt.r])
                    po = FP.next()
                    mm_group(po.t[:, 0:128], [(xv[:, mb, hd * 128:(hd + 1) * 128], pt.t[:, mb * 128:(mb + 1) * 128])
                                              for mb in range(2)], R=[pt.r], W=[po.r])
                    act(xo[hd][:, jc], po.t[:, 0:128], AF.Copy, R=[po.r], W=bigr(4 + hd))
                for fg in range(4):
                    py = FP.next()
                    mm_group(py.t[:, 0:512], [(xo[hd][:, jc], wov[:, hd, fg * 512:(fg + 1) * 512]) for hd in range(4)],
                             R=bigr(4, 4) + [wo_.r], W=[py.r])
                    y_evac(py, j, fg, "B")
                residual_update(j, "B")

        def stageC(nblk, first, out_blk0):
            Tt = 128 * nblk
            rmsnorm_T([(xs[:, j, :], xs_r[j]) for j in range(nblk)], gpf, nblk)
            hR = hT_r[0:nblk]

            ringA = Ring(P32.items[0:4])
            ringB = Ring(P32.items[4:8])

            def conv_chunk(w_, wv_, i, c, gate):
                ph = FP.next()
                mm_group(ph.t[:, 0:Tt], [(wv_[:, kc, i * 128:(i + 1) * 128], hT[:, kc, 0:Tt]) for kc in range(16)],
                         R=hR + [w_.r], W=[ph.r])
                hb = ringA.next()
                if first:
                    ph2 = FP.next()
                    mm_group(ph2.t[:, 0:2], [(wv_[:, kc, i * 128:(i + 1) * 128], hprev.t[:, kc, :]) for kc in range(16)],
                             R=[hprev.r, w_.r], W=[ph2.r])
                    act(hb.t[:, 0:2], ph2.t[:, 0:2], AF.Copy, R=[ph2.r], W=[hb.r], scale=cm[:, 0:1])
                else:
                    act(hb.t[:, 0:2], halo[:, c, :], AF.Copy, R=[halo_r], W=[hb.r])
                act(hb.t[:, 2:Tt + 2], ph.t[:, 0:Tt], AF.Copy, R=[ph.r], W=[hb.r])
                act(halo[:, c, :], hb.t[:, Tt:Tt + 2], AF.Copy, R=[hb.r], W=[halo_r])
                acc = (ringB if gate else ringA).next()
                act(acc.t[:, 0:Tt], ph.t[:, 0:Tt], AF.Identity, R=[ph.r], W=[acc.r], scale=cw[:, 2, c:c + 1],
                    bias=cb[:, c:c + 1])
                op(DVE, lambda: V.scalar_tensor_tensor(out=acc.t[:, 0:Tt], in0=hb.t[:, 1:Tt + 1], scalar=cw[:, 1, c:c + 1],
                                                       in1=acc.t[:, 0:Tt], op0=ALU.mult, op1=ALU.add),
                   R=[hb.r, acc.r], W=[acc.r])
                op(DVE, lambda: V.scalar_tensor_tensor(out=acc.t[:, 0:Tt], in0=hb.t[:, 0:Tt], scalar=cw[:, 0, c:c + 1],
                                                       in1=acc.t[:, 0:Tt], op0=ALU.mult, op1=ALU.add),
                   R=[hb.r, acc.r], W=[acc.r])
                if gate:
                    act(acc.t[:, 0:Tt], acc.t[:, 0:Tt], AF.Gelu_apprx_tanh, R=[acc.r], W=[acc.r])
                return acc

            for gt in range(11):
                wg = load_w_std(wup_d, gt * 512)
                wgv = wview(wg, 16, 512)
                gas = [conv_chunk(wg, wgv, i, gt * 4 + i, True) for i in range(4)]
                wu = load_w_std(wup_d, DFF + gt * 512)
                wuv = wview(wu, 16, 512)
                for i in range(4):
                    c = gt * 4 + i
                    ga = gas[i]
                    au = conv_chunk(wu, wuv, i, c + 44, False)
                    op(DVE, lambda ga=ga, au=au, c=c: V.tensor_tensor(out=bigc(c)[:, 0:Tt], in0=ga.t[:, 0:Tt],
                                                                     in1=au.t[:, 0:Tt], op=ALU.mult),
                       R=[ga.r, au.r], W=bigr(c))
            load_gbc(post_ffn_d)
            pieces = [(0, 16), (16, 16), (32, 12)]
            for fg in range(4):
                pys = [FP.next() for _ in range(nblk)]
                for pi, (k0, nk) in enumerate(pieces):
                    wd = load_w_std(wdn_d, fg * 512, nrows=nk * 128, r0=k0 * 128)
                    wdv = wview(wd, nk, 512)
                    for j in range(nblk):
                        py = pys[j]
                        for r in bigr(k0, nk) + [wd.r]:
                            PE.wait(r.w)
                        if pi == 0:
                            PE.wait(py.r.w)
                            for t in py.r.rs.values():
                                PE.wait(t)
                        for kc in range(nk):
                            ins = nc.tensor.matmul(py.t[:, 0:512], bigc(k0 + kc)[:, j * 128:(j + 1) * 128], wdv[:, kc, :],
                                                   start=(pi == 0 and kc == 0), stop=(pi == 2 and kc == nk - 1))
                        PE.sem.n += 1
                        ins.then_inc(PE.sem.h, 1)
                        tok = (PE.sem, PE.sem.n)
                        for r in bigr(k0, nk) + [wd.r]:
                            r.rs["pe"] = tok
                        py.r.w = tok
                        py.r.rs = {}
                for j in range(nblk):
                    y_evac(pys[j], j, fg, "C")
            for j in range(nblk):
                residual_update(j, "C")
                dma(SP, out_sem[j], out_d.ap()[(out_blk0 + j) * 128:(out_blk0 + j + 1) * 128, :], xs[:, j, :],
                    R=[xs_r[j]])

        for mb in range(2):
            dma(SP, xs_sem[0], xs[:, 0, :], mem_d.ap()[mb * 128:(mb + 1) * 128, :], W=[xs_r[0]])
            xa, xr = xs[:, 0, :], xs_r[0]
            act(xn.t[:], xa, AF.Square, R=[xr], W=[xn.r, st_r], accum_out=st[:, 0:1])
            act(st[:, 1:2], st[:, 0:1], AF.Ln, R=[], W=[st_r], scale=1.0 / D, bias=EPS)
            act(st[:, 1:2], st[:, 1:2], AF.Exp, R=[], W=[st_r], scale=-0.5)
            act(xn.t[:], xa, AF.Copy, R=[xr, st_r], W=[xn.r], scale=st[:, 1:2])
            for h in range(2):
                pb_ = HP.next()
                pe_multi([lambda i=i, pb_=pb_, h=h: nc.tensor.transpose(
                    pb_.t[:, i * 128:(i + 1) * 128], xn.t[:, (h * 8 + i) * 128:(h * 8 + i + 1) * 128], ident[:])
                    for i in range(8)], R=[xn.r], W=[pb_.r])
                op(DVE, lambda h=h, pb_=pb_, mb=mb: V.tensor_tensor(
                    out=hT[:, h * 8:(h + 1) * 8, mb * 128:(mb + 1) * 128],
                    in0=pb_.t[:, 0:1024].rearrange("p (k t) -> p k t", k=8),
                    in1=gmem[:, h * 8:(h + 1) * 8].unsqueeze(2).broadcast_to([128, 8, 128]), op=ALU.mult),
                   R=[pb_.r, cst_r], W=[hT_r[mb]])
        wk_ = load_w_std(wk_d, 0)
        wkv = wview(wk_, 16, 512)
        for hd in range(4):
            pk_ = FP.next()
            mm_group(pk_.t[:, 0:256], [(wkv[:, kc, hd * 128:(hd + 1) * 128], hT[:, kc, 0:256]) for kc in range(16)],
                     R=hT_r[0:2] + [wk_.r], W=[pk_.r])
            act(xk[:, hd, :], pk_.t[:, 0:256], AF.Copy, R=[pk_.r], W=[cst_r])
        wv2 = load_w_std(wv_d, 0)
        wvv2 = wview(wv2, 16, 512)
        for mb in range(2):
            pv = FP.next()
            mm_group(pv.t[:, 0:512], [(hT[:, kc, mb * 128:(mb + 1) * 128], wvv2[:, kc, :]) for kc in range(16)],
                     R=hT_r[0:2] + [wv2.r], W=[pv.r])
            act(xv[:, mb, :], pv.t[:, 0:512], AF.Copy, R=[pv.r], W=[cst_r])

        if use_cc:
            for t in range(NT2):
                load_x(t * NB, NB)
                stageA(NB, 1)
            cin = cin_d.ap()
            dma(SP, ccio_sem, cin[0:1024, :].rearrange("(dc d) e -> d dc e", d=128), S[:], R=S_r)
            dma(SP, ccio_sem, cin[1024:1026, :].rearrange("a x -> (a x)").rearrange("(d dc) -> d dc", dc=8),
                Dtot.t[:], R=[Dtot.r])
            POOL.wait((ccio_sem, ccio_sem.n))
            POOL.wait((kv_sem, kv_sem.n))
            for sc_t_, sc_r_ in SCR.values():
                POOL.wait(sc_r_.w)
            for sl_ in wsl:
                POOL.wait((sl_.sem, sl_.sem.n))
            POOL.wait((cst_sem, cst_sem.n))
            ins = nc.gpsimd.collective_compute("AllGather", ALU.bypass, replica_groups=[list(range(NCORES))],
                                               ins=[cin_d.ap().opt()], outs=[cout_d.ap().opt()])
            cc_sem.n += 1
            ins.then_inc(cc_sem.h, 1)
            cc_tok = (cc_sem, cc_sem.n)
            SP.wait(cc_tok)
            DVE.wait((ccio_sem, ccio_sem.n))
            op(DVE, lambda: V.memset(S[:], 0.0), W=S_r)
            cout = cout_d.ap()
            Sl = big[:, 0:8192].bitcast(F32).rearrange("p (dc e) -> p dc e", e=512)
            yh_r = big_r[0:16]
            for r in range(NCORES):
                dma(SP, yl_sem, Sl, cout[r * CROWS:r * CROWS + 1024, :].rearrange("(dc d) e -> d dc e", d=128),
                    W=yh_r)
                dma(SP, Dl.sem, Dl.t[:], cout[r * CROWS + 1024:r * CROWS + 1026, :].rearrange("a x -> (a x)")
                    .rearrange("(d dc) -> d dc", dc=8), W=[Dl.r])
                op(DVE, lambda r=r: V.tensor_scalar(out=De[:], in0=Dl.t[:], scalar1=cm[:, 1 + r:2 + r],
                                                    scalar2=omm[:, 1 + r:2 + r], op0=ALU.mult, op1=ALU.add),
                   R=[Dl.r], W=[De_r])
                for dc in range(8):
                    op(DVE, lambda dc=dc: V.tensor_scalar(out=S[:, dc, :], in0=S[:, dc, :], scalar1=De[:, dc:dc + 1],
                                                          scalar2=None, op0=ALU.mult), R=[De_r], W=[S_r[dc]])
                    op(DVE, lambda dc=dc, r=r: V.scalar_tensor_tensor(
                        out=S[:, dc, :], in0=Sl[:, dc, :], scalar=cm[:, 1 + r:2 + r], in1=S[:, dc, :],
                        op0=ALU.mult, op1=ALU.add), R=yh_r, W=[S_r[dc]])
        if not use_cc:
            NPRE = 3 * SEG // T
            bufs = [(hT, hT_r), (hT2, hT2_r)]
            load_x(0, NB, src=xpre_d)
            rmsnorm_T([(xs[:, j, :], xs_r[j]) for j in range(NB)], gpm, NB, hsel=bufs[0])
            if NPRE > 1:
                load_x(NB, NB, src=xpre_d)
            for t in range(NPRE):
                cur["hT"], cur["hT_r"] = bufs[t % 2]
                nxt = bufs[(t + 1) % 2]

                def hook(hd, t=t, nxt=nxt):
                    issue_conv(1)
                    if t + 1 >= NPRE:
                        return
                    rmsnorm_T([(xs[:, j, :], xs_r[j]) for j in range(NB)], gpm, NB, hsel=nxt, only=hd)
                    dma(SP, xs_sem[hd], xs[:, hd, :],
                        xpre_d.ap()[((t + 2) * NB + hd) * 128:((t + 2) * NB + hd + 1) * 128, :], W=[xs_r[hd]]) \
                        if t + 2 < NPRE else None

                stageA(NB, 1, pre_done=True, hook=hook)
            cur["hT"], cur["hT_r"] = hT, hT_r
            issue_conv(len(conv_jobs))
            barrier()
        for dc in range(8):
            act(Sb[:, dc, :], S[:, dc, :], AF.Copy, R=[S_r[dc]], W=[Sb_r[dc]])

        load_x(0, 1)
        stageA(1, 2)
        if "B" in stages:
            stageB(1)
        if "C" in stages:
            rmsnorm_T([(xs[:, 0, :], xs_r[0])], gpf, 1)
            op(DVE, lambda: V.tensor_copy(out=hprev.t[:], in_=hT[:, :, 126:128]), R=[hT_r[0]], W=[hprev.r])
        for t in range(NT2):
            load_x(1 + t * NB, NB)
            DBG["on"] = (t == 0)
            stageA(NB, 2)
            DBG["on"] = False
            if "B" in stages:
                stageB(NB)
            if "C" in stages:
                stageC(NB, t == 0, t * NB)
            else:
                for j in range(NB):
                    dma(SP, out_sem[j], out_d.ap()[(t * NB + j) * 128:(t * NB + j + 1) * 128, :], xs[:, j, :],
                        R=[xs_r[j]])
        for j in range(NB):
            SP.wait((out_sem[j], out_sem[j].n))
        SP.wait((dbg_sem, dbg_sem.n))
        for E in (PE, ACT, DVE, POOL):
            SP.wait((E.sem, E.sem.n))
    return nc


WEIGHT_KEYS = ["w_in", "pre_norm_mix", "sg_ln_g", "sg_ln_b", "sg_w", "sg_b", "gla_w_gate2", "gla_b_gate",
               "gla_norm_g", "w_proj_a", "w_proj_b", "w_out", "post_norm_mix", "pre_norm_xa", "mem_norm_g",
               "xa_wq", "xa_wk", "xa_wv", "xa_wo", "post_norm_xa", "pre_norm_ffn", "ffn_w_up", "ffn_conv_w",
               "ffn_conv_b", "ffn_w_down", "post_norm_ffn"]

_CACHE = {}


def kernel(_stages="ABC", _use_cc=False, _cores=None, _debug=False, **inputs):
    x = np.asarray(inputs["x"], dtype=np.float32)
    mem = np.asarray(inputs["mem"], dtype=np.float32)
    B, SEQ, _ = x.shape
    assert B == 2
    SEG = SEQ // 4
    key = (SEG, _stages, _use_cc, _debug)
    if key not in _CACHE:
        _CACHE[key] = build(SEG, _stages, _use_cc, _debug)
    nc = _CACHE[key]
    shared = {}
    for k in WEIGHT_KEYS:
        a = np.asarray(inputs[k], dtype=np.float32)
        shared[k] = np.ascontiguousarray(a[0])
    in_maps = []
    for c in range(NCORES):
        b, p = divmod(c, 4)
        xe = np.zeros((SEG + 128, D), np.float32)
        if p > 0:
            xe[:] = x[b, p * SEG - 128:(p + 1) * SEG]
        else:
            xe[128:] = x[b, 0:SEG]
        cmv = np.zeros((128, 16), np.float32)
        cmv[:, 0] = 1.0 if p > 0 else 0.0
        for j in range(NCORES):
            if j // 4 == b and j < c:
                cmv[:, 1 + j] = 1.0
        m = dict(shared)
        m["x"] = xe
        m["mem"] = np.ascontiguousarray(mem[b])
        m["cm"] = cmv
        if not _use_cc:
            xp = np.zeros((3 * SEG, D), np.float32)
            n = p * SEG - 128
            if n > 0:
                xp[3 * SEG - n:] = x[b, 0:n]
            m["xpre"] = xp
        in_maps.append(m)
    if _cores is not None:
        res = run_bass_kernel_spmd(nc, [in_maps[c] for c in _cores], core_ids=list(range(len(_cores))))
        return res.results
    res = run_bass_kernel_spmd(nc, in_maps, core_ids=list(range(NCORES)))
    out = np.empty((B, SEQ, D), np.float32)
    for c in range(NCORES):
        b, p = divmod(c, 4)
        out[b, p * SEG:(p + 1) * SEG] = res.results[c]["out"]
    return out
```

```python
import contextlib
import numpy as np
import concourse.bass as bass
import concourse.mybir as mybir
from concourse.bass_utils import run_bass_kernel_spmd

F32 = mybir.dt.float32
BF16 = mybir.dt.bfloat16
AF = mybir.ActivationFunctionType
ALU = mybir.AluOpType
AX = mybir.AxisListType

D = 2048
N_IN = 12304
DFF = 5632
EPS = 1e-6
NB = 4
NCORES = 8
C_U, C_VS, C_Q, C_K, C_V, C_OG, C_GLR, C_MA, C_MB = 0, 1024, 2048, 3072, 4096, 6144, 8192, 8208, 10256
NW = 2
CROWS = 1026


class SemW:
    def __init__(self, h):
        self.h = h
        self.n = 0


class Eng:
    def __init__(self, raw, semw, name):
        self.raw = raw
        self.sem = semw
        self.name = name
        self.seen = {}

    def wait(self, tok):
        if tok is None:
            return
        sw, v = tok
        if v <= 0 or (sw is self.sem and self.name in ("pe", "sp")):
            return
        if self.seen.get(id(sw), 0) >= v:
            return
        self.raw.wait_ge(sw.h, v)
        self.seen[id(sw)] = v


class Reg:
    __slots__ = ("w", "rs")

    def __init__(self):
        self.w = None
        self.rs = {}


def op(E, fn, R=(), W=(), sig=True):
    for r in R:
        E.wait(r.w)
    for w in W:
        E.wait(w.w)
        for t in w.rs.values():
            E.wait(t)
    ins = fn()
    if sig or E.name != "pe":
        E.sem.n += 1
        ins.then_inc(E.sem.h, 1)
        tok = (E.sem, E.sem.n)
    else:
        tok = (E.sem, E.sem.n + 1)
    for r in R:
        r.rs[E.name] = tok
    for w in W:
        w.w = tok
        w.rs = {}
    return tok


def dma(Q, semw, out, in_, R=(), W=(), **kw):
    for r in R:
        Q.wait(r.w)
    for w in W:
        Q.wait(w.w)
        for t in w.rs.values():
            Q.wait(t)
    ins = Q.raw.dma_start(out=out, in_=in_, **kw)
    semw.n += 16
    ins.then_inc(semw.h, 16)
    tok = (semw, semw.n)
    for r in R:
        r.rs["dma%d" % id(semw)] = tok
    for w in W:
        w.w = tok
        w.rs = {}
    return tok


class Ring:
    def __init__(self, items):
        self.items = items
        self.i = 0

    def next(self):
        it = self.items[self.i]
        self.i = (self.i + 1) % len(self.items)
        return it


class Buf:
    def __init__(self, t):
        self.t = t
        self.r = Reg()


def build(SEG, stages="ABC", use_cc=True, debug=False):
    nc = bass.Bass("TRN2", target_bir_lowering=False)
    NBLK = SEG // 128 + 1
    T = 128 * NB
    assert (NBLK - 1) % NB == 0
    NT2 = (NBLK - 1) // NB

    def din(name, shape):
        return nc.dram_tensor(name, shape, F32, kind="ExternalInput")

    x_d = din("x", [NBLK * 128, D])
    xpre_d = None if use_cc else din("xpre", [3 * SEG, D])
    mem_d = din("mem", [256, D])
    cm_d = din("cm", [128, 16])
    w_in_d = din("w_in", [D, N_IN])
    pre_mix_d = din("pre_norm_mix", [D])
    sg_ln_g_d = din("sg_ln_g", [1024])
    sg_ln_b_d = din("sg_ln_b", [1024])
    sg_w_d = din("sg_w", [8, 128, 128])
    sg_b_d = din("sg_b", [8, 128])
    w2_d = din("gla_w_gate2", [16, 1024])
    bg_d = din("gla_b_gate", [1024])
    gn_d = din("gla_norm_g", [512])
    wpa_d = din("w_proj_a", [1024, D])
    wpb_d = din("w_proj_b", [D, D])
    wout_d = din("w_out", [D, D])
    post_mix_d = din("post_norm_mix", [D])
    pre_xa_d = din("pre_norm_xa", [D])
    memg_d = din("mem_norm_g", [D])
    wq_d = din("xa_wq", [D, 512])
    wk_d = din("xa_wk", [D, 512])
    wv_d = din("xa_wv", [D, 512])
    wo_d = din("xa_wo", [512, D])
    post_xa_d = din("post_norm_xa", [D])
    pre_ffn_d = din("pre_norm_ffn", [D])
    wup_d = din("ffn_w_up", [D, 2 * DFF])
    cw_d = din("ffn_conv_w", [3, 2 * DFF])
    cb_d = din("ffn_conv_b", [2 * DFF])
    wdn_d = din("ffn_w_down", [DFF, D])
    post_ffn_d = din("post_norm_ffn", [D])
    out_d = nc.dram_tensor("out", [SEG, D], F32, kind="ExternalOutput")
    dbgb_d = nc.dram_tensor("dbgb", [12, 128, 512], BF16, kind="ExternalOutput") if debug else None
    dbgf_d = nc.dram_tensor("dbgf", [12, 128, 512], F32, kind="ExternalOutput") if debug else None
    DBG = {"on": False}
    cin_d = nc.dram_tensor("cc_in", [CROWS, 512], F32)
    cout_d = nc.dram_tensor("cc_out", [NCORES * CROWS, 512], F32)

    es = contextlib.ExitStack()
    with es:
        def sb(name, shape, dt):
            return es.enter_context(nc.sbuf_tensor(name, shape, dt))

        def newsem(name):
            return SemW(es.enter_context(nc.semaphore(name)))

        PE = Eng(nc.tensor, newsem("s_pe"), "pe")
        ACT = Eng(nc.scalar, newsem("s_act"), "act")
        DVE = Eng(nc.vector, newsem("s_dve"), "dve")
        POOL = Eng(nc.gpsimd, newsem("s_pool"), "pool")
        SP = Eng(nc.sync, newsem("s_sp"), "sp")

        xs = sb("xs", [128, NB, D], F32)
        xs_r = [Reg() for _ in range(NB)]
        xs_sem = [newsem("xs%d" % j) for j in range(NB)]
        S = sb("S", [128, 8, 512], F32)
        S_r = [Reg() for _ in range(8)]
        Sb = sb("Sb", [128, 8, 512], BF16)
        Sb_r = [Reg() for _ in range(8)]
        hT = sb("hT", [128, 16, T], BF16)
        hT_r = [Reg() for _ in range(NB)]
        big = sb("big", [128, 44 * T], BF16)
        big_r = [Reg() for _ in range(44)]
        wsl = []
        for i in range(NW):
            b = Buf(sb("wsl%d" % i, [128, 16 * 512], BF16))
            b.sem = newsem("w%d" % i)
            wsl.append(b)
        wring = Ring(wsl)
        xn = Buf(sb("xn", [128, D], BF16))
        gbc = Buf(sb("gbc", [128, D], F32))
        gbc.sem = newsem("gbc")
        NP32 = 8
        p32all = sb("p32all", [128, NP32 * (T + 4)], F32)
        P32 = Ring([Buf(p32all[:, i * (T + 4):(i + 1) * (T + 4)]) for i in range(NP32)])
        p32w = sb("p32w", [128, 1024], F32)
        P32W = Ring([Buf(p32w[:, i * 512:(i + 1) * 512]) for i in range(2)])

        class _GV:
            t = p32w
        gv = _GV()
        gv_R = [b_.r for b_ in P32W.items]

        def yhv(stage, j):
            if stage == "A":
                if j < 3:
                    return big[:, 8 * j * T:8 * (j + 1) * T].bitcast(F32), big_r[8 * j:8 * j + 8]
                return hT[:, 0:8, :].rearrange("p k t -> p (k t)").bitcast(F32), list(hT_r)
            if stage == "B":
                return big[:, (8 + 8 * j) * T:(16 + 8 * j) * T].bitcast(F32), big_r[8 + 8 * j:16 + 8 * j]
            if j < 2:
                return hT[:, 8 * j:8 * j + 8, :].rearrange("p k t -> p (k t)").bitcast(F32), list(hT_r)
            lo = (j - 2) * 2048
            regs = [P32.items[i].r for i in range(NP32) if i * (T + 4) < lo + 2048 and (i + 1) * (T + 4) > lo]
            return p32all[:, lo:lo + 2048], regs
        junk = sb("junk", [128, 512], BF16)
        atb = Ring([Buf(sb("atb%d" % i, [128, 256], BF16)) for i in range(2)])
        ptb = Ring([Buf(sb("ptb%d" % i, [128, 256], BF16)) for i in range(2)])
        cvec = sb("cvec", [128, 128], F32)
        gpm = cvec[:, 0:16]
        gpx = cvec[:, 16:32]
        gpf = cvec[:, 32:48]
        gmem = cvec[:, 48:64]
        nbg = cvec[:, 64:72]
        gn = cvec[:, 72:76]
        lng = cvec[:, 76:84]
        rawv = big[:, 0:1024].bitcast(F32).rearrange("p (s c) -> p s c", c=128)
        cw = sb("cw", [128, 3, 88], F32)
        cb = sb("cb", [128, 88], F32)
        cm = sb("cm_sb", [128, 16], F32)
        omm = sb("omm", [128, 16], F32)
        ident = sb("ident", [128, 128], BF16)
        identf = sb("identf", [128, 128], F32)
        tri = sb("tri", [128, 128], F32)
        rmask = sb("rmask", [128, T], F32)
        hmask = sb("hmask", [128, T], BF16)
        WmT = sb("WmT", [128, 8, 128], BF16)
        Qsg = sb("Qsg", [128, 8, 128], F32)
        wsgb = big[:, 1024:2048].rearrange("p (g s) -> p g s", s=128)
        Bb = big[:, 2048:3072]
        bsbc = big[:, 3072:5120].bitcast(F32)
        xk = sb("xk", [128, 4, 256], BF16)
        xv = sb("xv", [128, 2, 512], BF16)
        W2b = sb("W2b", [16, 1024], BF16)
        Wglr = sb("Wglr", [128, 16, 16], BF16)
        glr = Buf(sb("glr", [16, T], BF16))
        halo = sb("halo", [128, 88, 2], F32)
        hprev = Buf(sb("hprev", [128, 16, 2], BF16))
        dec = Buf(sb("dec", [128, 8, NB], F32))
        Dtot = Buf(sb("Dtot", [128, 8], F32))
        Dl = Buf(sb("Dl", [128, 8], F32))
        Dl.sem = newsem("dl")
        De = sb("De", [128, 8], F32)
        st = sb("st", [128, 64], F32)
        st_r = Reg()
        halo_r = Reg()
        De_r = Reg()
        cst_r = Reg()
        cst_sem = newsem("cst")
        out_sem = [newsem("o%d" % j) for j in range(NB)]
        cc_sem = newsem("cc")
        ccio_sem = newsem("ccio")
        yl_sem = newsem("yl")

        print("SBUF bytes remaining per partition:", nc.sbuf_bytes_remaining)
        FP = Ring([Buf(es.enter_context(nc.psum_tensor("pf%d" % i, [128, 512], F32))) for i in range(6)])
        HP = Ring([Buf(es.enter_context(nc.psum_tensor("ph%d" % i, [128, 1024], BF16))) for i in range(2)])

        def bigc(c, n=1):
            return big[:, c * T:(c + n) * T]

        def bigr(c, n=1):
            return big_r[c:c + n]

        def mm_group(out_ap, pairs, R, W):
            n = len(pairs)
            for r in R:
                PE.wait(r.w)
            for w in W:
                PE.wait(w.w)
                for t in w.rs.values():
                    PE.wait(t)
            for i, (l, r_) in enumerate(pairs):
                ins = nc.tensor.matmul(out_ap, l, r_, start=(i == 0), stop=(i == n - 1))
            PE.sem.n += 1
            ins.then_inc(PE.sem.h, 1)
            tok = (PE.sem, PE.sem.n)
            for r in R:
                r.rs["pe"] = tok
            for w in W:
                w.w = tok
                w.rs = {}
            return tok

        def pe_multi(fns, R, W):
            for r in R:
                PE.wait(r.w)
            for w in W:
                PE.wait(w.w)
                for t in w.rs.values():
                    PE.wait(t)
            for f in fns:
                ins = f()
            PE.sem.n += 1
            ins.then_inc(PE.sem.h, 1)
            tok = (PE.sem, PE.sem.n)
            for r in R:
                r.rs["pe"] = tok
            for w in W:
                w.w = tok
                w.rs = {}
            return tok

        def act(out, in_, func, R, W, **kw):
            return op(ACT, lambda: nc.scalar.activation(out=out, in_=in_, func=func, **kw), R, W)

        SCR = {}

        conv_jobs = []

        def mkscratch(name, src_d, rows, cols, rchunk, now=False, front=False):
            t_ = nc.dram_tensor(name + "_bf", [rows, cols], BF16)
            reg = Reg()
            semw = newsem("cv_" + name)
            SCR[id(src_d)] = (t_, reg)
            r0s = list(range(0, rows, rchunk))
            jobs = [(t_, src_d, r0, rchunk, semw, reg, r0 == r0s[-1]) for r0 in r0s]
            if now or front:
                conv_jobs[0:0] = jobs
            else:
                conv_jobs.extend(jobs)
            if now:
                issue_conv(len(r0s))

        def issue_conv(n):
            for _ in range(n):
                if not conv_jobs:
                    return
                t_, src_d, r0, rchunk, semw, reg, last = conv_jobs.pop(0)
                ins = nc.gpsimd.dma_start(out=t_.ap()[r0:r0 + rchunk, :], in_=src_d.ap()[r0:r0 + rchunk, :])
                semw.n += 16
                ins.then_inc(semw.h, 16)
                if last:
                    reg.w = (semw, semw.n)

        def wsrc(dt_, r0, nr, c0, ncol):
            if dt_ is w_in_d and c0 >= C_K and c0 + ncol <= C_OG:
                return kv_t.ap()[r0:r0 + nr, c0 - C_K:c0 - C_K + ncol], kv_reg
            sc_t, sc_r = SCR[id(dt_)]
            return sc_t.ap()[r0:r0 + nr, c0:c0 + ncol], sc_r
        mkscratch("wq", wq_d, D, 512, D)
        mkscratch("wo", wo_d, 512, D, 512)
        mkscratch("wpa", wpa_d, 1024, D, 512)
        mkscratch("wpb", wpb_d, D, D, 512)
        mkscratch("wout", wout_d, D, D, 512)
        mkscratch("wup", wup_d, D, 2 * DFF, 128)
        mkscratch("wdn", wdn_d, DFF, D, 704)

        def load_w(pieces):
            slot = wring.next()
            for (dt_, r0, nr, c0, ncol, wdt) in pieces:
                nk = nr // 128
                src0, sc_r = wsrc(dt_, r0, nr, c0, ncol)
                src = src0.rearrange("(kc p) c -> p kc c", p=128)
                dst = slot.t[:, 0:nk * wdt].rearrange("p (kc c) -> p kc c", c=wdt)[:, :, 0:ncol]
                dma(POOL, slot.sem, dst, src, R=[sc_r], W=[slot.r])
            return slot

        def wview(slot, nk, wdt):
            return slot.t[:, 0:nk * wdt].rearrange("p (kc c) -> p kc c", c=wdt)

        def load_w_std(dt_, c0, ncol=512, nrows=D, r0=0):
            slot = wring.next()
            nk = nrows // 128
            src0, sc_r = wsrc(dt_, r0, nrows, c0, ncol)
            src = src0.rearrange("(kc p) c -> p kc c", p=128)
            dst = slot.t[:, 0:nk * 512].rearrange("p (kc c) -> p kc c", c=512)[:, :, 0:ncol]
            dma(POOL, slot.sem, dst, src, R=[sc_r], W=[slot.r])
            return slot

        def colvec(dst, dram1d, k):
            dma(SP, cst_sem, dst, dram1d.ap().rearrange("(k p) -> p k", p=128), W=[cst_r],
                allow_slow_non_contiguous=True)

        def rawrows(slot, r0, dram1d, k):
            dma(SP, cst_sem, rawv[r0:r0 + k, slot, :], dram1d.rearrange("(k p) -> k p", p=128), W=[cst_r])

        op(DVE, lambda: nc.vector.memset(rawv, 0.0), W=[cst_r])
        rawrows(0, 0, pre_mix_d.ap(), 16)
        rawrows(0, 16, pre_xa_d.ap(), 16)
        rawrows(0, 32, pre_ffn_d.ap(), 16)
        rawrows(0, 48, memg_d.ap(), 16)
        rawrows(0, 64, bg_d.ap(), 8)
        rawrows(0, 72, gn_d.ap(), 4)
        rawrows(0, 76, sg_ln_g_d.ap(), 8)
        rawrows(1, 0, cb_d.ap(), 88)
        rawrows(2, 0, cw_d.ap()[0, :], 88)
        rawrows(3, 0, cw_d.ap()[1, :], 88)
        dma(SP, cst_sem, cm[:], cm_d.ap(), W=[cst_r])
        dma(SP, cst_sem, bsbc, sg_b_d.ap().rearrange("g t -> (g t)").partition_broadcast(128), W=[cst_r])
        dma(POOL, cst_sem, W2b[:], w2_d.ap(), W=[cst_r])
        dma(POOL, cst_sem, Wglr[:], w_in_d.ap()[:, C_GLR:C_GLR + 16].rearrange("(kc p) c -> p kc c", p=128),
            W=[cst_r], allow_slow_non_contiguous=True)
        dma(POOL, cst_sem, wsgb, sg_w_d.ap().rearrange("g t s -> t g s"), W=[cst_r])
        dma(POOL, cst_sem, Bb, sg_ln_b_d.ap().partition_broadcast(128), W=[cst_r])
        V = nc.vector
        op(DVE, lambda: V.memset(identf[:], 1.0), W=[cst_r], sig=False)
        op(DVE, lambda: V.memset(tri[:], 1.0), sig=False)
        op(DVE, lambda: V.memset(rmask[:], 1.0), sig=False)
        op(DVE, lambda: V.memset(hmask[:], 0.0), sig=False)
        op(DVE, lambda: V.memset(halo[:], 0.0), sig=False)
        op(DVE, lambda: V.memset(S[:], 0.0), W=S_r, sig=False)
        op(DVE, lambda: V.memset(Dtot.t[:], 1.0), W=[Dtot.r], sig=False)
        for j in range(NB):
            op(DVE, lambda j=j: V.memset(rmask[:, j * 128:j * 128 + 1], 0.0), sig=False)
            op(DVE, lambda j=j: V.memset(hmask[:, j * 128:j * 128 + 64], 1.0), sig=False)
        tok_c = op(DVE, lambda: V.memset(st[:], 0.0), W=[st_r])
        POOL.wait(tok_c)
        G = nc.gpsimd
        op(POOL, lambda: G.affine_select(out=identf[:], in_=identf[:], pattern=[[-1, 128]],
                                         compare_op=ALU.is_equal, fill=0.0, base=0, channel_multiplier=1),
           W=[cst_r], sig=False)
        op(POOL, lambda: G.affine_select(out=tri[:], in_=tri[:], pattern=[[1, 128]],
                                         compare_op=ALU.is_ge, fill=0.0, base=0, channel_multiplier=-1),
           W=[cst_r], sig=False)
        op(POOL, lambda: G.tensor_copy(out=ident[:], in_=identf[:]), W=[cst_r])
        for sl, dst in ((0, cvec[:, :]), (1, None), (2, None), (3, None)):
            pr = FP.next()
            pe_multi([lambda sl=sl, pr=pr: nc.tensor.transpose(pr.t[:, 0:128], rawv[:, sl, :], identf[:])],
                     R=[cst_r], W=[pr.r])
            if sl == 0:
                op(DVE, lambda pr=pr: V.tensor_copy(out=cvec[:, :], in_=pr.t[:, 0:128]), R=[pr.r], W=[cst_r])
            elif sl == 1:
                op(DVE, lambda pr=pr: V.tensor_copy(out=cb[:, :], in_=pr.t[:, 0:88]), R=[pr.r], W=[cst_r])
            else:
                op(DVE, lambda pr=pr, sl=sl: V.tensor_copy(out=cw[:, sl - 2, :], in_=pr.t[:, 0:88]), R=[pr.r], W=[cst_r])
        dma(SP, cst_sem, rawv[0:88, 0, :], cw_d.ap()[2, :].rearrange("(k p) -> k p", p=128), W=[cst_r])
        pr = FP.next()
        pe_multi([lambda: nc.tensor.transpose(pr.t[:, 0:128], rawv[:, 0, :], identf[:])], R=[cst_r], W=[pr.r])
        op(DVE, lambda: V.tensor_copy(out=cw[:, 2, :], in_=pr.t[:, 0:88]), R=[pr.r], W=[cst_r])
        op(DVE, lambda: V.tensor_scalar(out=nbg, in0=nbg, scalar1=-1.0, scalar2=None, op0=ALU.mult),
           R=[cst_r], W=[cst_r])
        op(DVE, lambda: V.tensor_scalar(out=omm[:], in0=cm[:], scalar1=-1.0, scalar2=1.0, op0=ALU.mult,
                                        op1=ALU.add), R=[cst_r])
        pb = HP.next()
        pe_multi([lambda g=g: nc.tensor.transpose(pb.t[:, g * 128:(g + 1) * 128], wsgb[:, g, :], ident[:])
                  for g in range(8)], R=[cst_r], W=[pb.r])
        op(DVE, lambda: V.tensor_tensor(out=WmT[:], in0=pb.t[:, 0:1024].rearrange("p (g t) -> p g t", g=8),
                                        in1=tri[:].unsqueeze(1).broadcast_to([128, 8, 128]), op=ALU.mult),
           R=[pb.r], W=[cst_r])
        for h2 in range(2):
            pq = FP.next()
            pe_multi([lambda g=g: nc.tensor.matmul(pq.t[:, (g % 4) * 128:(g % 4 + 1) * 128],
                                                   Bb[:, g * 128:(g + 1) * 128], WmT[:, g, :],
                                                   start=True, stop=True)
                      for g in range(h2 * 4, h2 * 4 + 4)], R=[cst_r], W=[pq.r])
            op(DVE, lambda h2=h2, pq=pq: V.tensor_tensor(
                out=Qsg[:, h2 * 4:(h2 + 1) * 4, :], in0=pq.t[:, 0:512].rearrange("p (g t) -> p g t", g=4),
                in1=bsbc[:, h2 * 512:(h2 + 1) * 512].rearrange("p (g t) -> p g t", g=4), op=ALU.add),
               R=[pq.r], W=[cst_r])

        kv_t = nc.dram_tensor("w_in_kv_bf", [D, 3072], BF16)
        kv_reg = Reg()
        kv_sem = newsem("cv_kv")
        for r0 in range(0, D, 512):
            ins = nc.gpsimd.dma_start(out=kv_t.ap()[r0:r0 + 512, :], in_=w_in_d.ap()[r0:r0 + 512, C_K:C_OG])
            kv_sem.n += 16
            ins.then_inc(kv_sem.h, 16)
        kv_reg.w = (kv_sem, kv_sem.n)
        mkscratch("wk", wk_d, D, 512, D, now=True)
        mkscratch("wv", wv_d, D, 512, D, now=True)
        mkscratch("w_in", w_in_d, D, N_IN, 128, front=True)

        dbg_sem = newsem("dbg")

        def dump(idx, ap, regions, w, f32=False):
            if not (debug and DBG["on"]):
                return
            dst = (dbgf_d if f32 else dbgb_d).ap()[idx, :, 0:w]
            dma(SP, dbg_sem, dst, ap, R=regions)

        def barrier():
            toks = [(E.sem, E.sem.n) for E in (PE, ACT, DVE, POOL)] + [(cst_sem, cst_sem.n)]
            for E in (PE, ACT, DVE, POOL, SP):
                for t_ in toks:
                    E.wait(t_)

        barrier()

        cur = {"hT": hT, "hT_r": hT_r}
        hT2 = big[:, 0:16 * T].rearrange("p (k t) -> p k t", t=T)
        hT2_r = [Reg() for _ in range(NB)]

        def rmsnorm_T(blocks, gcol, nblk, hsel=None, only=None):
            hT = cur["hT"] if hsel is None else hsel[0]
            hT_r = cur["hT_r"] if hsel is None else hsel[1]
            for j, (xa, xr) in enumerate(blocks):
                if only is not None and j != only:
                    continue
                act(xn.t[:], xa, AF.Square, R=[xr], W=[xn.r, st_r], accum_out=st[:, 0:1])
                act(st[:, 1:2], st[:, 0:1], AF.Ln, R=[], W=[st_r], scale=1.0 / D, bias=EPS)
                act(st[:, 1:2], st[:, 1:2], AF.Exp, R=[], W=[st_r], scale=-0.5)
                act(xn.t[:], xa, AF.Copy, R=[xr, st_r], W=[xn.r], scale=st[:, 1:2])
                for h in range(2):
                    pb_ = HP.next()
                    pe_multi([lambda i=i, pb_=pb_, h=h: nc.tensor.transpose(
                        pb_.t[:, i * 128:(i + 1) * 128], xn.t[:, (h * 8 + i) * 128:(h * 8 + i + 1) * 128], ident[:])
                        for i in range(8)], R=[xn.r], W=[pb_.r])
                    op(DVE, lambda h=h, pb_=pb_, j=j: V.tensor_tensor(
                        out=hT[:, h * 8:(h + 1) * 8, j * 128:(j + 1) * 128],
                        in0=pb_.t[:, 0:1024].rearrange("p (k t) -> p k t", k=8),
                        in1=gcol[:, h * 8:(h + 1) * 8].unsqueeze(2).broadcast_to([128, 8, 128]), op=ALU.mult),
                       R=[pb_.r], W=[hT_r[j]])

        def load_gbc(dram1d):
            dma(SP, gbc.sem, gbc.t[:], dram1d.ap().partition_broadcast(128), W=[gbc.r])

        def y_evac(py, j, fg, stage):
            ya, yr = yhv(stage, j)
            act(ya[:, fg * 512:(fg + 1) * 512], py.t[:, 0:512], AF.Copy, R=[py.r], W=yr)
            act(junk[:], py.t[:, 0:512], AF.Square, R=[py.r], W=[st_r], accum_out=st[:, 40 + j * 4 + fg:41 + j * 4 + fg])

        def residual_update(j, stage):
            ya, yr = yhv(stage, j)
            op(DVE, lambda: V.reduce_sum(out=st[:, 4:5], in_=st[:, 40 + j * 4:44 + j * 4], axis=AX.X),
               R=[st_r], W=[st_r])
            act(st[:, 5:6], st[:, 4:5], AF.Ln, R=[st_r], W=[st_r], scale=1.0 / D, bias=EPS)
            act(st[:, 5:6], st[:, 5:6], AF.Exp, R=[], W=[st_r], scale=-0.5)
            op(DVE, lambda: V.scalar_tensor_tensor(out=ya, in0=ya, scalar=st[:, 5:6],
                                                   in1=gbc.t[:], op0=ALU.mult, op1=ALU.mult),
               R=[st_r, gbc.r] + yr, W=yr)
            op(DVE, lambda: V.tensor_tensor(out=xs[:, j, :], in0=xs[:, j, :], in1=ya, op=ALU.add),
               R=yr, W=[xs_r[j]])

        def load_x(blk0, nblk, src=None):
            src = x_d if src is None else src
            for j in range(nblk):
                dma(SP, xs_sem[j], xs[:, j, :], src.ap()[(blk0 + j) * 128:(blk0 + j + 1) * 128, :], W=[xs_r[j]])

        def stageA(nblk, phase, pre_done=False, hook=None):
            Tt = 128 * nblk
            hT = cur["hT"]
            hT_r = cur["hT_r"]
            if not pre_done:
                rmsnorm_T([(xs[:, j, :], xs_r[j]) for j in range(nblk)], gpm, nblk)
            hR = hT_r[0:nblk]
            full = (phase == 2)
            dump(0, hT[:, 0, 0:Tt], hR, Tt)
            dump(1, hT[:, 15, 0:Tt], hR, Tt)
            if full:
                wa = load_w_std(w_in_d, C_VS)
                wb = load_w_std(w_in_d, C_VS + 512)
                vln = big[:, 24 * T:24 * T + NB * 1024].rearrange("p (j e) -> p j e", e=1024)
                vln_r = bigr(24, 8)
                for j in range(nblk):
                    for hf, w_ in enumerate((wa, wb)):
                        ps = FP.next()
                        wv_ = wview(w_, 16, 512)
                        mm_group(ps.t[:, 0:512], [(hT[:, kc, j * 128:(j + 1) * 128], wv_[:, kc, :]) for kc in range(16)],
                                 R=[hT_r[j], w_.r], W=[ps.r])
                        act(gv.t[:, hf * 512:(hf + 1) * 512], ps.t[:, 0:512], AF.Gelu, R=[ps.r], W=gv_R + [st_r],
                            accum_out=st[:, 16 + hf:17 + hf])
                        act(junk[:], gv.t[:, hf * 512:(hf + 1) * 512], AF.Square, R=gv_R, W=[st_r],
                            accum_out=st[:, 18 + hf:19 + hf])
                    op(DVE, lambda: V.tensor_tensor(out=st[:, 20:21], in0=st[:, 16:17], in1=st[:, 17:18], op=ALU.add),
                       R=[st_r], W=[st_r])
                    op(DVE, lambda: V.tensor_tensor(out=st[:, 21:22], in0=st[:, 18:19], in1=st[:, 19:20], op=ALU.add),
                       W=[st_r])
                    op(DVE, lambda: V.tensor_scalar(out=st[:, 20:21], in0=st[:, 20:21], scalar1=1.0 / 1024, scalar2=None,
                                                    op0=ALU.mult), W=[st_r])
                    op(DVE, lambda: V.tensor_tensor(out=st[:, 22:23], in0=st[:, 20:21], in1=st[:, 20:21], op=ALU.mult),
                       W=[st_r])
                    op(DVE, lambda: V.scalar_tensor_tensor(out=st[:, 22:23], in0=st[:, 21:22], scalar=1.0 / 1024,
                                                           in1=st[:, 22:23], op0=ALU.mult, op1=ALU.subtract), W=[st_r])
                    act(st[:, 23:24], st[:, 22:23], AF.Ln, R=[st_r], W=[st_r], bias=EPS)
                    act(st[:, 23:24], st[:, 23:24], AF.Exp, R=[], W=[st_r], scale=-0.5)
                    op(DVE, lambda j=j: V.tensor_scalar(out=vln[:, j, :], in0=gv.t[:, 0:1024], scalar1=st[:, 20:21],
                                                        scalar2=st[:, 23:24], op0=ALU.subtract, op1=ALU.mult),
                       R=gv_R + [st_r], W=vln_r)
                for t2 in range(2):
                    wu = load_w_std(w_in_d, C_U + t2 * 512)
                    wuv = wview(wu, 16, 512)
                    for gl in range(4):
                        g = t2 * 4 + gl
                        pg = FP.next()
                        pe_multi([lambda j=j, pg=pg, g=g: nc.tensor.matmul(
                            pg.t[:, j * 128:(j + 1) * 128], vln[:, j, g * 128:(g + 1) * 128], WmT[:, g, :],
                            start=True, stop=True) for j in range(nblk)], R=vln_r, W=[pg.r])
                        pu = FP.next()
                        mm_group(pu.t[:, 0:Tt], [(wuv[:, kc, gl * 128:(gl + 1) * 128], hT[:, kc, 0:Tt]) for kc in range(16)],
                                 R=hR + [wu.r], W=[pu.r])
                        ub = P32.next()
                        act(ub.t[:, 0:Tt], pu.t[:, 0:Tt], AF.Gelu, R=[pu.r], W=[ub.r])
                        svb = P32.next()
                        op(DVE, lambda pg=pg, g=g, svb=svb: V.scalar_tensor_tensor(
                            out=svb.t[:, 0:Tt].rearrange("p (j t) -> p j t", t=128),
                            in0=pg.t[:, 0:Tt].rearrange("p (j t) -> p j t", t=128), scalar=lng[:, g:g + 1],
                            in1=Qsg[:, g, :].unsqueeze(1).broadcast_to([128, nblk, 128]), op0=ALU.mult, op1=ALU.add),
                           R=[pg.r], W=[svb.r])
                        op(DVE, lambda g=g, ub=ub, svb=svb: V.tensor_tensor(
                            out=bigc(g)[:, 0:Tt], in0=ub.t[:, 0:Tt], in1=svb.t[:, 0:Tt], op=ALU.mult),
                           R=[ub.r, svb.r], W=bigr(g))
            if full:
                dump(2, bigc(0)[:, 0:Tt], bigr(0), Tt)
                dump(3, bigc(7)[:, 0:Tt], bigr(7), Tt)
                dump(0, Qsg[:, 0, :], [cst_r], 128, f32=True)
                dump(1, cvec[:, :], [cst_r], 128, f32=True)
            pgl = FP.next()
            mm_group(pgl.t[0:16, 0:Tt], [(Wglr[:, kc, :], hT[:, kc, 0:Tt]) for kc in range(16)], R=hR, W=[pgl.r])
            act(glr.t[:, 0:Tt], pgl.t[0:16, 0:Tt], AF.Copy, R=[pgl.r], W=[glr.r])
            qp = [bigc(24 + d2) for d2 in range(2)]
            qt = [bigc(26 + d2) for d2 in range(2)]
            kp = [bigc(28 + d2) for d2 in range(2)]
            kz = [bigc(30 + d2) for d2 in range(2)]
            kh = [bigc(32 + d2) for d2 in range(2)]
            khT = big[:, 34 * T:34 * T + NB * 256].rearrange("p (j d) -> p j d", d=256)
            khT_r = bigr(34, 2)
            vt = big[:, 36 * T:36 * T + NB * 512].rearrange("p (j e) -> p j e", e=512)
            vt_r = bigr(36, 4)
            ybt_ring = Ring([Buf(big[:, 40 * T + i * 512:40 * T + (i + 1) * 512]) for i in range(4 * T // 512)])
            for i_, b_ in enumerate(ybt_ring.items):
                b_.r = big_r[40 + (i_ * 512) // T]
            for hd in range(4):
                wqk = load_w([(w_in_d, 0, D, C_K + hd * 256, 256, 512)]) if not full else None
                if full:
                    wqk = wring.next()
                    for ci, c0 in enumerate((C_Q + hd * 256, C_K + hd * 256)):
                        src0, sc_r = wsrc(w_in_d, 0, D, c0, 256)
                        src = src0.rearrange("(kc p) c -> p kc c", p=128)
                        dst = wqk.t[:].rearrange("p (kc c) -> p kc c", c=512)[:, :, ci * 256:(ci + 1) * 256]
                        dma(POOL, wqk.sem, dst, src, R=[sc_r], W=[wqk.r])
                    kcol0 = 256
                else:
                    kcol0 = 0
                wqv = wview(wqk, 16, 512)
                for d2 in range(2):
                    dc = 2 * hd + d2
                    pl = FP.next()
                    mm_group(pl.t[:, 0:Tt], [(W2b[0:16, dc * 128:(dc + 1) * 128], glr.t[0:16, 0:Tt])], R=[glr.r], W=[pl.r])
                    la = P32.next()
                    act(la.t[:, 0:Tt], pl.t[:, 0:Tt], AF.Exp, R=[pl.r], W=[la.r], scale=-1.0, bias=nbg[:, dc:dc + 1])
                    act(la.t[:, 0:Tt], la.t[:, 0:Tt], AF.Ln, R=[], W=[la.r], bias=1.0)
                    op(DVE, lambda la=la: V.tensor_scalar(out=la.t[:, 0:Tt], in0=la.t[:, 0:Tt], scalar1=-1.0 / 16,
                                                          scalar2=-1.0, op0=ALU.mult, op1=ALU.max), R=[la.r], W=[la.r])
                    bc = P32.next()
                    op(DVE, lambda la=la, bc=bc: V.tensor_tensor_scan(
                        out=bc.t[:, 0:Tt], data0=rmask[:, 0:Tt], data1=la.t[:, 0:Tt], initial=0.0,
                        op0=ALU.mult, op1=ALU.add), R=[la.r], W=[bc.r])
                    bc3 = bc.t[:, 0:Tt].rearrange("p (j t) -> p j t", t=128)
                    A2 = P32.next()
                    A23 = A2.t[:, 0:Tt].rearrange("p (j t) -> p j t", t=128)
                    op(DVE, lambda bc3=bc3, A23=A23: V.tensor_tensor(
                        out=A23, in0=bc3[:, :, 127:128].broadcast_to([128, nblk, 128]), in1=bc3, op=ALU.subtract),
                       R=[bc.r], W=[A2.r])
                    if full:
                        A1 = P32.next()
                        A13 = A1.t[:, 0:Tt].rearrange("p (j t) -> p j t", t=128)
                        op(DVE, lambda bc3=bc3, A13=A13: V.tensor_tensor(
                            out=A13, in0=bc3, in1=bc3[:, :, 63:64].broadcast_to([128, nblk, 128]), op=ALU.subtract),
                           R=[bc.r], W=[A1.r])
                        Ek = P32.next()
                        act(Ek.t[:, 0:Tt], A1.t[:, 0:Tt], AF.Exp, R=[A1.r], W=[Ek.r], scale=-1.0)
                        act(A1.t[:, 0:Tt], A1.t[:, 0:Tt], AF.Exp, R=[], W=[A1.r])
                    act(A2.t[:, 0:Tt], A2.t[:, 0:Tt], AF.Exp, R=[A2.r], W=[A2.r])
                    act(dec.t[:, dc, 0:nblk], bc3[:, :, 127], AF.Exp, R=[bc.r], W=[dec.r])
                    if full:
                        act(bc.t[:, 0:Tt], bc.t[:, 0:Tt], AF.Exp, R=[], W=[bc.r])
                        pq_ = FP.next()
                        mm_group(pq_.t[:, 0:Tt], [(wqv[:, kc, d2 * 128:(d2 + 1) * 128], hT[:, kc, 0:Tt]) for kc in range(16)],
                                 R=hR + [wqk.r], W=[pq_.r])
                        op(DVE, lambda pq_=pq_, A1=A1, d2=d2: V.scalar_tensor_tensor(
                            out=qp[d2][:, 0:Tt], in0=pq_.t[:, 0:Tt], scalar=1.0 / 16, in1=A1.t[:, 0:Tt],
                            op0=ALU.mult, op1=ALU.mult), R=[pq_.r, A1.r], W=bigr(24 + d2))
                        op(DVE, lambda pq_=pq_, bc=bc, d2=d2: V.scalar_tensor_tensor(
                            out=qt[d2][:, 0:Tt], in0=pq_.t[:, 0:Tt], scalar=1.0 / 16, in1=bc.t[:, 0:Tt],
                            op0=ALU.mult, op1=ALU.mult), R=[pq_.r, bc.r], W=bigr(26 + d2))
                    pk_ = FP.next()
                    mm_group(pk_.t[:, 0:Tt], [(wqv[:, kc, kcol0 + d2 * 128:kcol0 + (d2 + 1) * 128], hT[:, kc, 0:Tt])
                                              for kc in range(16)], R=hR + [wqk.r], W=[pk_.r])
                    if full:
                        op(DVE, lambda pk_=pk_, Ek=Ek, d2=d2: V.tensor_tensor(
                            out=kp[d2][:, 0:Tt], in0=pk_.t[:, 0:Tt], in1=Ek.t[:, 0:Tt], op=ALU.mult),
                           R=[pk_.r, Ek.r], W=bigr(28 + d2))
                        op(DVE, lambda d2=d2: V.tensor_tensor(
                            out=kz[d2][:, 0:Tt], in0=kp[d2][:, 0:Tt], in1=hmask[:, 0:Tt], op=ALU.mult),
                           R=bigr(28 + d2), W=bigr(30 + d2))
                    op(DVE, lambda pk_=pk_, A2=A2, d2=d2: V.tensor_tensor(
                        out=kh[d2][:, 0:Tt], in0=pk_.t[:, 0:Tt], in1=A2.t[:, 0:Tt], op=ALU.mult),
                       R=[pk_.r, A2.r], W=bigr(32 + d2))
                wv_ = load_w_std(w_in_d, C_V + hd * 512)
                wvv = wview(wv_, 16, 512)
                for j in range(nblk):
                    if j == nblk // 2:
                        if full and hd == 0:
                            dump(4, qp[0][:, 0:Tt], bigr(24), Tt)
                            dump(5, qt[0][:, 0:Tt], bigr(26), Tt)
                            dump(6, kp[0][:, 0:Tt], bigr(28), Tt)
                            dump(7, kh[0][:, 0:Tt], bigr(32), Tt)
                            dump(2, dec.t[:, :, :].rearrange("p a b -> p (a b)"), [dec.r], 8 * NB, f32=True)
                        ph = HP.next()
                        pe_multi([lambda j=j, d2=d2, ph=ph: nc.tensor.transpose(
                            ph.t[:, (j * 2 + d2) * 128:(j * 2 + d2 + 1) * 128], kh[d2][:, j * 128:(j + 1) * 128], ident[:])
                            for j in range(nblk) for d2 in range(2)], R=bigr(32, 2), W=[ph.r])
                        act(khT[:, 0:nblk, :], ph.t[:, 0:nblk * 256].rearrange("p (j d) -> p j d", d=256), AF.Copy,
                            R=[ph.r], W=khT_r)

                    pv = FP.next()
                    mm_group(pv.t[:, 0:512], [(hT[:, kc, j * 128:(j + 1) * 128], wvv[:, kc, :]) for kc in range(16)],
                             R=[hT_r[j], wv_.r], W=[pv.r])
                    act(vt[:, j, :], pv.t[:, 0:512], AF.Copy, R=[pv.r], W=vt_r)
                if full:
                    wog = load_w_std(w_in_d, C_OG + hd * 512)
                    wogv = wview(wog, 16, 512)
                def e_pp(j):
                    jc = slice(j * 128, (j + 1) * 128)
                    pp = FP.next()
                    pe_multi([
                        lambda: nc.tensor.matmul(pp.t[:, 0:64], kz[0][:, jc], qp[0][:, j * 128:j * 128 + 64],
                                                 start=True, stop=False),
                        lambda: nc.tensor.matmul(pp.t[:, 0:64], kz[1][:, jc], qp[1][:, j * 128:j * 128 + 64],
                                                 start=False, stop=True),
                        lambda: nc.tensor.matmul(pp.t[:, 64:128], kp[0][:, jc], qp[0][:, j * 128 + 64:(j + 1) * 128],
                                                 start=True, stop=False),
                        lambda: nc.tensor.matmul(pp.t[:, 64:128], kp[1][:, jc], qp[1][:, j * 128 + 64:(j + 1) * 128],
                                                 start=False, stop=True)],
                        R=bigr(24, 2) + bigr(28, 4), W=[pp.r])
                    at_ = atb.next()
                    op(DVE, lambda: V.tensor_tensor(out=at_.t[:, 0:128], in0=pp.t[:, 0:128],
                                                    in1=tri[:], op=ALU.mult), R=[pp.r], W=[at_.r])
                    return at_

                def e_pog(j):
                    jc = slice(j * 128, (j + 1) * 128)
                    pog = FP.next()
                    mm_group(pog.t[:, 0:512], [(hT[:, kc, jc], wogv[:, kc, :]) for kc in range(16)],
                             R=[hT_r[j], wog.r], W=[pog.r])
                    return pog

                def e_po(j, at_):
                    jc = slice(j * 128, (j + 1) * 128)
                    po = FP.next()
                    mm_group(po.t[:, 0:512], [(at_.t[:, 0:128], vt[:, j, :]),
                                              (qt[0][:, jc], Sb[:, 2 * hd, :]),
                                              (qt[1][:, jc], Sb[:, 2 * hd + 1, :])],
                             R=[at_.r] + vt_r + bigr(26, 2) + Sb_r[2 * hd:2 * hd + 2], W=[po.r])
                    return po

                def e_state(j):
                    for d2 in range(2):
                        dc = 2 * hd + d2
                        psu = FP.next()
                        mm_group(psu.t[:, 0:512], [(khT[:, j, d2 * 128:(d2 + 1) * 128], vt[:, j, :])],
                                 R=khT_r + vt_r, W=[psu.r])
                        op(DVE, lambda: V.scalar_tensor_tensor(
                            out=S[:, dc, :], in0=S[:, dc, :], scalar=dec.t[:, dc, j:j + 1], in1=psu.t[:, 0:512],
                            op0=ALU.mult, op1=ALU.add), R=[psu.r, dec.r], W=[S_r[dc]])
                        if full:
                            act(Sb[:, dc, :], S[:, dc, :], AF.Copy, R=[S_r[dc]], W=[Sb_r[dc]])

                def e_gate(j, po, pog):
                    act(junk[:], po.t[:, 0:512], AF.Square, R=[po.r], W=[st_r], accum_out=st[:, 24:25])
                    act(st[:, 25:26], st[:, 24:25], AF.Ln, R=[], W=[st_r], scale=1.0 / 512, bias=EPS)
                    act(st[:, 25:26], st[:, 25:26], AF.Exp, R=[], W=[st_r], scale=-0.5)
                    sgo = P32W.next()
                    act(sgo.t[:, 0:512], pog.t[:, 0:512], AF.Silu, R=[pog.r], W=[sgo.r])
                    yb_ = ybt_ring.next()
                    op(DVE, lambda: V.scalar_tensor_tensor(
                        out=yb_.t, in0=po.t[:, 0:512], scalar=st[:, 25:26], in1=sgo.t[:, 0:512],
                        op0=ALU.mult, op1=ALU.mult), R=[po.r, sgo.r, st_r], W=[yb_.r])
                    return yb_

                def e_ytrans(j, yb_):
                    jc = slice(j * 128, (j + 1) * 128)
                    ph2 = HP.next()
                    pe_multi([lambda e=e: nc.tensor.transpose(
                        ph2.t[:, e * 128:(e + 1) * 128], yb_.t[:, e * 128:(e + 1) * 128], ident[:])
                        for e in range(4)], R=[yb_.r], W=[ph2.r])
                    ybd = big[:, (8 + hd * 4) * T:(12 + hd * 4) * T].rearrange("p (e t) -> p e t", t=T)[:, :, jc]
                    op(DVE, lambda: V.tensor_tensor(
                        out=ybd, in0=ph2.t[:, 0:512].rearrange("p (e t) -> p e t", t=128),
                        in1=gn[:, 0:4].unsqueeze(2).broadcast_to([128, 4, 128]), op=ALU.mult),
                       R=[ph2.r], W=bigr(8 + hd * 4, 4))

                if not full:
                    for j in range(nblk):
                        e_state(j)
                    if hook is not None:
                        hook(hd)
                else:
                    at_n = e_pp(0)
                    pog_n = e_pog(0)
                    for j in range(nblk):
                        at_c, pog_c = at_n, pog_n
                        po = e_po(j, at_c)
                        yb_ = e_gate(j, po, pog_c)
                        e_state(j)
                        if j + 1 < nblk:
                            at_n = e_pp(j + 1)
                            pog_n = e_pog(j + 1)
                        e_ytrans(j, yb_)
            if not full:
                for j in range(nblk):
                    op(DVE, lambda j=j: V.tensor_tensor(out=Dtot.t[:], in0=Dtot.t[:], in1=dec.t[:, :, j], op=ALU.mult),
                       R=[dec.r], W=[Dtot.r])
                return
            dump(8, bigc(8)[:, 0:Tt], bigr(8), Tt)
            dump(9, bigc(23)[:, 0:Tt], bigr(23), Tt)
            dump(3, S[:, 0, :], [S_r[0]], 512, f32=True)
            for fg in range(4):
                sa, sbb = [], []
                for (c0, lst) in ((C_MA, sa), (C_MB, sbb)):
                    wm = load_w_std(w_in_d, c0 + fg * 512)
                    wmv = wview(wm, 16, 512)
                    for i in range(4):
                        pm = FP.next()
                        mm_group(pm.t[:, 0:Tt], [(wmv[:, kc, i * 128:(i + 1) * 128], hT[:, kc, 0:Tt]) for kc in range(16)],
                                 R=hR + [wm.r], W=[pm.r])
                        sg_ = P32.next()
                        act(sg_.t[:, 0:Tt], pm.t[:, 0:Tt], AF.Sigmoid, R=[pm.r], W=[sg_.r])
                        lst.append(sg_)
                wa_ = load_w_std(wpa_d, fg * 512, nrows=1024)
                wav = wview(wa_, 8, 512)
                for i in range(4):
                    pa_ = FP.next()
                    mm_group(pa_.t[:, 0:Tt], [(wav[:, kc, i * 128:(i + 1) * 128], bigc(kc)[:, 0:Tt]) for kc in range(8)],
                             R=bigr(0, 8) + [wa_.r], W=[pa_.r])
                    op(DVE, lambda pa_=pa_, s_=sa[i]: V.tensor_tensor(out=s_.t[:, 0:Tt], in0=pa_.t[:, 0:Tt],
                                                                     in1=s_.t[:, 0:Tt], op=ALU.mult),
                       R=[pa_.r, sa[i].r], W=[sa[i].r])
                wb_ = load_w_std(wpb_d, fg * 512)
                wbv = wview(wb_, 16, 512)
                for i in range(4):
                    pb_ = FP.next()
                    mm_group(pb_.t[:, 0:Tt], [(wbv[:, kc, i * 128:(i + 1) * 128], bigc(8 + kc)[:, 0:Tt]) for kc in range(16)],
                             R=bigr(8, 16) + [wb_.r], W=[pb_.r])
                    op(DVE, lambda pb_=pb_, s_=sbb[i]: V.tensor_tensor(out=s_.t[:, 0:Tt], in0=pb_.t[:, 0:Tt],
                                                                      in1=s_.t[:, 0:Tt], op=ALU.mult),
                       R=[pb_.r, sbb[i].r], W=[sbb[i].r])
                    op(DVE, lambda i=i, fg=fg, a_=sa[i], b_=sbb[i]: V.tensor_tensor(
                        out=bigc(24 + fg * 4 + i)[:, 0:Tt], in0=a_.t[:, 0:Tt], in1=b_.t[:, 0:Tt], op=ALU.add),
                       R=[sa[i].r, sbb[i].r], W=bigr(24 + fg * 4 + i))
            dump(10, bigc(24)[:, 0:Tt], bigr(24), Tt)
            dump(11, bigc(39)[:, 0:Tt], bigr(39), Tt)
            load_gbc(post_mix_d)
            for fg in range(4):
                wo4 = load_w_std(wout_d, fg * 512)
                wov = wview(wo4, 16, 512)
                for j in range(nblk):
                    py = FP.next()
                    mm_group(py.t[:, 0:512], [(bigc(24 + kc)[:, j * 128:(j + 1) * 128], wov[:, kc, :]) for kc in range(16)],
                             R=bigr(24, 16) + [wo4.r], W=[py.r])
                    y_evac(py, j, fg, "A")
            for j in range(nblk):
                residual_update(j, "A")
            dump(5, xs[:, 0, 0:512], [xs_r[0]], 512, f32=True)

        def stageB(nblk):
            Tt = 128 * nblk
            rmsnorm_T([(xs[:, j, :], xs_r[j]) for j in range(nblk)], gpx, nblk)
            hR = hT_r[0:nblk]
            wq_ = load_w_std(wq_d, 0)
            wqv = wview(wq_, 16, 512)
            xq = [bigc(hd) for hd in range(4)]
            xo = [bigc(4 + hd) for hd in range(4)]
            for hd in range(4):
                pq_ = FP.next()
                mm_group(pq_.t[:, 0:Tt], [(wqv[:, kc, hd * 128:(hd + 1) * 128], hT[:, kc, 0:Tt]) for kc in range(16)],
                         R=hR + [wq_.r], W=[pq_.r])
                act(xq[hd][:, 0:Tt], pq_.t[:, 0:Tt], AF.Copy, R=[pq_.r], W=bigr(hd))
            wo_ = wring.next()
            dma(POOL, wo_.sem, wo_.t[:, 0:4 * D].rearrange("p (k c) -> p k c", c=D),
                SCR[id(wo_d)][0].ap().rearrange("(k p) c -> p k c", p=128), R=[SCR[id(wo_d)][1]], W=[wo_.r])
            wov = wo_.t[:, 0:4 * D].rearrange("p (k c) -> p k c", c=D)
            load_gbc(post_xa_d)
            sc = 128.0 ** -0.5
            pn = big[:, 40 * T:40 * T + 1024].rearrange("p (h m) -> p h m", h=4)
            pn_r = bigr(40, 2)
            ptf = big[:, 42 * T:42 * T + 1024]
            pt_r = bigr(42, 2)
            xo4 = big[:, 4 * T:8 * T].rearrange("p (h t) -> p h t", t=T)
            for j in range(nblk):
                jc = slice(j * 128, (j + 1) * 128)
                pss = [FP.next(), FP.next()]
                for hd in range(4):
                    ps_ = pss[hd // 2]
                    mm_group(ps_.t[:, (hd % 2) * 256:(hd % 2 + 1) * 256], [(xq[hd][:, jc], xk[:, hd, :])],
                             R=bigr(hd), W=[ps_.r])
                for b2 in range(2):
                    ps_ = pss[b2]
                    v3 = ps_.t[:, 0:512].rearrange("p (h m) -> p h m", h=2)
                    op(DVE, lambda: V.reduce_max(out=st[:, 6 + 2 * b2:8 + 2 * b2], in_=v3, axis=AX.X),
                       R=[ps_.r], W=[st_r])
                    op(DVE, lambda: V.tensor_scalar(out=st[:, 10 + 2 * b2:12 + 2 * b2], in0=st[:, 6 + 2 * b2:8 + 2 * b2],
                                                    scalar1=-sc, scalar2=None, op0=ALU.mult), W=[st_r])
                    pe_ = P32.next()
                    for h2 in range(2):
                        hd = 2 * b2 + h2
                        act(pe_.t[:, h2 * 256:(h2 + 1) * 256], ps_.t[:, h2 * 256:(h2 + 1) * 256], AF.Exp,
                            R=[ps_.r, st_r], W=[pe_.r, st_r], scale=sc, bias=st[:, 10 + hd:11 + hd],
                            accum_out=st[:, 26 + hd:27 + hd])
                    op(DVE, lambda: V.reciprocal(out=st[:, 56 + 2 * b2:58 + 2 * b2], in_=st[:, 26 + 2 * b2:28 + 2 * b2]),
                       R=[st_r], W=[st_r])
                    op(DVE, lambda: V.tensor_tensor(
                        out=pn[:, 2 * b2:2 * b2 + 2, :], in0=pe_.t[:, 0:512].rearrange("p (h m) -> p h m", h=2),
                        in1=st[:, 56 + 2 * b2:58 + 2 * b2].unsqueeze(2).broadcast_to([128, 2, 256]), op=ALU.mult),
                       R=[pe_.r, st_r], W=pn_r)
                ph = HP.next()
                pe_multi([lambda hd=hd, mb=mb: nc.tensor.transpose(
                    ph.t[:, (hd * 2 + mb) * 128:(hd * 2 + mb + 1) * 128], pn[:, hd, mb * 128:(mb + 1) * 128], ident[:])
                    for hd in range(4) for mb in range(2)], R=pn_r, W=[ph.r])
                act(ptf, ph.t[:, 0:1024], AF.Copy, R=[ph.r], W=pt_r)
                po = FP.next()
                pe_multi([lambda hd=hd, mb=mb: nc.tensor.matmul(
                    po.t[:, hd * 128:(hd + 1) * 128], xv[:, mb, hd * 128:(hd + 1) * 128],
                    ptf[:, (hd * 2 + mb) * 128:(hd * 2 + mb + 1) * 128], start=(mb == 0), stop=(mb == 1))
                    for hd in range(4) for mb in range(2)], R=pt_r, W=[po.r])
                act(xo4[:, :, jc], po.t[:, 0:512].rearrange("p (h t) -> p h t", h=4), AF.Copy, R=[po.r], W=bigr(4, 4))
                for fg in range(4):
                    py = FP.next()
                    mm_group(py.t[:, 0:512], [(xo[hd][:, jc], wov[:, hd, fg * 512:(fg + 1) * 512]) for hd in range(4)],
                             R=bigr(4, 4) + [wo_.r], W=[py.r])
                    y_evac(py, j, fg, "B")
                residual_update(j, "B")

        def stageC(nblk, first, out_blk0):
            Tt = 128 * nblk
            rmsnorm_T([(xs[:, j, :], xs_r[j]) for j in range(nblk)], gpf, nblk)
            hR = hT_r[0:nblk]

            ringA = Ring(P32.items[0:4])
            ringB = Ring(P32.items[4:8])

            def conv_chunk(w_, wv_, i, c, gate):
                ph = FP.next()
                mm_group(ph.t[:, 0:Tt], [(wv_[:, kc, i * 128:(i + 1) * 128], hT[:, kc, 0:Tt]) for kc in range(16)],
                         R=hR + [w_.r], W=[ph.r])
                hb = ringA.next()
                if first:
                    ph2 = FP.next()
                    mm_group(ph2.t[:, 0:2], [(wv_[:, kc, i * 128:(i + 1) * 128], hprev.t[:, kc, :]) for kc in range(16)],
                             R=[hprev.r, w_.r], W=[ph2.r])
                    act(hb.t[:, 0:2], ph2.t[:, 0:2], AF.Copy, R=[ph2.r], W=[hb.r], scale=cm[:, 0:1])
                else:
                    act(hb.t[:, 0:2], halo[:, c, :], AF.Copy, R=[halo_r], W=[hb.r])
                act(hb.t[:, 2:Tt + 2], ph.t[:, 0:Tt], AF.Copy, R=[ph.r], W=[hb.r])
                act(halo[:, c, :], hb.t[:, Tt:Tt + 2], AF.Copy, R=[hb.r], W=[halo_r])
                acc = (ringB if gate else ringA).next()
                act(acc.t[:, 0:Tt], ph.t[:, 0:Tt], AF.Identity, R=[ph.r], W=[acc.r], scale=cw[:, 2, c:c + 1],
                    bias=cb[:, c:c + 1])
                op(DVE, lambda: V.scalar_tensor_tensor(out=acc.t[:, 0:Tt], in0=hb.t[:, 1:Tt + 1], scalar=cw[:, 1, c:c + 1],
                                                       in1=acc.t[:, 0:Tt], op0=ALU.mult, op1=ALU.add),
                   R=[hb.r, acc.r], W=[acc.r])
                op(DVE, lambda: V.scalar_tensor_tensor(out=acc.t[:, 0:Tt], in0=hb.t[:, 0:Tt], scalar=cw[:, 0, c:c + 1],
                                                       in1=acc.t[:, 0:Tt], op0=ALU.mult, op1=ALU.add),
                   R=[hb.r, acc.r], W=[acc.r])
                if gate:
                    act(acc.t[:, 0:Tt], acc.t[:, 0:Tt], AF.Gelu_apprx_tanh, R=[acc.r], W=[acc.r])
                return acc

            for gt in range(11):
                wg = load_w_std(wup_d, gt * 512)
                wgv = wview(wg, 16, 512)
                gas = [conv_chunk(wg, wgv, i, gt * 4 + i, True) for i in range(4)]
                wu = load_w_std(wup_d, DFF + gt * 512)
                wuv = wview(wu, 16, 512)
                for i in range(4):
                    c = gt * 4 + i
                    ga = gas[i]
                    au = conv_chunk(wu, wuv, i, c + 44, False)
                    op(DVE, lambda ga=ga, au=au, c=c: V.tensor_tensor(out=bigc(c)[:, 0:Tt], in0=ga.t[:, 0:Tt],
                                                                     in1=au.t[:, 0:Tt], op=ALU.mult),
                       R=[ga.r, au.r], W=bigr(c))
            load_gbc(post_ffn_d)
            pieces = [(0, 16), (16, 16), (32, 12)]
            for fg in range(4):
                pys = [FP.next() for _ in range(nblk)]
                for pi, (k0, nk) in enumerate(pieces):
                    wd = load_w_std(wdn_d, fg * 512, nrows=nk * 128, r0=k0 * 128)
                    wdv = wview(wd, nk, 512)
                    for j in range(nblk):
                        py = pys[j]
                        for r in bigr(k0, nk) + [wd.r]:
                            PE.wait(r.w)
                        if pi == 0:
                            PE.wait(py.r.w)
                            for t in py.r.rs.values():
                                PE.wait(t)
                        for kc in range(nk):
                            ins = nc.tensor.matmul(py.t[:, 0:512], bigc(k0 + kc)[:, j * 128:(j + 1) * 128], wdv[:, kc, :],
                                                   start=(pi == 0 and kc == 0), stop=(pi == 2 and kc == nk - 1))
                        PE.sem.n += 1
                        ins.then_inc(PE.sem.h, 1)
                        tok = (PE.sem, PE.sem.n)
                        for r in bigr(k0, nk) + [wd.r]:
                            r.rs["pe"] = tok
                        py.r.w = tok
                        py.r.rs = {}
                for j in range(nblk):
                    y_evac(pys[j], j, fg, "C")
            for j in range(nblk):
                residual_update(j, "C")
                dma(SP, out_sem[j], out_d.ap()[(out_blk0 + j) * 128:(out_blk0 + j + 1) * 128, :], xs[:, j, :],
                    R=[xs_r[j]])

        for mb in range(2):
            dma(SP, xs_sem[0], xs[:, 0, :], mem_d.ap()[mb * 128:(mb + 1) * 128, :], W=[xs_r[0]])
            xa, xr = xs[:, 0, :], xs_r[0]
            act(xn.t[:], xa, AF.Square, R=[xr], W=[xn.r, st_r], accum_out=st[:, 0:1])
            act(st[:, 1:2], st[:, 0:1], AF.Ln, R=[], W=[st_r], scale=1.0 / D, bias=EPS)
            act(st[:, 1:2], st[:, 1:2], AF.Exp, R=[], W=[st_r], scale=-0.5)
            act(xn.t[:], xa, AF.Copy, R=[xr, st_r], W=[xn.r], scale=st[:, 1:2])
            for h in range(2):
                pb_ = HP.next()
                pe_multi([lambda i=i, pb_=pb_, h=h: nc.tensor.transpose(
                    pb_.t[:, i * 128:(i + 1) * 128], xn.t[:, (h * 8 + i) * 128:(h * 8 + i + 1) * 128], ident[:])
                    for i in range(8)], R=[xn.r], W=[pb_.r])
                op(DVE, lambda h=h, pb_=pb_, mb=mb: V.tensor_tensor(
                    out=hT[:, h * 8:(h + 1) * 8, mb * 128:(mb + 1) * 128],
                    in0=pb_.t[:, 0:1024].rearrange("p (k t) -> p k t", k=8),
                    in1=gmem[:, h * 8:(h + 1) * 8].unsqueeze(2).broadcast_to([128, 8, 128]), op=ALU.mult),
                   R=[pb_.r, cst_r], W=[hT_r[mb]])
        wk_ = load_w_std(wk_d, 0)
        wkv = wview(wk_, 16, 512)
        for hd in range(4):
            pk_ = FP.next()
            mm_group(pk_.t[:, 0:256], [(wkv[:, kc, hd * 128:(hd + 1) * 128], hT[:, kc, 0:256]) for kc in range(16)],
                     R=hT_r[0:2] + [wk_.r], W=[pk_.r])
            act(xk[:, hd, :], pk_.t[:, 0:256], AF.Copy, R=[pk_.r], W=[cst_r])
        wv2 = load_w_std(wv_d, 0)
        wvv2 = wview(wv2, 16, 512)
        for mb in range(2):
            pv = FP.next()
            mm_group(pv.t[:, 0:512], [(hT[:, kc, mb * 128:(mb + 1) * 128], wvv2[:, kc, :]) for kc in range(16)],
                     R=hT_r[0:2] + [wv2.r], W=[pv.r])
            act(xv[:, mb, :], pv.t[:, 0:512], AF.Copy, R=[pv.r], W=[cst_r])

        if use_cc:
            for t in range(NT2):
                load_x(t * NB, NB)
                stageA(NB, 1)
            cin = cin_d.ap()
            dma(SP, ccio_sem, cin[0:1024, :].rearrange("(dc d) e -> d dc e", d=128), S[:], R=S_r)
            dma(SP, ccio_sem, cin[1024:1026, :].rearrange("a x -> (a x)").rearrange("(d dc) -> d dc", dc=8),
                Dtot.t[:], R=[Dtot.r])
            POOL.wait((ccio_sem, ccio_sem.n))
            POOL.wait((kv_sem, kv_sem.n))
            for sc_t_, sc_r_ in SCR.values():
                POOL.wait(sc_r_.w)
            for sl_ in wsl:
                POOL.wait((sl_.sem, sl_.sem.n))
            POOL.wait((cst_sem, cst_sem.n))
            ins = nc.gpsimd.collective_compute("AllGather", ALU.bypass, replica_groups=[list(range(NCORES))],
                                               ins=[cin_d.ap().opt()], outs=[cout_d.ap().opt()])
            cc_sem.n += 1
            ins.then_inc(cc_sem.h, 1)
            cc_tok = (cc_sem, cc_sem.n)
            SP.wait(cc_tok)
            DVE.wait((ccio_sem, ccio_sem.n))
            op(DVE, lambda: V.memset(S[:], 0.0), W=S_r)
            cout = cout_d.ap()
            Sl = big[:, 0:8192].bitcast(F32).rearrange("p (dc e) -> p dc e", e=512)
            yh_r = big_r[0:16]
            for r in range(NCORES):
                dma(SP, yl_sem, Sl, cout[r * CROWS:r * CROWS + 1024, :].rearrange("(dc d) e -> d dc e", d=128),
                    W=yh_r)
                dma(SP, Dl.sem, Dl.t[:], cout[r * CROWS + 1024:r * CROWS + 1026, :].rearrange("a x -> (a x)")
                    .rearrange("(d dc) -> d dc", dc=8), W=[Dl.r])
                op(DVE, lambda r=r: V.tensor_scalar(out=De[:], in0=Dl.t[:], scalar1=cm[:, 1 + r:2 + r],
                                                    scalar2=omm[:, 1 + r:2 + r], op0=ALU.mult, op1=ALU.add),
                   R=[Dl.r], W=[De_r])
                for dc in range(8):
                    op(DVE, lambda dc=dc: V.tensor_scalar(out=S[:, dc, :], in0=S[:, dc, :], scalar1=De[:, dc:dc + 1],
                                                          scalar2=None, op0=ALU.mult), R=[De_r], W=[S_r[dc]])
                    op(DVE, lambda dc=dc, r=r: V.scalar_tensor_tensor(
                        out=S[:, dc, :], in0=Sl[:, dc, :], scalar=cm[:, 1 + r:2 + r], in1=S[:, dc, :],
                        op0=ALU.mult, op1=ALU.add), R=yh_r, W=[S_r[dc]])
        if not use_cc:
            NPRE = 3 * SEG // T
            bufs = [(hT, hT_r), (hT2, hT2_r)]
            load_x(0, NB, src=xpre_d)
            rmsnorm_T([(xs[:, j, :], xs_r[j]) for j in range(NB)], gpm, NB, hsel=bufs[0])
            if NPRE > 1:
                load_x(NB, NB, src=xpre_d)
            for t in range(NPRE):
                cur["hT"], cur["hT_r"] = bufs[t % 2]
                nxt = bufs[(t + 1) % 2]

                def hook(hd, t=t, nxt=nxt):
                    issue_conv(1)
                    if t + 1 >= NPRE:
                        return
                    rmsnorm_T([(xs[:, j, :], xs_r[j]) for j in range(NB)], gpm, NB, hsel=nxt, only=hd)
                    dma(SP, xs_sem[hd], xs[:, hd, :],
                        xpre_d.ap()[((t + 2) * NB + hd) * 128:((t + 2) * NB + hd + 1) * 128, :], W=[xs_r[hd]]) \
                        if t + 2 < NPRE else None

                stageA(NB, 1, pre_done=True, hook=hook)
            cur["hT"], cur["hT_r"] = hT, hT_r
            issue_conv(len(conv_jobs))
            barrier()
        for dc in range(8):
            act(Sb[:, dc, :], S[:, dc, :], AF.Copy, R=[S_r[dc]], W=[Sb_r[dc]])

        load_x(0, 1)
        stageA(1, 2)
        if "B" in stages:
            stageB(1)
        if "C" in stages:
            rmsnorm_T([(xs[:, 0, :], xs_r[0])], gpf, 1)
            op(DVE, lambda: V.tensor_copy(out=hprev.t[:], in_=hT[:, :, 126:128]), R=[hT_r[0]], W=[hprev.r])
        for t in range(NT2):
            load_x(1 + t * NB, NB)
            DBG["on"] = (t == 0)
            stageA(NB, 2)
            DBG["on"] = False
            if "B" in stages:
                stageB(NB)
            if "C" in stages:
                stageC(NB, t == 0, t * NB)
            else:
                for j in range(NB):
                    dma(SP, out_sem[j], out_d.ap()[(t * NB + j) * 128:(t * NB + j + 1) * 128, :], xs[:, j, :],
                        R=[xs_r[j]])
        for j in range(NB):
            SP.wait((out_sem[j], out_sem[j].n))
        SP.wait((dbg_sem, dbg_sem.n))
        for E in (PE, ACT, DVE, POOL):
            SP.wait((E.sem, E.sem.n))
    return nc


WEIGHT_KEYS = ["w_in", "pre_norm_mix", "sg_ln_g", "sg_ln_b", "sg_w", "sg_b", "gla_w_gate2", "gla_b_gate",
               "gla_norm_g", "w_proj_a", "w_proj_b", "w_out", "post_norm_mix", "pre_norm_xa", "mem_norm_g",
               "xa_wq", "xa_wk", "xa_wv", "xa_wo", "post_norm_xa", "pre_norm_ffn", "ffn_w_up", "ffn_conv_w",
               "ffn_conv_b", "ffn_w_down", "post_norm_ffn"]

_CACHE = {}


def kernel(_stages="ABC", _use_cc=False, _cores=None, _debug=False, **inputs):
    x = np.asarray(inputs["x"], dtype=np.float32)
    mem = np.asarray(inputs["mem"], dtype=np.float32)
    B, SEQ, _ = x.shape
    assert B == 2
    SEG = SEQ // 4
    key = (SEG, _stages, _use_cc, _debug)
    if key not in _CACHE:
        _CACHE[key] = build(SEG, _stages, _use_cc, _debug)
    nc = _CACHE[key]
    shared = {}
    for k in WEIGHT_KEYS:
        a = np.asarray(inputs[k], dtype=np.float32)
        shared[k] = np.ascontiguousarray(a[0])
    in_maps = []
    for c in range(NCORES):
        b, p = divmod(c, 4)
        xe = np.zeros((SEG + 128, D), np.float32)
        if p > 0:
            xe[:] = x[b, p * SEG - 128:(p + 1) * SEG]
        else:
            xe[128:] = x[b, 0:SEG]
        cmv = np.zeros((128, 16), np.float32)
        cmv[:, 0] = 1.0 if p > 0 else 0.0
        for j in range(NCORES):
            if j // 4 == b and j < c:
                cmv[:, 1 + j] = 1.0
        m = dict(shared)
        m["x"] = xe
        m["mem"] = np.ascontiguousarray(mem[b])
        m["cm"] = cmv
        if not _use_cc:
            xp = np.zeros((3 * SEG, D), np.float32)
            n = p * SEG - 128
            if n > 0:
                xp[3 * SEG - n:] = x[b, 0:n]
            m["xpre"] = xp
        in_maps.append(m)
    if _cores is not None:
        res = run_bass_kernel_spmd(nc, [in_maps[c] for c in _cores], core_ids=list(range(len(_cores))))
        return res.results
    res = run_bass_kernel_spmd(nc, in_maps, core_ids=list(range(NCORES)))
    out = np.empty((B, SEQ, D), np.float32)
    for c in range(NCORES):
        b, p = divmod(c, 4)
        out[b, p * SEG:(p + 1) * SEG] = res.results[c]["out"]
    return out
```
